# Optimizing a Trainium2 kernel written in Bass

```python
import jax, jax.numpy as jnp
from jax import lax
import numpy as np

D_MODEL = 1024
BATCH = 2
SEQ = 16384
DEPTH = 4

N_MIXERS = 2
HEAD_DIM = 64
NSA_HEADS = D_MODEL // HEAD_DIM
NSA_KV_GROUPS = 4
NSA_HEADS_PER_GROUP = NSA_HEADS // NSA_KV_GROUPS
CMP_BLOCK = 32
CMP_STRIDE = 16
CMP_HIDDEN = 256
SLC_BLOCK = 64
SLC_TOPK = 16
WINDOW = 512
Q_BLOCK = 128
N_BRANCHES = 3
NSA_Q_WIDTH = NSA_HEADS * HEAD_DIM
NSA_KV_WIDTH = NSA_KV_GROUPS * HEAD_DIM
NSA_PROJ = NSA_Q_WIDTH + 2 * N_BRANCHES * NSA_KV_WIDTH + N_BRANCHES * NSA_HEADS
ROPE_THETA = 500000.0
ROPE_DIM = HEAD_DIM // 4
RWKV_HEAD_SIZE = 64
RWKV_HEADS = D_MODEL // RWKV_HEAD_SIZE
DECAY_LORA = 64
AAA_LORA = 64
MV_LORA = 32
GATE_LORA = 160
D_FF = 4 * D_MODEL
N_NSA = (DEPTH + N_MIXERS - 1) // N_MIXERS
N_RWKV = DEPTH // N_MIXERS
N_VRES = max(N_RWKV - 1, 0)
NORM_EPS = 1e-5
GN_EPS = 64e-5
NEG_INF = -1e30
SEL_FORCE = 1e30

kernel_name = "nsa_rwkv7_hybrid_trunk"


def rms_norm(x, g):
    xf = x.astype(jnp.float32)
    y = xf * lax.rsqrt(jnp.mean(xf * xf, axis=-1, keepdims=True) + NORM_EPS)
    return (y * g).astype(x.dtype)


def partial_rope(x, pos):
    half = ROPE_DIM // 2
    inv = ROPE_THETA ** (-jnp.arange(0, ROPE_DIM, 2, dtype=jnp.float32) / ROPE_DIM)
    ang = pos.astype(jnp.float32)[:, None] * inv[None, :]
    cos = jnp.cos(ang)[:, None, :]
    sin = jnp.sin(ang)[:, None, :]
    x1 = x[..., :half].astype(jnp.float32)
    x2 = x[..., half:ROPE_DIM].astype(jnp.float32)
    rot = jnp.concatenate([x1 * cos - x2 * sin, x2 * cos + x1 * sin], axis=-1)
    return jnp.concatenate([rot.astype(x.dtype), x[..., ROPE_DIM:]], axis=-1)


def compress_blocks(t, pe, w1, w2):
    B, S, G, dh = t.shape
    ratio = CMP_BLOCK // CMP_STRIDE
    n_chunks = S // CMP_STRIDE
    n_cmp = n_chunks - ratio + 1
    c = t.reshape(B, n_chunks, CMP_STRIDE, G, dh)
    blocks = jnp.concatenate([c[:, i:i + n_cmp] for i in range(ratio)], axis=2)
    blocks = blocks + pe[None, None, :, None, :]
    flat = jnp.moveaxis(blocks, 3, 2).reshape(B, n_cmp, G, CMP_BLOCK * dh)
    return jax.nn.gelu(flat @ w1) @ w2


def nsa_mixer(h, w_in, pe_k, w1_k, w2_k, pe_v, w1_v, w2_v, w_out):
    B, S, _ = h.shape
    G, Hg, dh = NSA_KV_GROUPS, NSA_HEADS_PER_GROUP, HEAD_DIM
    proj = h @ w_in
    q = proj[..., :NSA_Q_WIDTH].reshape(B, S, NSA_HEADS, dh)
    kv = [proj[..., NSA_Q_WIDTH + i * NSA_KV_WIDTH:NSA_Q_WIDTH + (i + 1) * NSA_KV_WIDTH].reshape(B, S, G, dh)
          for i in range(2 * N_BRANCHES)]
    k_cmp, v_cmp, k_slc, v_slc, k_win, v_win = kv
    gates = jax.nn.sigmoid(proj[..., NSA_Q_WIDTH + 2 * N_BRANCHES * NSA_KV_WIDTH:].astype(jnp.float32))
    gates = gates.reshape(B, S, G, Hg, N_BRANCHES)
    pos = jnp.arange(S)
    q = partial_rope(q, pos)
    k_cmp = partial_rope(k_cmp, pos)
    k_slc = partial_rope(k_slc, pos)
    k_win = partial_rope(k_win, pos)

    kc = compress_blocks(k_cmp, pe_k, w1_k, w2_k)
    vc = compress_blocks(v_cmp, pe_v, w1_v, w2_v)
    n_cmp = kc.shape[1]
    cmp_end = jnp.arange(n_cmp) * CMP_STRIDE + CMP_BLOCK - 1

    n_slc = S // SLC_BLOCK
    n_sel = min(SLC_TOPK, n_slc)
    ks_blocks = k_slc.reshape(B, n_slc, SLC_BLOCK, G, dh).transpose(0, 3, 1, 2, 4)
    vs_blocks = v_slc.reshape(B, n_slc, SLC_BLOCK, G, dh).transpose(0, 3, 1, 2, 4)
    ci = jnp.arange(n_cmp)[:, None]
    sj = jnp.arange(n_slc)[None, :]
    overlap = ((ci * CMP_STRIDE <= sj * SLC_BLOCK + SLC_BLOCK - 1)
               & (ci * CMP_STRIDE + CMP_BLOCK - 1 >= sj * SLC_BLOCK)).astype(jnp.float32)

    kw_pad = jnp.pad(k_win, ((0, 0), (WINDOW, 0), (0, 0), (0, 0)))
    vw_pad = jnp.pad(v_win, ((0, 0), (WINDOW, 0), (0, 0), (0, 0)))
    scale = HEAD_DIM ** -0.5
    b_idx = jnp.arange(B)[:, None, None, None]
    g_idx = jnp.arange(G)[None, :, None, None]
    blk = jnp.arange(n_slc)

    def query_block(bi):
        t0 = bi * Q_BLOCK
        tpos = t0 + jnp.arange(Q_BLOCK)
        qb = lax.dynamic_slice_in_dim(q, t0, Q_BLOCK, axis=1).reshape(B, Q_BLOCK, G, Hg, dh)
        gb = lax.dynamic_slice_in_dim(gates, t0, Q_BLOCK, axis=1)

        s_c = jnp.einsum('btghd,bngd->bghtn', qb, kc).astype(jnp.float32) * scale
        valid_c = cmp_end[None, :] <= tpos[:, None]
        p_c = jax.nn.softmax(jnp.where(valid_c, s_c, NEG_INF), axis=-1) * valid_c
        o_c = jnp.einsum('bghtn,bngd->btghd', p_c.astype(vc.dtype), vc)

        imp = jnp.einsum('bgtn,ns->bgts', p_c.sum(axis=2), overlap)
        cur = (tpos // SLC_BLOCK)[:, None]
        forced = (blk[None] == 0) | (blk[None] == cur) | (blk[None] == cur - 1)
        imp = jnp.where(forced, SEL_FORCE, jnp.where(blk[None] <= cur, imp, NEG_INF))
        _, idx = lax.top_k(imp, n_sel)
        k_sel = ks_blocks[b_idx, g_idx, idx]
        v_sel = vs_blocks[b_idx, g_idx, idx]
        s_s = jnp.einsum('btghd,bgtnkd->bghtnk', qb, k_sel).astype(jnp.float32) * scale
        kpos = idx[..., None] * SLC_BLOCK + jnp.arange(SLC_BLOCK)
        valid_s = (kpos <= tpos[None, None, :, None, None])[:, :, None]
        s_s = jnp.where(valid_s, s_s, NEG_INF).reshape(B, G, Hg, Q_BLOCK, n_sel * SLC_BLOCK)
        p_s = jax.nn.softmax(s_s, axis=-1).reshape(B, G, Hg, Q_BLOCK, n_sel, SLC_BLOCK)
        o_s = jnp.einsum('bghtnk,bgtnkd->btghd', p_s.astype(v_sel.dtype), v_sel)

        kw = lax.dynamic_slice_in_dim(kw_pad, t0, Q_BLOCK + WINDOW, axis=1)
        vw = lax.dynamic_slice_in_dim(vw_pad, t0, Q_BLOCK + WINDOW, axis=1)
        wpos = t0 - WINDOW + jnp.arange(Q_BLOCK + WINDOW)
        valid_w = ((wpos[None] <= tpos[:, None]) & (wpos[None] > tpos[:, None] - WINDOW)
                   & (wpos[None] >= 0))
        s_w = jnp.einsum('btghd,bkgd->bghtk', qb, kw).astype(jnp.float32) * scale
        p_w = jax.nn.softmax(jnp.where(valid_w, s_w, NEG_INF), axis=-1)
        o_w = jnp.einsum('bghtk,bkgd->btghd', p_w.astype(vw.dtype), vw)

        o = gb[..., 0:1] * o_c + gb[..., 1:2] * o_s + gb[..., 2:3] * o_w
        return o.reshape(B, Q_BLOCK, NSA_Q_WIDTH).astype(h.dtype)

    out = lax.map(query_block, jnp.arange(S // Q_BLOCK))
    out = jnp.moveaxis(out, 0, 1).reshape(B, S, NSA_Q_WIDTH)
    return out @ w_out


def rwkv7_mixer(h, v_first, x_mix, w_rkv, w0, w1, w2, a0, a1, a2, g1, g2,
                k_k, k_a, r_k, ln_w, ln_b, w_out, v_res):
    B, S, D = h.shape
    H, N = RWKV_HEADS, RWKV_HEAD_SIZE
    f32 = jnp.float32
    dx = jnp.pad(h, ((0, 0), (1, 0), (0, 0)))[:, :-1] - h
    xr, xw, xk, xv, xa, xg = (h + dx * x_mix[i] for i in range(6))
    r = xr @ w_rkv[0]
    k = xk @ w_rkv[1]
    v = xv @ w_rkv[2]
    w_log = -jax.nn.softplus(-(w0 + jnp.tanh(xw @ w1) @ w2).astype(f32)) - 0.5
    decay = jnp.exp(-jnp.exp(w_log))
    if v_res is None:
        v_first = v
    else:
        v0, v1, v2 = v_res
        v = v + (v_first - v) * jax.nn.sigmoid(v0 + (xv @ v1) @ v2)
    a = jax.nn.sigmoid((a0 + (xa @ a1) @ a2).astype(f32))
    g = jax.nn.sigmoid(xg @ g1) @ g2

    def heads(t):
        return t.astype(f32).reshape(B, S, H, N)

    kk = heads(k * k_k)
    kk = kk / jnp.maximum(jnp.sqrt(jnp.sum(kk * kk, axis=-1, keepdims=True)), 1e-12)
    kh = heads(k.astype(f32) * (1.0 + (a - 1.0) * k_a.astype(f32)))
    rh, vh, wh, ah = heads(r), heads(v), heads(decay), heads(a)

    def step(state, inp):
        r_t, w_t, k_t, v_t, kk_t, a_t = inp
        sa = jnp.einsum('bhij,bhj->bhi', state, -kk_t)
        state = (state * w_t[:, :, None, :] + sa[..., None] * (kk_t * a_t)[:, :, None, :]
                 + v_t[..., None] * k_t[:, :, None, :])
        return state, jnp.einsum('bhij,bhj->bhi', state, r_t)

    xs = tuple(jnp.moveaxis(t, 1, 0) for t in (rh, wh, kh, vh, kk, ah))
    _, ys = lax.scan(step, jnp.zeros((B, H, N, N), f32), xs)
    y = jnp.moveaxis(ys, 0, 1)
    mu = jnp.mean(y, axis=-1, keepdims=True)
    var = jnp.mean(jnp.square(y - mu), axis=-1, keepdims=True)
    yn = ((y - mu) * lax.rsqrt(var + GN_EPS)).reshape(B, S, D) * ln_w + ln_b
    bonus = jnp.sum(rh * kh * r_k.astype(f32).reshape(H, N), axis=-1, keepdims=True) * vh
    y = yn + bonus.reshape(B, S, D)
    return (y.astype(h.dtype) * g) @ w_out, v_first


def sq_relu_mlp(h, w1, w2):
    return jnp.square(jax.nn.relu(h @ w1)) @ w2


def setup_inputs(seed: int = 0) -> dict:
    key = jax.random.key(seed)
    ks = iter(jax.random.split(key, 64))
    f32 = jnp.float32

    def nrm(shape, scale):
        return jax.random.normal(next(ks), shape, f32) * scale

    D = D_MODEL
    return {
        "x": nrm((BATCH, SEQ, D), 1.0),
        "norm_mix": 1.0 + nrm((DEPTH, D), 0.02),
        "norm_mlp": 1.0 + nrm((DEPTH, D), 0.02),
        "norm_final": 1.0 + nrm((D,), 0.02),
        "mlp_w1": nrm((DEPTH, D, D_FF), D ** -0.5),
        "mlp_w2": nrm((DEPTH, D_FF, D), D_FF ** -0.5),
        "nsa_w_in": nrm((N_NSA, D, NSA_PROJ), D ** -0.5),
        "nsa_cmp_pe_k": nrm((N_NSA, CMP_BLOCK, HEAD_DIM), 0.1),
        "nsa_cmp_w1_k": nrm((N_NSA, CMP_BLOCK * HEAD_DIM, CMP_HIDDEN), (CMP_BLOCK * HEAD_DIM) ** -0.5),
        "nsa_cmp_w2_k": nrm((N_NSA, CMP_HIDDEN, HEAD_DIM), CMP_HIDDEN ** -0.5),
        "nsa_cmp_pe_v": nrm((N_NSA, CMP_BLOCK, HEAD_DIM), 0.1),
        "nsa_cmp_w1_v": nrm((N_NSA, CMP_BLOCK * HEAD_DIM, CMP_HIDDEN), (CMP_BLOCK * HEAD_DIM) ** -0.5),
        "nsa_cmp_w2_v": nrm((N_NSA, CMP_HIDDEN, HEAD_DIM), CMP_HIDDEN ** -0.5),
        "nsa_w_out": nrm((N_NSA, NSA_Q_WIDTH, D), NSA_Q_WIDTH ** -0.5),
        "rwkv_x_mix": jax.random.uniform(next(ks), (N_RWKV, 6, D), f32),
        "rwkv_w_rkv": nrm((N_RWKV, 3, D, D), D ** -0.5),
        "rwkv_w0": jax.random.uniform(next(ks), (N_RWKV, D), f32, minval=-6.5, maxval=-1.5),
        "rwkv_w1": nrm((N_RWKV, D, DECAY_LORA), D ** -0.5),
        "rwkv_w2": nrm((N_RWKV, DECAY_LORA, D), 0.1 * DECAY_LORA ** -0.5),
        "rwkv_a0": nrm((N_RWKV, D), 0.1),
        "rwkv_a1": nrm((N_RWKV, D, AAA_LORA), D ** -0.5),
        "rwkv_a2": nrm((N_RWKV, AAA_LORA, D), 0.5 * AAA_LORA ** -0.5),
        "rwkv_g1": nrm((N_RWKV, D, GATE_LORA), D ** -0.5),
        "rwkv_g2": nrm((N_RWKV, GATE_LORA, D), GATE_LORA ** -0.5),
        "rwkv_k_k": 0.85 + nrm((N_RWKV, D), 0.02),
        "rwkv_k_a": 1.0 + nrm((N_RWKV, D), 0.02),
        "rwkv_r_k": -0.04 + nrm((N_RWKV, D), 0.02),
        "rwkv_ln_w": 1.0 + nrm((N_RWKV, D), 0.02),
        "rwkv_ln_b": nrm((N_RWKV, D), 0.02),
        "rwkv_w_out": nrm((N_RWKV, D, D), D ** -0.5),
        "rwkv_v0": 1.0 + nrm((N_VRES, D), 0.1),
        "rwkv_v1": nrm((N_VRES, D, MV_LORA), D ** -0.5),
        "rwkv_v2": nrm((N_VRES, MV_LORA, D), 0.5 * MV_LORA ** -0.5),
    }


def reference(x, norm_mix, norm_mlp, norm_final, mlp_w1, mlp_w2,
              nsa_w_in, nsa_cmp_pe_k, nsa_cmp_w1_k, nsa_cmp_w2_k,
              nsa_cmp_pe_v, nsa_cmp_w1_v, nsa_cmp_w2_v, nsa_w_out,
              rwkv_x_mix, rwkv_w_rkv, rwkv_w0, rwkv_w1, rwkv_w2,
              rwkv_a0, rwkv_a1, rwkv_a2, rwkv_g1, rwkv_g2,
              rwkv_k_k, rwkv_k_a, rwkv_r_k, rwkv_ln_w, rwkv_ln_b, rwkv_w_out,
              rwkv_v0, rwkv_v1, rwkv_v2):
    v_first = None
    for i in range(DEPTH):
        j = i // N_MIXERS
        hn = rms_norm(x, norm_mix[i])
        if i % N_MIXERS == 0:
            x = x + nsa_mixer(hn, nsa_w_in[j], nsa_cmp_pe_k[j], nsa_cmp_w1_k[j], nsa_cmp_w2_k[j],
                              nsa_cmp_pe_v[j], nsa_cmp_w1_v[j], nsa_cmp_w2_v[j], nsa_w_out[j])
        else:
            v_res = None if j == 0 else (rwkv_v0[j - 1], rwkv_v1[j - 1], rwkv_v2[j - 1])
            y, v_first = rwkv7_mixer(hn, v_first, rwkv_x_mix[j], rwkv_w_rkv[j], rwkv_w0[j],
                                     rwkv_w1[j], rwkv_w2[j], rwkv_a0[j], rwkv_a1[j], rwkv_a2[j],
                                     rwkv_g1[j], rwkv_g2[j], rwkv_k_k[j], rwkv_k_a[j], rwkv_r_k[j],
                                     rwkv_ln_w[j], rwkv_ln_b[j], rwkv_w_out[j], v_res)
            x = x + y
        x = x + sq_relu_mlp(rms_norm(x, norm_mlp[i]), mlp_w1[i], mlp_w2[i])
    return rms_norm(x, norm_final)
```

```python
import numpy as np
from contextlib import ExitStack
import concourse.bass as bass
import concourse.mybir as mybir
from concourse.bass_utils import run_bass_kernel_spmd

F32 = mybir.dt.float32
BF16 = mybir.dt.bfloat16
AF = mybir.ActivationFunctionType
ALU = mybir.AluOpType
AX = mybir.AxisListType

NCORES = 8
D = 1024
STG = 1024
SAME_SYNC = True


class Buf:
    def __init__(self, t, name):
        self.t = t
        self.name = name
        self.writers = {}
        self.readers = {}
        self.wgroup = None

    def __getitem__(self, idx):
        return self.t[idx]


class Op:
    __slots__ = ("eng", "fn", "stream", "pos", "waits", "needs_inc", "val", "is_dma")


class Prog:
    ENGS = ["pe", "act", "dve", "pool", "sp"]

    def __init__(self, nc):
        self.nc = nc
        self.stack = ExitStack()
        self.ops = {e: [] for e in self.ENGS}
        self.streams = {e: [] for e in self.ENGS}
        self.seen = {e: {} for e in self.ENGS}
        self.nbuf = 0

    def sb(self, name, shape, dt):
        self.nbuf += 1
        t = self.stack.enter_context(self.nc.sbuf_tensor(f"{name}_{self.nbuf}", list(shape), dt))
        return Buf(t, f"{name}_{self.nbuf}")

    def ps(self, name, shape, dt=F32):
        self.nbuf += 1
        t = self.stack.enter_context(self.nc.psum_tensor(f"{name}_{self.nbuf}", list(shape), dt))
        return Buf(t, f"{name}_{self.nbuf}")

    def add(self, eng, fn, reads=(), writes=(), dma_buf=None, group=None):
        op = Op()
        op.eng = eng
        op.fn = fn
        op.is_dma = dma_buf is not None
        op.needs_inc = op.is_dma
        op.val = None
        op.stream = ("dma", dma_buf.name) if op.is_dma else eng
        st = self.streams.setdefault(op.stream, [])
        op.pos = len(st)
        st.append(op)
        deps = {}

        def need(p):
            if p is op:
                return
            if (not p.is_dma) and p.stream == eng and not op.is_dma:
                if eng == "pe" or not SAME_SYNC:
                    return
            cur = deps.get(p.stream)
            if cur is None or cur.pos < p.pos:
                deps[p.stream] = p

        for b in reads:
            for p in b.writers.values():
                need(p)
        for b in writes:
            same_group = group is not None and b.wgroup == group
            if not same_group:
                for p in b.writers.values():
                    need(p)
            for p in b.readers.values():
                need(p)
        op.waits = []
        seen = self.seen[eng]
        for s, p in deps.items():
            if seen.get(s, -1) >= p.pos:
                continue
            seen[s] = p.pos
            p.needs_inc = True
            op.waits.append(p)
        for b in reads:
            b.readers[op.stream] = op
        for b in writes:
            same_group = group is not None and b.wgroup == group
            if same_group:
                b.writers[op.stream] = op
            else:
                b.writers = {op.stream: op}
                b.wgroup = group
            b.readers = {}
        self.ops[eng].append(op)
        return op

    def dma(self, q, out, in_, reads=(), writes=(), dma_buf=None, group=None, **kw):
        return self.add(q, lambda e: e.dma_start(out=out, in_=in_, **kw), reads, writes,
                        dma_buf=dma_buf, group=group)

    def mm(self, out, lhsT, rhs, start, stop, reads=(), writes=(), **kw):
        return self.add("pe", lambda e: e.matmul(out, lhsT, rhs, start=start, stop=stop, **kw), reads, writes)

    def finalize(self):
        nc = self.nc
        sems = {}
        for s, st in self.streams.items():
            if not any(o.needs_inc for o in st):
                continue
            nm = s if isinstance(s, str) else "d_" + s[1]
            sems[s] = self.stack.enter_context(nc.semaphore("s_" + nm))
            c = 0
            for o in st:
                if o.needs_inc:
                    c += 16 if o.is_dma else 1
                o.val = c
        self.nsem = len(sems)
        finals = [(sems[s], st[-1].val) for s, st in self.streams.items() if not isinstance(s, str) and st]
        block = self.stack.enter_context(nc.Block())
        engmap = {"pe": block.tensor, "act": block.scalar, "dve": block.vector, "pool": block.gpsimd,
                  "sp": block.sync}

        def make(engname):
            ops = self.ops[engname]

            def body(e):
                for o in ops:
                    for p in o.waits:
                        e.wait_ge(sems[p.stream], p.val)
                    inst = o.fn(e)
                    if o.needs_inc:
                        inst.then_inc(sems[o.stream], 16 if o.is_dma else 1)
                if engname == "sp":
                    for sem, v in finals:
                        e.wait_ge(sem, v)
            return body

        for en in self.ENGS:
            engmap[en](make(en))
        self.stack.close()


class Ctx:
    def __init__(self, nc):
        self.nc = nc
        self.P = Prog(nc)
        P = self.P
        self.ones32 = P.sb("ones32", [128, 128], F32)
        P.add("pool", lambda e: e.memset(self.ones32[:, :], 1.0), writes=[self.ones32])
        self.banks = [P.ps(f"bank{i}", [128, 512], F32) for i in range(8)]
        self.stg = [P.sb(f"stg{i}", [128, STG], F32) for i in range(2)]
        self.nstg = 0
        self.ncast = 0

    def load_cast(self, dst_buf, dst_ap, src_ap, ncols, eng=None, parts=128):
        P = self.P
        stg = self.stg[self.nstg % 2]
        q = "sp"
        self.nstg += 1
        P.dma(q, stg[0:parts, :ncols], src_ap, writes=[stg], dma_buf=stg)
        if eng is None:
            eng = ["pool", "dve"][self.ncast % 2]
            self.ncast += 1
        P.add(eng, lambda e: e.tensor_copy(out=dst_ap, in_=stg[0:parts, :ncols]), reads=[stg], writes=[dst_buf])

    def load_weight(self, name, w_dram, K, N, eng=None):
        P = self.P
        wv = w_dram.rearrange("(kc p) n -> p kc n", p=128)
        out = []
        for kc in range(K // 128):
            b = P.sb(f"{name}{kc}", [128, N], BF16)
            for c0 in range(0, N, STG):
                n = min(STG, N - c0)
                self.load_cast(b, b[:, c0:c0 + n], wv[:, kc, c0:c0 + n], n, eng=eng)
            out.append(b)
        return out

    def rmsnorm(self, xt, g_sb, hn, TT, sq, rstd, bank, eps_t):
        P = self.P
        P.add("act", lambda e: e.activation(out=sq[:, :, :], in_=xt[:, :, :TT], func=AF.Square),
              reads=[xt], writes=[sq])
        for c in range(8):
            P.mm(bank[:, :TT], self.ones32[:, :], sq[:, c, :], start=(c == 0), stop=(c == 7),
                 reads=[self.ones32, sq], writes=[bank])
        P.add("act", lambda e: e.activation(out=rstd[:, :], in_=bank[:, :TT], func=AF.Sqrt,
                                            bias=eps_t[:, 0:1], scale=1.0 / D),
              reads=[bank, eps_t], writes=[rstd])
        P.add("dve", lambda e: e.reciprocal(out=rstd[:, :], in_=rstd[:, :]), reads=[rstd], writes=[rstd])
        for c in range(8):
            P.add("dve", lambda e, c=c: e.scalar_tensor_tensor(
                out=hn[:, c, :], in0=xt[:, c, :TT], scalar=g_sb[:, c:c + 1], in1=rstd[:, :],
                op0=ALU.mult, op1=ALU.mult), reads=[xt, g_sb, rstd], writes=[hn])


def build_mlp(T, TT=256, final=False):
    nc = bass.Bass("TRN2", target_bir_lowering=False)
    xT = nc.dram_tensor("xT", [D, T], F32, kind="ExternalInput").ap()
    g = nc.dram_tensor("g", [128, 8], F32, kind="ExternalInput").ap()
    w1 = nc.dram_tensor("w1", [D, 4 * D], F32, kind="ExternalInput").ap()
    w2 = nc.dram_tensor("w2", [4 * D, D], F32, kind="ExternalInput").ap()
    yT = nc.dram_tensor("yT", [D, T], F32, kind="ExternalOutput").ap()
    C = Ctx(nc)
    P = C.P
    xv = xT.rearrange("(c p) t -> p c t", p=128)
    yv = yT.rearrange("(c p) t -> p c t", p=128)
    g_sb = P.sb("g", [128, 8], F32)
    P.dma("sp", g_sb[:, :], g, writes=[g_sb], dma_buf=g_sb)
    if final:
        gf = nc.dram_tensor("gf", [128, 8], F32, kind="ExternalInput").ap()
        gf_sb = P.sb("gf", [128, 8], F32)
        P.dma("sp", gf_sb[:, :], gf, writes=[gf_sb], dma_buf=gf_sb)
    eps_t = P.sb("eps", [128, 1], F32)
    P.add("pool", lambda e: e.memset(eps_t[:, :], 1e-5), writes=[eps_t])
    xb = [P.sb(f"x{i}", [128, 8, TT], F32) for i in range(2)]
    sq = P.sb("sq", [128, 8, TT], F32)
    rstd = P.sb("rstd", [128, TT], F32)
    hn = P.sb("hn", [128, 8, TT], BF16)
    h1 = [P.sb(f"h1_{i}", [128, 8, TT], BF16) for i in range(4)]
    rl = [P.sb(f"rl{i}", [128, TT], F32) for i in range(3)]
    P.dma("sp", xb[0][:, :, :], xv[:, :, 0:TT], writes=[xb[0]], dma_buf=xb[0])
    w1s = C.load_weight("w1_", w1, D, 4 * D)
    w2s = C.load_weight("w2_", w2, 4 * D, D)
    nt = T // TT
    nb = 0
    for tt in range(nt):
        xt = xb[tt % 2]
        if tt + 1 < nt:
            nx = xb[(tt + 1) % 2]
            P.dma("sp", nx[:, :, :], xv[:, :, (tt + 1) * TT:(tt + 2) * TT], writes=[nx], dma_buf=nx)
        C.rmsnorm(xt, g_sb, hn, TT, sq, rstd, C.banks[7], eps_t)
        for fc in range(32):
            bank = C.banks[nb % 4]
            nb += 1
            for kc in range(8):
                P.mm(bank[:, :TT], w1s[kc][:, fc * 128:(fc + 1) * 128], hn[:, kc, :], start=(kc == 0),
                     stop=(kc == 7), reads=[w1s[kc], hn], writes=[bank])
            r = rl[fc % 3]
            P.add("act", lambda e, r=r, bank=bank: e.activation(out=r[:, :], in_=bank[:, :TT], func=AF.Relu),
                  reads=[bank], writes=[r])
            hb = h1[fc // 8]
            P.add("pool", lambda e, r=r, hb=hb, fc=fc: e.tensor_tensor(
                out=hb[:, fc % 8, :], in0=r[:, :], in1=r[:, :], op=ALU.mult), reads=[r], writes=[hb])
        for fc in range(8):
            bank = C.banks[4 + fc % 2]
            for kc in range(32):
                P.mm(bank[:, :TT], w2s[kc][:, fc * 128:(fc + 1) * 128], h1[kc // 8][:, kc % 8, :],
                     start=(kc == 0), stop=(kc == 31), reads=[w2s[kc], h1[kc // 8]], writes=[bank])
            P.add("dve", lambda e, xt=xt, bank=bank, fc=fc: e.tensor_tensor(
                out=xt[:, fc, :], in0=xt[:, fc, :], in1=bank[:, :TT], op=ALU.add), reads=[xt, bank], writes=[xt])
        if final:
            C.rmsnorm(xt, gf_sb, xt, TT, sq, rstd, C.banks[7], eps_t)
        P.dma("pool", yv[:, :, tt * TT:(tt + 1) * TT], xt[:, :, :], reads=[xt], dma_buf=xt)
    P.finalize()
    return nc


def run_mlp_T(xT_full, g, w1, w2, gf=None):
    ntok = xT_full.shape[1]
    T = ntok // NCORES
    nc = build_mlp(T, final=gf is not None)
    in_maps = []
    for c in range(NCORES):
        m = {"xT": np.ascontiguousarray(xT_full[:, c * T:(c + 1) * T]), "g": lay8(g), "w1": w1, "w2": w2}
        if gf is not None:
            m["gf"] = lay8(gf)
        in_maps.append(m)
    res = run_bass_kernel_spmd(nc, in_maps, core_ids=list(range(NCORES)))
    return np.concatenate([r["yT"] for r in res.results], axis=1)


def run_mlp(x_tok, g, w1, w2):
    ntok = x_tok.shape[0]
    T = ntok // NCORES
    nc = build_mlp(T)
    g_l = np.ascontiguousarray(g.reshape(8, 128).T)
    in_maps = [{"xT": np.ascontiguousarray(x_tok[c * T:(c + 1) * T].T), "g": g_l, "w1": w1, "w2": w2}
               for c in range(NCORES)]
    res = run_bass_kernel_spmd(nc, in_maps, core_ids=list(range(NCORES)))
    return np.concatenate([r["yT"].T for r in res.results], axis=0)


def build_linres(T, n_in=1, TT=512):
    TT = min(TT, T)
    nc = bass.Bass("TRN2", target_bir_lowering=False)
    xT = nc.dram_tensor("xT", [D, T], F32, kind="ExternalInput").ap()
    aT = nc.dram_tensor("aT", [D, T], F32, kind="ExternalInput").ap()
    if n_in == 3:
        bT = nc.dram_tensor("bT", [D, T], F32, kind="ExternalInput").ap()
        cT = nc.dram_tensor("cT", [D, T], F32, kind="ExternalInput").ap()
    w = nc.dram_tensor("w", [D, D], F32, kind="ExternalInput").ap()
    yT = nc.dram_tensor("yT", [D, T], F32, kind="ExternalOutput").ap()
    C = Ctx(nc)
    P = C.P
    v3 = lambda ap: ap.rearrange("(c p) t -> p c t", p=128)
    xv, av, yv = v3(xT), v3(aT), v3(yT)
    ws = C.load_weight("w_", w, D, D)
    xb = [P.sb(f"x{i}", [128, 8, TT], F32) for i in range(2)]
    ab = [P.sb(f"a{i}", [128, 8, TT], F32) for i in range(2)]
    if n_in == 3:
        bv, cv = v3(bT), v3(cT)
        bb = [P.sb(f"b{i}", [128, 8, TT], F32) for i in range(2)]
        cb = [P.sb(f"c{i}", [128, 8, TT], F32) for i in range(2)]
    z = P.sb("z", [128, 8, TT], BF16)
    nt = T // TT
    for tt in range(nt):
        sl = slice(tt * TT, (tt + 1) * TT)
        xt, at = xb[tt % 2], ab[tt % 2]
        P.dma("sp", xt[:, :, :], xv[:, :, sl], writes=[xt], dma_buf=xt)
        P.dma("sp", at[:, :, :], av[:, :, sl], writes=[at], dma_buf=at)
        if n_in == 3:
            bt, ct = bb[tt % 2], cb[tt % 2]
            P.dma("sp", bt[:, :, :], bv[:, :, sl], writes=[bt], dma_buf=bt)
            P.dma("sp", ct[:, :, :], cv[:, :, sl], writes=[ct], dma_buf=ct)
            P.add("pool", lambda e, at=at, bt=bt: e.tensor_tensor(out=at[:, :, :], in0=at[:, :, :], in1=bt[:, :, :],
                                                                  op=ALU.add), reads=[at, bt], writes=[at])
            P.add("dve", lambda e, at=at, ct=ct: e.tensor_tensor(out=z[:, :, :], in0=at[:, :, :], in1=ct[:, :, :],
                                                                 op=ALU.mult), reads=[at, ct], writes=[z])
        else:
            P.add("pool", lambda e, at=at: e.tensor_copy(out=z[:, :, :], in_=at[:, :, :]), reads=[at], writes=[z])
        for fc in range(8):
            bank = C.banks[fc % 4]
            for kc in range(8):
                P.mm(bank[:, :TT], ws[kc][:, fc * 128:(fc + 1) * 128], z[:, kc, :], start=(kc == 0),
                     stop=(kc == 7), reads=[ws[kc], z], writes=[bank])
            P.add("dve", lambda e, xt=xt, bank=bank, fc=fc: e.tensor_tensor(
                out=xt[:, fc, :], in0=xt[:, fc, :], in1=bank[:, :TT], op=ALU.add), reads=[xt, bank], writes=[xt])
        P.dma("pool", yv[:, :, sl], xt[:, :, :], reads=[xt], dma_buf=xt)
    P.finalize()
    return nc


NSA_PROJ = 2608


def build_nsa_inproj(T, TT=256):
    nc = bass.Bass("TRN2", target_bir_lowering=False)
    xT = nc.dram_tensor("xT", [D, T], F32, kind="ExternalInput").ap()
    g = nc.dram_tensor("g", [128, 8], F32, kind="ExternalInput").ap()
    w = nc.dram_tensor("w", [D, NSA_PROJ], F32, kind="ExternalInput").ap()
    cosx = nc.dram_tensor("cosx", [T, 64], F32, kind="ExternalInput").ap()
    sinx = nc.dram_tensor("sinx", [T, 64], F32, kind="ExternalInput").ap()
    qkv = nc.dram_tensor("qkv", [T, 2560], BF16, kind="ExternalOutput").ap()
    gates = nc.dram_tensor("gates", [T, 48], F32, kind="ExternalOutput").ap()
    C = Ctx(nc)
    P = C.P
    xv = xT.rearrange("(c p) t -> p c t", p=128)
    g_sb = P.sb("g", [128, 8], F32)
    P.dma("sp", g_sb[:, :], g, writes=[g_sb], dma_buf=g_sb)
    eps_t = P.sb("eps", [128, 1], F32)
    P.add("pool", lambda e: e.memset(eps_t[:, :], 1e-5), writes=[eps_t])
    nsub = T // 128
    cs = P.sb("cs", [128, nsub, 8, 8], F32)
    sn = P.sb("sn", [128, nsub, 8, 8], F32)
    P.dma("sp", cs[:, :, :, :], cosx.rearrange("(n p) (h d) -> p n h d", p=128, d=8), writes=[cs], dma_buf=cs)
    P.dma("sp", sn[:, :, :, :], sinx.rearrange("(n p) (h d) -> p n h d", p=128, d=8), writes=[sn], dma_buf=sn)
    xb = [P.sb(f"x{i}", [128, 8, TT], F32) for i in range(2)]
    sq = P.sb("sq", [128, 8, TT], F32)
    rstd = P.sb("rstd", [128, TT], F32)
    hn = P.sb("hn", [128, 8, TT], BF16)
    ob = [P.sb(f"ob{i}", [128, 8, 64], F32) for i in range(3)]
    tmp = [P.sb(f"tmp{i}", [128, 8, 8], F32) for i in range(4)]
    obig = [P.sb(f"obig{i}", [128, 2560], BF16) for i in range(2)]
    gsb = [P.sb(f"gsb{i}", [128, 48], F32) for i in range(2)]
    P.dma("sp", xb[0][:, :, :], xv[:, :, 0:TT], writes=[xb[0]], dma_buf=xb[0])
    ws = C.load_weight("w_", w, D, NSA_PROJ)
    nt = T // TT
    nb = 0
    for tt in range(nt):
        xt = xb[tt % 2]
        if tt + 1 < nt:
            nx = xb[(tt + 1) % 2]
            P.dma("sp", nx[:, :, :], xv[:, :, (tt + 1) * TT:(tt + 2) * TT], writes=[nx], dma_buf=nx)
        C.rmsnorm(xt, g_sb, hn, TT, sq, rstd, C.banks[7], eps_t)
        for ts in range(TT // 128):
            isub = tt * (TT // 128) + ts
            og = obig[isub % 2]
            for cc in range(6):
                c0 = cc * 512
                n = min(512, NSA_PROJ - c0)
                bank = C.banks[nb % 4]
                nb += 1
                for kc in range(8):
                    P.mm(bank[:, :n], hn[:, kc, ts * 128:(ts + 1) * 128], ws[kc][:, c0:c0 + n], start=(kc == 0),
                         stop=(kc == 7), reads=[ws[kc], hn], writes=[bank])
                if cc == 5:
                    gs = gsb[isub % 2]
                    P.add("act", lambda e, gs=gs, bank=bank: e.activation(out=gs[:, :], in_=bank[:, :48],
                                                                         func=AF.Sigmoid), reads=[bank], writes=[gs])
                    P.dma("pool", gates[isub * 128:(isub + 1) * 128, :], gs[:, :], reads=[gs], dma_buf=gs)
                    continue
                o = ob[nb % 3]
                P.add("act", lambda e, o=o, bank=bank: e.activation(
                    out=o[:, :, :], in_=bank[:, :512].rearrange("p (h d) -> p h d", d=64), func=AF.Copy),
                    reads=[bank], writes=[o])
                nh = 8 if cc < 2 else 4
                A = o[:, 0:nh, 0:8]
                B = o[:, 0:nh, 8:16]
                cst = cs[:, isub, 0:nh, :]
                snt = sn[:, isub, 0:nh, :]
                t1, t2, t3, t4 = [t[:, 0:nh, :] for t in tmp]
                for (dst, a, b) in ((t1, A, cst), (t2, B, snt), (t3, B, cst), (t4, A, snt)):
                    P.add("dve", lambda e, dst=dst, a=a, b=b: e.tensor_tensor(out=dst, in0=a, in1=b, op=ALU.mult),
                          reads=[o, cs, sn], writes=[tmp[0]])
                P.add("dve", lambda e, A=A, t1=t1, t2=t2: e.tensor_tensor(out=A, in0=t1, in1=t2, op=ALU.subtract),
                      reads=[tmp[0]], writes=[o])
                P.add("dve", lambda e, B=B, t3=t3, t4=t4: e.tensor_tensor(out=B, in0=t3, in1=t4, op=ALU.add),
                      reads=[tmp[0]], writes=[o])
                P.add("pool", lambda e, og=og, o=o, c0=c0: e.tensor_copy(
                    out=og[:, c0:c0 + 512].rearrange("p (h d) -> p h d", d=64), in_=o[:, :, :]),
                    reads=[o], writes=[og])
            P.dma("pool", qkv[isub * 128:(isub + 1) * 128, :], og[:, :], reads=[og], dma_buf=og)
    P.finalize()
    return nc


def rope_tables(S):
    inv = (500000.0 ** (-np.arange(0, 16, 2, dtype=np.float32) / 16.0)).astype(np.float32)
    ang = np.arange(S, dtype=np.float32)[:, None] * inv[None, :]
    return np.cos(ang).astype(np.float32), np.sin(ang).astype(np.float32)


def run_nsa_inproj(x_tok, g, w_in, S):
    ntok = x_tok.shape[0]
    T = ntok // NCORES
    nc = build_nsa_inproj(T)
    g_l = np.ascontiguousarray(g.reshape(8, 128).T)
    cos, sin = rope_tables(S)
    cosx = np.tile(cos, (1, 8))
    sinx = np.tile(sin, (1, 8))
    in_maps = []
    for c in range(NCORES):
        pos = (np.arange(c * T, (c + 1) * T)) % S
        in_maps.append({"xT": np.ascontiguousarray(x_tok[c * T:(c + 1) * T].T), "g": g_l, "w": w_in,
                        "cosx": np.ascontiguousarray(cosx[pos]), "sinx": np.ascontiguousarray(sinx[pos])})
    res = run_bass_kernel_spmd(nc, in_maps, core_ids=list(range(NCORES)))
    qkv = np.concatenate([r["qkv"] for r in res.results], axis=0)
    gates = np.concatenate([r["gates"] for r in res.results], axis=0)
    return qkv, gates


NEGV = -30000.0
SCALE = 0.125


def nsa_consts():
    tl = np.arange(128)[:, None]
    kl = np.arange(128)[None, :]
    c = {}
    c["ident"] = np.eye(128, dtype=np.float32)
    c["i4"] = np.tile(np.eye(128, dtype=np.float32), (1, 4))
    c["causal"] = np.where(kl > tl, NEGV, 0.0).astype(np.float32)
    c["winneg"] = np.where(kl > tl, 0.0, NEGV).astype(np.float32)
    negc = np.zeros((128, 17, 128), np.float32)
    for dl in range(17):
        negc[:, dl, :] = np.where(16 * kl - tl <= 128 * dl - 31, 0.0, NEGV)
    c["negc"] = negc.reshape(128, 17 * 128)
    cc = np.arange(8)[None, :]
    c["cnc"] = np.where(16 * cc + 15 <= tl, 0.0, NEGV).astype(np.float32)
    f = np.zeros((128, 3), np.float32)
    f[:64] = [1e30, 2e30, -1e30]
    f[64:] = [0.0, 1e30, 2e30]
    c["force"] = f
    return c


def build_nsa_attn(S, debug=False):
    NT = S // 128
    NCMP = S // 16
    nc = bass.Bass("TRN2", target_bir_lowering=False)
    dt_in = lambda name, shape, dt: nc.dram_tensor(name, shape, dt, kind="ExternalInput").ap()
    QT = dt_in("QT", [64, 4, S], BF16)
    KcT = dt_in("KcT", [64, S], BF16)
    VcT = dt_in("VcT", [64, S], BF16)
    KsT = dt_in("KsT", [64, S], BF16)
    KwT = dt_in("KwT", [64, S], BF16)
    Vs1 = dt_in("Vs1", [128, NT, 65], BF16)
    Vw1 = dt_in("Vw1", [128, NT, 65], BF16)
    gat = dt_in("gat", [128, NT, 12], F32)
    w1k = dt_in("w1k", [64, 32 * 256], F32)
    w1v = dt_in("w1v", [64, 32 * 256], F32)
    w2k = dt_in("w2k", [128, 2 * 64], F32)
    w2v = dt_in("w2v", [128, 2 * 64], F32)
    pekT = dt_in("pekT", [64, 32], F32)
    pevT = dt_in("pevT", [64, 32], F32)
    c_ident = dt_in("ident", [128, 128], F32)
    c_i4 = dt_in("i4", [128, 512], F32)
    c_causal = dt_in("causal", [128, 128], F32)
    c_winneg = dt_in("winneg", [128, 128], F32)
    c_negc = dt_in("negc", [128, 17 * 128], F32)
    c_cnc = dt_in("cnc", [128, 8], F32)
    c_force = dt_in("force", [128, 3], F32)
    out = nc.dram_tensor("o", [S, 256], F32, kind="ExternalOutput").ap()
    C = Ctx(nc)
    P = C.P
    banks = C.banks

    def resident(name, src, shape, dt):
        b = P.sb(name, shape, dt)
        P.dma("sp", b[tuple(slice(None) for _ in shape)], src, writes=[b], dma_buf=b)
        return b

    KsT_sb = resident("KsT", KsT, [64, S], BF16)
    KwT_sb = resident("KwT", KwT, [64, S], BF16)
    Vs_sb = resident("Vs1", Vs1, [128, NT, 65], BF16)
    Vw_sb = resident("Vw1", Vw1, [128, NT, 65], BF16)
    gat_sb = resident("gat", gat, [128, NT, 12], F32)
    force_sb = resident("force", c_force, [128, 3], F32)

    def const_bf(name, src, n):
        b = P.sb(name, [128, n], BF16)
        for c0 in range(0, n, STG):
            m = min(STG, n - c0)
            C.load_cast(b, b[:, c0:c0 + m], src[:, c0:c0 + m], m)
        return b

    ident = const_bf("ident", c_ident, 128)
    i4 = const_bf("i4", c_i4, 512)
    causal = const_bf("causal", c_causal, 128)
    winneg = const_bf("winneg", c_winneg, 128)
    negc = const_bf("negc", c_negc, 17 * 128)
    cnc = const_bf("cnc", c_cnc, 8)

    KcC = P.sb("KcC", [64, NCMP], BF16)
    Vc1 = P.sb("Vc1", [128, NCMP // 128, 65], BF16)
    P.add("pool", lambda e: e.memset(KcC[:, :], 0.0), writes=[KcC])
    P.add("pool", lambda e: e.memset(Vc1[:, :, :], 0.0), writes=[Vc1])
    P.add("pool", lambda e: e.memset(Vc1[:, :, 64:65], 1.0), writes=[Vc1])
    src_sb = P.sb("cmpsrc", [64, S], BF16)
    w1_sb = P.sb("w1c", [64, 32 * 256], BF16)
    w2_sb = P.sb("w2c", [128, 128], BF16)
    pe_sb = P.sb("pec", [64, 32], BF16)
    bias_sb = P.sb("biasc", [128, 2], F32)
    hid = [P.sb(f"hid{i}", [128, NCMP], BF16) for i in range(2)]
    xs = P.sb("xs", [128, 512], F32)
    x2 = P.sb("x2", [128, 512], F32)
    sg = P.sb("sg", [128, 512], F32)
    NB = NCMP - 1
    for which, (srcT, w1d, w2d, ped) in enumerate(((KcT, w1k, w2k, pekT), (VcT, w1v, w2v, pevT))):
        P.dma("sp", src_sb[:, :], srcT, writes=[src_sb], dma_buf=src_sb)
        for c0 in range(0, 32 * 256, STG):
            C.load_cast(w1_sb, w1_sb[:, c0:c0 + STG], w1d[:, c0:c0 + STG], STG, parts=64)
        C.load_cast(w2_sb, w2_sb[:, :], w2d[:, :], 128)
        stg = C.stg[C.nstg % 2]
        C.nstg += 1
        P.dma("sp", stg[0:64, 0:32], ped, writes=[stg], dma_buf=stg)
        P.add("dve", lambda e, stg=stg: e.tensor_copy(out=pe_sb[:, :], in_=stg[0:64, 0:32]), reads=[stg],
              writes=[pe_sb])
        for hc in range(2):
            bb = banks[6]
            for j in range(32):
                P.mm(bb[:, 0:1], w1_sb[:, j * 256 + hc * 128: j * 256 + (hc + 1) * 128], pe_sb[:, j:j + 1],
                     start=(j == 0), stop=(j == 31), reads=[w1_sb, pe_sb], writes=[bb])
            P.add("dve", lambda e, bb=bb, hc=hc: e.tensor_copy(out=bias_sb[:, hc:hc + 1], in_=bb[:, 0:1]),
                  reads=[bb], writes=[bias_sb])
            for b0 in range(0, NB, 512):
                n = min(512, NB - b0)
                bank = banks[(b0 // 512) % 2]
                for j in range(32):
                    rhs = src_sb[:, j + 16 * b0: j + 16 * (b0 + n - 1) + 1: 16]
                    P.mm(bank[:, :n], w1_sb[:, j * 256 + hc * 128: j * 256 + (hc + 1) * 128], rhs,
                         start=(j == 0), stop=(j == 31), reads=[w1_sb, src_sb], writes=[bank])
                P.add("act", lambda e, bank=bank, n=n, hc=hc: e.activation(
                    out=xs[:, :n], in_=bank[:, :n], func=AF.Identity, bias=bias_sb[:, hc:hc + 1]),
                    reads=[bank, bias_sb], writes=[xs])
                P.add("dve", lambda e, n=n: e.tensor_tensor(out=x2[:, :n], in0=xs[:, :n], in1=xs[:, :n],
                                                            op=ALU.mult), reads=[xs], writes=[x2])
                P.add("dve", lambda e, n=n: e.tensor_scalar(out=x2[:, :n], in0=x2[:, :n], scalar1=0.044715,
                                                            scalar2=1.0, op0=ALU.mult, op1=ALU.add),
                      reads=[x2], writes=[x2])
                P.add("dve", lambda e, n=n: e.tensor_tensor(out=x2[:, :n], in0=x2[:, :n], in1=xs[:, :n],
                                                            op=ALU.mult), reads=[x2, xs], writes=[x2])
                P.add("act", lambda e, n=n: e.activation(out=sg[:, :n], in_=x2[:, :n], func=AF.Sigmoid,
                                                         scale=1.5957691216057308), reads=[x2], writes=[sg])
                P.add("dve", lambda e, n=n, b0=b0, hc=hc: e.tensor_tensor(
                    out=hid[hc][:, b0:b0 + n], in0=xs[:, :n], in1=sg[:, :n], op=ALU.mult),
                    reads=[xs, sg], writes=[hid[hc]])
        if which == 0:
            for b0 in range(0, NB, 512):
                n = min(512, NB - b0)
                bank = banks[2 + (b0 // 512) % 2]
                for hc in range(2):
                    P.mm(bank[0:64, :n], w2_sb[:, hc * 64:(hc + 1) * 64], hid[hc][:, b0:b0 + n], start=(hc == 0),
                         stop=(hc == 1), reads=[w2_sb, hid[hc]], writes=[bank])
                P.add("act", lambda e, bank=bank, n=n, b0=b0: e.activation(
                    out=KcC[:, b0:b0 + n], in_=bank[0:64, :n], func=AF.Copy), reads=[bank], writes=[KcC])
        else:
            for nt in range(NCMP // 128):
                n = min(128, NB - nt * 128)
                bank = banks[2 + nt % 2]
                for hc in range(2):
                    P.mm(bank[0:n, 0:64], hid[hc][:, nt * 128: nt * 128 + n], w2_sb[:, hc * 64:(hc + 1) * 64],
                         start=(hc == 0), stop=(hc == 1), reads=[w2_sb, hid[hc]], writes=[bank])
                P.add("act", lambda e, bank=bank, n=n, nt=nt: e.activation(
                    out=Vc1[0:n, nt, 0:64], in_=bank[0:n, 0:64], func=AF.Copy), reads=[bank], writes=[Vc1])

    qsb = [P.sb(f"q{i}", [64, 4, 128], BF16) for i in range(3)]
    PT = [P.sb(f"PT{i}", [128, 512], BF16) for i in range(3)]
    Eh = [P.sb(f"Eh{i}", [128, NCMP], F32) for i in range(2)]
    acc = P.sb("acc", [128, NCMP + 8], F32)
    imp = P.sb("imp", [128, 256], F32)
    work = P.sb("work", [128, 256], F32)
    m8 = P.sb("m8", [128, 16], F32)
    thr = P.sb("thr", [128, 1], F32)
    negm = P.sb("negm", [128, 256], BF16)
    nmx = [P.sb(f"nmx{i}", [128, 2, 64], BF16) for i in range(3)]
    osb = P.sb("osb", [128, 3, 260], F32)
    ocs = P.sb("ocs", [128, 260], F32)
    rinv = P.sb("rinv", [128, 4], F32)
    rden = P.sb("rden", [128, 3, 4], F32)
    coef = P.sb("coef", [128, 4, 3], F32)
    ot = [P.sb(f"ot{i}", [128, 4, 64], F32) for i in range(2)]
    P.add("pool", lambda e: e.memset(acc[:, :], 0.0), writes=[acc])
    bS = [banks[0], banks[1], banks[2]]
    bOc, bOs, bOw = banks[3], banks[4], banks[5]
    bE = [banks[6], banks[7]]
    cnt = {"s": 0, "e": 0, "x": 0}

    def attn_tile(KT_sb, kt, Q2, q_buf, masks, V_sb, v_idx, Obank, first):
        bank = bS[cnt["s"] % 3]
        pt = PT[cnt["s"] % 3]
        cnt["s"] += 1
        P.mm(bank[:, :], KT_sb[:, kt * 128:(kt + 1) * 128], Q2, start=True, stop=(len(masks) == 0),
             reads=[KT_sb, q_buf], writes=[bank])
        for mi, (m_ap, m_bufs) in enumerate(masks):
            P.mm(bank[:, :], m_ap, i4[:, :], start=False, stop=(mi == len(masks) - 1),
                 reads=list(m_bufs) + [i4], writes=[bank])
        P.add("act", lambda e: e.activation(out=pt[:, :], in_=bank[:, :], func=AF.Exp, scale=SCALE),
              reads=[bank], writes=[pt])
        for h in range(4):
            P.mm(Obank[:, h * 65:(h + 1) * 65], pt[:, h * 128:(h + 1) * 128], V_sb[:, v_idx, :],
                 start=(first and h == 0), stop=True, reads=[pt, V_sb], writes=[Obank], skip_group_check=True)

    for qt in range(NT):
        qb = qsb[qt % 3]
        P.dma("sp", qb[:, :, :], QT[:, :, qt * 128:(qt + 1) * 128], writes=[qb], dma_buf=qb)
        Q2 = qb[:, :, :].rearrange("p h t -> p (h t)")
        ncols = min(8 * qt + 7, NB)
        nnt = (ncols + 127) // 128
        for nt in range(nnt):
            dl = qt - 16 * nt
            masks = [(negc[:, dl * 128:(dl + 1) * 128], [negc])] if dl <= 16 else []
            attn_tile(KcC, nt, Q2, qb, masks, Vc1, nt, bOc, nt == 0)
        P.add("act", lambda e: e.activation(out=ocs[:, :], in_=bOc[:, 0:260], func=AF.Copy), reads=[bOc],
              writes=[ocs])
        P.add("pool", lambda e: e.tensor_copy(out=osb[:, 0, :], in_=ocs[:, :]), reads=[ocs], writes=[osb])
        sel = qt >= 8
        if sel:
            ocv = ocs[:, :].rearrange("p (h c) -> p h c", c=65)
            P.add("dve", lambda e, ocv=ocv: e.tensor_scalar(out=rinv[:, :], in0=ocv[:, :, 64], scalar1=1e-30,
                                                            scalar2=None, op0=ALU.add), reads=[ocs], writes=[rinv])
            P.add("dve", lambda e: e.reciprocal(out=rinv[:, :], in_=rinv[:, :]), reads=[rinv], writes=[rinv])
            w0 = 8 * qt - 1
            for h in range(4):
                eh = Eh[cnt["e"] % 2]
                for c0 in range(0, ncols, 512):
                    n = min(512, ncols - c0)
                    bank = bE[cnt["e"] % 2]
                    cnt["e"] += 1
                    lo = max(w0, c0)
                    hi = min(w0 + 8, c0 + n)
                    P.mm(bank[:, :n], qb[:, h, :], KcC[:, c0:c0 + n], start=True, stop=(hi <= lo),
                         reads=[qb, KcC], writes=[bank])
                    if hi > lo:
                        P.mm(bank[:, lo - c0:hi - c0], ident[:, :], cnc[:, lo - w0:hi - w0], start=False, stop=True,
                             reads=[ident, cnc], writes=[bank], skip_group_check=True)
                    P.add("act", lambda e, eh=eh, bank=bank, c0=c0, n=n: e.activation(
                        out=eh[:, c0:c0 + n], in_=bank[:, :n], func=AF.Exp, scale=SCALE), reads=[bank], writes=[eh])
                if h == 0:
                    P.add("dve", lambda e, eh=eh, ncols=ncols: e.tensor_scalar(
                        out=acc[:, :ncols], in0=eh[:, :ncols], scalar1=rinv[:, 0:1], scalar2=None, op0=ALU.mult),
                        reads=[eh, rinv], writes=[acc])
                else:
                    P.add("dve", lambda e, eh=eh, ncols=ncols, h=h: e.scalar_tensor_tensor(
                        out=acc[:, :ncols], in0=eh[:, :ncols], scalar=rinv[:, h:h + 1], in1=acc[:, :ncols],
                        op0=ALU.mult, op1=ALU.add), reads=[eh, rinv, acc], writes=[acc])
            nbk = 2 * qt + 2
            P.add("dve", lambda e, nbk=nbk: e.tensor_reduce(
                out=imp[:, 0:nbk], in_=acc[:, 0:4 * nbk].rearrange("p (s r) -> p s r", r=4), axis=AX.X, op=ALU.add),
                reads=[acc], writes=[imp])
            P.add("dve", lambda e, nbk=nbk: e.tensor_tensor(
                out=imp[:, 1:nbk], in0=imp[:, 1:nbk], in1=acc[:, 3:4 * (nbk - 1):4], op=ALU.add),
                reads=[imp, acc], writes=[imp])
            P.add("dve", lambda e, qt=qt: e.tensor_tensor(
                out=imp[:, 2 * qt - 1:2 * qt + 2], in0=imp[:, 2 * qt - 1:2 * qt + 2], in1=force_sb[:, :], op=ALU.add),
                reads=[imp, force_sb], writes=[imp])
            P.add("dve", lambda e: e.memset(imp[:, 0:1], 3e30), reads=[imp], writes=[imp])
            P.add("dve", lambda e, nbk=nbk: e.max(out=m8[:, 0:8], in_=imp[:, 0:nbk]), reads=[imp], writes=[m8])
            P.add("dve", lambda e, nbk=nbk: e.match_replace(out=work[:, 0:nbk], in_to_replace=m8[:, 0:8],
                                                            in_values=imp[:, 0:nbk], imm_value=-3e38),
                  reads=[imp, m8], writes=[work])
            P.add("dve", lambda e, nbk=nbk: e.max(out=m8[:, 8:16], in_=work[:, 0:nbk]), reads=[work], writes=[m8])
            P.add("dve", lambda e: e.tensor_reduce(out=thr[:, :], in_=m8[:, 8:16], axis=AX.X, op=ALU.min),
                  reads=[m8], writes=[thr])
            P.add("dve", lambda e, nbk=nbk: e.tensor_scalar(out=work[:, 0:nbk], in0=imp[:, 0:nbk], scalar1=thr[:, 0:1],
                                                            scalar2=None, op0=ALU.is_ge),
                  reads=[imp, thr], writes=[work])
            P.add("dve", lambda e, nbk=nbk: e.tensor_scalar(out=negm[:, 0:nbk], in0=work[:, 0:nbk], scalar1=-NEGV,
                                                            scalar2=NEGV, op0=ALU.mult, op1=ALU.add),
                  reads=[work], writes=[negm])
        k0 = max(0, qt - 4)
        for kt in range(k0, qt + 1):
            masks = []
            if kt == qt:
                masks.append((causal[:, :], [causal]))
            if kt == qt - 4:
                masks.append((winneg[:, :], [winneg]))
            attn_tile(KwT_sb, kt, Q2, qb, masks, Vw_sb, kt, bOw, kt == k0)
        P.add("act", lambda e: e.activation(out=osb[:, 2, :], in_=bOw[:, 0:260], func=AF.Copy), reads=[bOw],
              writes=[osb])
        for kt in range(0, qt + 1):
            masks = []
            if sel:
                nx = nmx[cnt["x"] % 3]
                cnt["x"] += 1
                P.add("pool", lambda e, nx=nx, kt=kt: e.tensor_copy(
                    out=nx[:, :, :], in_=negm[:, 2 * kt:2 * kt + 2].unsqueeze(2).to_broadcast([128, 2, 64])),
                    reads=[negm], writes=[nx])
                masks.append((nx[:, :, :].rearrange("p b k -> p (b k)"), [nx]))
            if kt == qt:
                masks.append((causal[:, :], [causal]))
            attn_tile(KsT_sb, kt, Q2, qb, masks, Vs_sb, kt, bOs, kt == 0)
        P.add("act", lambda e: e.activation(out=osb[:, 1, :], in_=bOs[:, 0:260], func=AF.Copy), reads=[bOs],
              writes=[osb])
        o = ot[qt % 2]
        ov = osb[:, :, :].rearrange("p b (h c) -> p b h c", c=65)
        P.add("dve", lambda e, ov=ov: e.tensor_scalar(out=rden[:, :, :], in0=ov[:, :, :, 64], scalar1=1e-30,
                                                      scalar2=None, op0=ALU.add), reads=[osb], writes=[rden])
        P.add("dve", lambda e: e.reciprocal(out=rden[:, :, :], in_=rden[:, :, :]), reads=[rden], writes=[rden])
        P.add("dve", lambda e, qt=qt: e.tensor_tensor(
            out=coef[:, :, :], in0=gat_sb[:, qt, :].rearrange("p (h b) -> p h b", b=3),
            in1=rden[:, :, :].rearrange("p b h -> p h b"), op=ALU.mult), reads=[gat_sb, rden], writes=[coef])
        for h in range(4):
            for br in range(3):
                if br == 0:
                    P.add("dve", lambda e, o=o, h=h: e.tensor_scalar(
                        out=o[:, h, :], in0=osb[:, 0, h * 65:h * 65 + 64], scalar1=coef[:, h, 0:1], scalar2=None,
                        op0=ALU.mult), reads=[osb, coef], writes=[o])
                else:
                    P.add("dve", lambda e, o=o, h=h, br=br: e.scalar_tensor_tensor(
                        out=o[:, h, :], in0=osb[:, br, h * 65:h * 65 + 64], scalar=coef[:, h, br:br + 1],
                        in1=o[:, h, :], op0=ALU.mult, op1=ALU.add), reads=[osb, coef, o], writes=[o])
        P.dma("pool", out[qt * 128:(qt + 1) * 128, :], o[:, :, :].rearrange("p h d -> p (h d)"), reads=[o],
              dma_buf=o)
    P.finalize()
    return nc


def run_nsa_attn(qkv, gates, w1k, w2k, pek, w1v, w2v, pev, B, S):
    NT = S // 128
    nc = build_nsa_attn(S)
    consts = nsa_consts()

    def w1l(w):
        return np.ascontiguousarray(w.reshape(32, 64, 256).transpose(1, 0, 2).reshape(64, 8192))

    def w2l(w):
        return np.ascontiguousarray(w.reshape(2, 128, 64).transpose(1, 0, 2).reshape(128, 128))

    shared = dict(consts)
    shared.update({"w1k": w1l(w1k), "w1v": w1l(w1v), "w2k": w2l(w2k), "w2v": w2l(w2v),
                   "pekT": np.ascontiguousarray(pek.T), "pevT": np.ascontiguousarray(pev.T)})
    in_maps = []
    for c in range(NCORES):
        b, g = c // 4, c % 4
        blk = qkv[b * S:(b + 1) * S]
        q = blk[:, 0:1024].reshape(S, 16, 64)[:, 4 * g:4 * g + 4, :]
        kv = [blk[:, 1024 + i * 256 + g * 64: 1024 + i * 256 + (g + 1) * 64] for i in range(6)]

        def v1(v):
            o = np.ones((128, NT, 65), dtype=qkv.dtype)
            o[:, :, :64] = v.reshape(NT, 128, 64).transpose(1, 0, 2)
            return o

        gt = gates[b * S:(b + 1) * S].reshape(S, 4, 12)[:, g, :]
        m = {"QT": np.ascontiguousarray(q.transpose(2, 1, 0)),
             "KcT": np.ascontiguousarray(kv[0].T), "VcT": np.ascontiguousarray(kv[1].T),
             "KsT": np.ascontiguousarray(kv[2].T), "Vs1": v1(kv[3]),
             "KwT": np.ascontiguousarray(kv[4].T), "Vw1": v1(kv[5]),
             "gat": np.ascontiguousarray(gt.reshape(NT, 128, 12).transpose(1, 0, 2))}
        m.update(shared)
        in_maps.append(m)
    res = run_bass_kernel_spmd(nc, in_maps, core_ids=list(range(NCORES)))
    o = np.zeros((B * S, 1024), np.float32)
    for c in range(NCORES):
        b, g = c // 4, c % 4
        o[b * S:(b + 1) * S, g * 256:(g + 1) * 256] = res.results[c]["o"]
    return o


CH = 64
SDT = F32
E05 = float(np.exp(-0.5))
RW_OUT_BF = ["aT", "rT", "bT", "kT", "BhT", "KhT", "vbT"]
RW_OUT_F = ["vT", "gT", "bonT"]


def build_rwkv_pre(T, has_vres, TT=128):
    nc = bass.Bass("TRN2", target_bir_lowering=False)
    din = lambda name, shape, dt=F32: nc.dram_tensor(name, shape, dt, kind="ExternalInput").ap()
    xT = din("xT", [D, T + 1])
    g = din("g", [128, 8])
    prm = din("prm", [128, 12 * 8])
    wrkv = [din(f"w{n}", [D, D]) for n in "rkv"]
    w1 = din("w1", [D, 64]); w2 = din("w2", [64, D])
    a1 = din("a1", [D, 64]); a2 = din("a2", [64, D])
    g1 = din("g1", [D, 160]); g2 = din("g2", [160, D])
    if has_vres:
        v1 = din("v1", [D, 32]); v2 = din("v2", [32, D])
        vfT = din("vfT", [D, T])
    bones = din("bones", [128, 128])
    rmask = din("rmask", [128, TT])
    outs = {n: nc.dram_tensor(n, [D, T], SDT, kind="ExternalOutput").ap() for n in RW_OUT_BF}
    outs.update({n: nc.dram_tensor(n, [D, T], F32, kind="ExternalOutput").ap() for n in RW_OUT_F})
    gC = nc.dram_tensor("gC", [D, T // CH], F32, kind="ExternalOutput").ap()
    C = Ctx(nc)
    P = C.P
    banks = C.banks
    v3 = lambda ap: ap.rearrange("(c p) t -> p c t", p=128)
    xv = v3(xT)
    ov = {n: v3(a) for n, a in outs.items()}
    gCv = v3(gC)

    def small(name, src, shape):
        b = P.sb(name, shape, F32)
        P.dma("sp", b[tuple(slice(None) for _ in shape)], src, writes=[b], dma_buf=b)
        return b

    g_sb = small("g", g, [128, 8])
    prm_sb = small("prm", prm, [128, 96])
    bones_sb = small("bones", bones, [128, 128])
    rmask_sb = small("rmask", rmask, [128, TT])
    pr = lambda i, c: prm_sb[:, i * 8 + c: i * 8 + c + 1]
    eps_t = P.sb("eps", [128, 1], F32)
    P.add("pool", lambda e: e.memset(eps_t[:, :], 1e-5), writes=[eps_t])
    Wr, Wk, Wv = [C.load_weight(f"W{n}_", w, D, D) for n, w in zip("rkv", wrkv)]
    W1 = C.load_weight("w1_", w1, D, 64)
    A1 = C.load_weight("a1_", a1, D, 64)
    G1 = C.load_weight("g1_", g1, D, 160)

    def load_rows(name, src, r0, nr):
        b = P.sb(name, [nr, D], BF16)
        C.load_cast(b, b[:, :], src[r0:r0 + nr, :], D, parts=nr)
        return b

    W2 = load_rows("w2_", w2, 0, 64)
    A2 = load_rows("a2_", a2, 0, 64)
    G2a = load_rows("g2a_", g2, 0, 128)
    G2b = load_rows("g2b_", g2, 128, 32)
    if has_vres:
        V1 = C.load_weight("v1_", v1, D, 32)
        V2 = load_rows("v2_", v2, 0, 32)
        vfv = v3(vfT)

    TH = TT + 1
    xb = [P.sb(f"x{i}", [128, 8, TH], F32) for i in range(2)]
    sq = P.sb("sq", [128, 8, TH], F32)
    rstd = P.sb("rstd", [128, TH], F32)
    hn = P.sb("hn", [128, 8, TH], F32)
    dx = P.sb("dx", [128, 8, TT], F32)
    xm = [P.sb(f"xm{i}", [128, 8, TT], BF16) for i in range(6)]
    lw = P.sb("lw", [64, TT], BF16)
    la = P.sb("la", [64, TT], BF16)
    lg = [P.sb("lga", [128, TT], BF16), P.sb("lgb", [32, TT], BF16)]
    lv = P.sb("lv", [32, TT], BF16)
    F = lambda name: P.sb(name, [128, TT], F32)
    r_t, k_t, v_t, a_t, dl_t, cum_t, kk_t, kh_t = [F(n) for n in ("r", "k", "v", "a", "dl", "cum", "kk", "kh")]
    t1, t2, t3, t4 = [F(n) for n in ("t1", "t2", "t3", "t4")]
    vf_t = F("vf")
    NO = len(RW_OUT_BF)
    obf = {n: [P.sb(f"o_{n}{i}", [128, 8, TT], SDT) for i in range(1)] for n in RW_OUT_BF}
    of32 = {n: [P.sb(f"o_{n}{i}", [128, 8, TT], F32) for i in range(1)] for n in RW_OUT_F}
    ogc = P.sb("ogc", [128, 8, TT // CH], F32)
    nt = T // TT
    nbk = [0]

    def proj(Wl, xin, fc, K=8):
        bank = banks[nbk[0] % 6]
        nbk[0] += 1
        for kc in range(K):
            P.mm(bank[:, :TT], Wl[kc][:, fc * 128:(fc + 1) * 128], xin[:, kc, :], start=(kc == 0), stop=(kc == K - 1),
                 reads=[Wl[kc], xin], writes=[bank])
        return bank

    def lora_down(Wl, xin, n):
        bank = banks[nbk[0] % 6]
        nbk[0] += 1
        for kc in range(8):
            P.mm(bank[0:n, :TT], Wl[kc][:, 0:n], xin[:, kc, :], start=(kc == 0), stop=(kc == 7),
                 reads=[Wl[kc], xin], writes=[bank])
        return bank

    def lora_up(parts, fc):
        bank = banks[nbk[0] % 6]
        nbk[0] += 1
        for i, (Wb, hb, n) in enumerate(parts):
            P.mm(bank[:, :TT], Wb[0:n, fc * 128:(fc + 1) * 128], hb[0:n, :], start=(i == 0),
                 stop=(i == len(parts) - 1), reads=[Wb, hb], writes=[bank])
        return bank

    for tt in range(nt):
        xt = xb[tt % 2]
        P.dma("sp", xt[:, :, :], xv[:, :, tt * TT: tt * TT + TH], writes=[xt], dma_buf=xt)
        P.add("act", lambda e, xt=xt: e.activation(out=sq[:, :, :], in_=xt[:, :, :], func=AF.Square),
              reads=[xt], writes=[sq])
        bk = banks[7]
        for c in range(8):
            P.mm(bk[:, :TH], C.ones32[:, :], sq[:, c, :], start=(c == 0), stop=(c == 7), reads=[C.ones32, sq],
                 writes=[bk])
        P.add("act", lambda e: e.activation(out=rstd[:, :], in_=bk[:, :TH], func=AF.Sqrt, bias=eps_t[:, 0:1],
                                            scale=1.0 / D), reads=[bk, eps_t], writes=[rstd])
        P.add("dve", lambda e: e.reciprocal(out=rstd[:, :], in_=rstd[:, :]), reads=[rstd], writes=[rstd])
        for c in range(8):
            P.add("dve", lambda e, c=c, xt=xt: e.scalar_tensor_tensor(
                out=hn[:, c, :], in0=xt[:, c, :], scalar=g_sb[:, c:c + 1], in1=rstd[:, :], op0=ALU.mult,
                op1=ALU.mult), reads=[xt, g_sb, rstd], writes=[hn])
        P.add("pool", lambda e: e.tensor_tensor(out=dx[:, :, :], in0=hn[:, :, 0:TT], in1=hn[:, :, 1:TH],
                                                op=ALU.subtract), reads=[hn], writes=[dx])
        for i in range(6):
            for c in range(8):
                P.add("dve", lambda e, i=i, c=c: e.scalar_tensor_tensor(
                    out=xm[i][:, c, :], in0=dx[:, c, :], scalar=pr(i, c), in1=hn[:, c, 1:TH], op0=ALU.mult,
                    op1=ALU.add), reads=[dx, hn, prm_sb], writes=[xm[i]])
        xr, xw, xk, xvv, xa, xg = xm
        bw = lora_down(W1, xw, 64)
        P.add("act", lambda e, bw=bw: e.activation(out=lw[:, :], in_=bw[0:64, :TT], func=AF.Tanh), reads=[bw],
              writes=[lw])
        ba = lora_down(A1, xa, 64)
        P.add("act", lambda e, ba=ba: e.activation(out=la[:, :], in_=ba[0:64, :TT], func=AF.Copy), reads=[ba],
              writes=[la])
        bg = lora_down(G1, xg, 128)
        P.add("act", lambda e, bg=bg: e.activation(out=lg[0][:, :], in_=bg[:, :TT], func=AF.Sigmoid), reads=[bg],
              writes=[lg[0]])
        bg2 = banks[nbk[0] % 6]
        nbk[0] += 1
        for kc in range(8):
            P.mm(bg2[0:32, :TT], G1[kc][:, 128:160], xg[:, kc, :], start=(kc == 0), stop=(kc == 7),
                 reads=[G1[kc], xg], writes=[bg2])
        P.add("act", lambda e, bg2=bg2: e.activation(out=lg[1][:, :], in_=bg2[0:32, :TT], func=AF.Sigmoid),
              reads=[bg2], writes=[lg[1]])
        if has_vres:
            bv = lora_down(V1, xvv, 32)
            P.add("act", lambda e, bv=bv: e.activation(out=lv[:, :], in_=bv[0:32, :TT], func=AF.Copy), reads=[bv],
                  writes=[lv])
        for fc in range(8):
            sl = slice(tt * TT, (tt + 1) * TT)
            b = proj(Wr, xr, fc)
            P.add("act", lambda e, b=b: e.activation(out=r_t[:, :], in_=b[:, :TT], func=AF.Copy), reads=[b],
                  writes=[r_t])
            b = proj(Wk, xk, fc)
            P.add("act", lambda e, b=b: e.activation(out=k_t[:, :], in_=b[:, :TT], func=AF.Copy), reads=[b],
                  writes=[k_t])
            b = proj(Wv, xvv, fc)
            P.add("act", lambda e, b=b: e.activation(out=v_t[:, :], in_=b[:, :TT], func=AF.Copy), reads=[b],
                  writes=[v_t])
            b = lora_up([(W2, lw, 64)], fc)
            P.add("act", lambda e, b=b, fc=fc: e.activation(out=dl_t[:, :], in_=b[:, :TT], func=AF.Sigmoid,
                                                            bias=pr(6, fc)), reads=[b, prm_sb], writes=[dl_t])
            P.add("pool", lambda e: e.tensor_scalar(out=dl_t[:, :], in0=dl_t[:, :], scalar1=-E05, scalar2=None,
                                                    op0=ALU.mult), reads=[dl_t], writes=[dl_t])
            b = lora_up([(A2, la, 64)], fc)
            P.add("act", lambda e, b=b, fc=fc: e.activation(out=a_t[:, :], in_=b[:, :TT], func=AF.Sigmoid,
                                                            bias=pr(7, fc)), reads=[b, prm_sb], writes=[a_t])
            b = lora_up([(G2a, lg[0], 128), (G2b, lg[1], 32)], fc)
            og = of32["gT"][0]
            P.add("act", lambda e, b=b, fc=fc, og=og: e.activation(out=og[:, fc, :], in_=b[:, :TT], func=AF.Copy),
                  reads=[b], writes=[og])
            if has_vres:
                b = lora_up([(V2, lv, 32)], fc)
                P.add("act", lambda e, b=b, fc=fc: e.activation(out=t1[:, :], in_=b[:, :TT], func=AF.Sigmoid,
                                                                bias=pr(11, fc)), reads=[b, prm_sb], writes=[t1])
                P.dma("sp", vf_t[:, :], vfv[:, fc, sl], writes=[vf_t], dma_buf=vf_t)
                P.add("pool", lambda e: e.tensor_tensor(out=vf_t[:, :], in0=vf_t[:, :], in1=v_t[:, :],
                                                        op=ALU.subtract), reads=[vf_t, v_t], writes=[vf_t])
                P.add("pool", lambda e: e.tensor_tensor(out=vf_t[:, :], in0=vf_t[:, :], in1=t1[:, :], op=ALU.mult),
                      reads=[vf_t, t1], writes=[vf_t])
                P.add("pool", lambda e: e.tensor_tensor(out=v_t[:, :], in0=v_t[:, :], in1=vf_t[:, :], op=ALU.add),
                      reads=[vf_t, v_t], writes=[v_t])
            ovf = of32["vT"][0]
            ovb = obf["vbT"][0]
            P.add("pool", lambda e, fc=fc, ovf=ovf: e.tensor_copy(out=ovf[:, fc, :], in_=v_t[:, :]), reads=[v_t],
                  writes=[ovf])
            P.add("pool", lambda e, fc=fc, ovb=ovb: e.tensor_copy(out=ovb[:, fc, :], in_=v_t[:, :]), reads=[v_t],
                  writes=[ovb])
            P.add("dve", lambda e, fc=fc: e.tensor_scalar(out=kk_t[:, :], in0=k_t[:, :], scalar1=pr(8, fc),
                                                          scalar2=None, op0=ALU.mult), reads=[k_t, prm_sb],
                  writes=[kk_t])
            P.add("dve", lambda e: e.tensor_tensor(out=t2[:, :], in0=kk_t[:, :], in1=kk_t[:, :], op=ALU.mult),
                  reads=[kk_t], writes=[t2])
            bn = banks[6]
            P.mm(bn[:, :TT], bones_sb[:, :], t2[:, :], start=True, stop=True, reads=[bones_sb, t2], writes=[bn])
            P.add("act", lambda e, bn=bn: e.activation(out=t2[:, :], in_=bn[:, :TT], func=AF.Sqrt), reads=[bn],
                  writes=[t2])
            P.add("dve", lambda e: e.tensor_scalar(out=t2[:, :], in0=t2[:, :], scalar1=1e-12, scalar2=None,
                                                   op0=ALU.max), reads=[t2], writes=[t2])
            P.add("dve", lambda e: e.reciprocal(out=t2[:, :], in_=t2[:, :]), reads=[t2], writes=[t2])
            P.add("dve", lambda e: e.tensor_tensor(out=kk_t[:, :], in0=kk_t[:, :], in1=t2[:, :], op=ALU.mult),
                  reads=[kk_t, t2], writes=[kk_t])
            P.add("dve", lambda e, fc=fc: e.tensor_scalar(out=kh_t[:, :], in0=a_t[:, :], scalar1=-1.0,
                                                          scalar2=pr(9, fc), op0=ALU.add, op1=ALU.mult),
                  reads=[a_t, prm_sb], writes=[kh_t])
            P.add("dve", lambda e: e.scalar_tensor_tensor(out=kh_t[:, :], in0=kh_t[:, :], scalar=1.0, in1=k_t[:, :],
                                                          op0=ALU.add, op1=ALU.mult), reads=[kh_t, k_t],
                  writes=[kh_t])
            P.add("dve", lambda e, fc=fc: e.scalar_tensor_tensor(out=t3[:, :], in0=r_t[:, :], scalar=pr(10, fc),
                                                                 in1=kh_t[:, :], op0=ALU.mult, op1=ALU.mult),
                  reads=[r_t, kh_t, prm_sb], writes=[t3])
            bn2 = banks[7]
            P.mm(bn2[:, :TT], bones_sb[:, :], t3[:, :], start=True, stop=True, reads=[bones_sb, t3], writes=[bn2])
            ob = of32["bonT"][0]
            P.add("dve", lambda e, fc=fc, ob=ob, bn2=bn2: e.tensor_tensor(out=ob[:, fc, :], in0=bn2[:, :TT],
                                                                         in1=v_t[:, :], op=ALU.mult),
                  reads=[bn2, v_t], writes=[ob])
            P.add("dve", lambda e: e.tensor_tensor_scan(out=cum_t[:, :], data0=rmask_sb[:, :], data1=dl_t[:, :],
                                                        initial=0.0, op0=ALU.mult, op1=ALU.add),
                  reads=[rmask_sb, dl_t], writes=[cum_t])
            P.add("act", lambda e: e.activation(out=t1[:, :], in_=cum_t[:, :], func=AF.Exp, scale=-1.0),
                  reads=[cum_t], writes=[t1])
            P.add("act", lambda e: e.activation(out=t2[:, :], in_=cum_t[:, :], func=AF.Exp), reads=[cum_t],
                  writes=[t2])
            P.add("pool", lambda e: e.tensor_tensor(out=t4[:, :], in0=cum_t[:, :], in1=dl_t[:, :], op=ALU.subtract),
                  reads=[cum_t, dl_t], writes=[t4])
            P.add("act", lambda e: e.activation(out=t4[:, :], in_=t4[:, :], func=AF.Exp), reads=[t4], writes=[t4])
            o = obf["aT"][0]
            P.add("dve", lambda e, fc=fc, o=o: e.scalar_tensor_tensor(out=o[:, fc, :], in0=kk_t[:, :], scalar=-1.0,
                                                                     in1=t4[:, :], op0=ALU.mult, op1=ALU.mult),
                  reads=[kk_t, t4], writes=[o])
            o = obf["rT"][0]
            P.add("pool", lambda e, fc=fc, o=o: e.tensor_tensor(out=o[:, fc, :], in0=r_t[:, :], in1=t2[:, :],
                                                               op=ALU.mult), reads=[r_t, t2], writes=[o])
            P.add("dve", lambda e: e.tensor_tensor(out=t3[:, :], in0=kk_t[:, :], in1=a_t[:, :], op=ALU.mult),
                  reads=[kk_t, a_t], writes=[t3])
            P.add("dve", lambda e: e.tensor_tensor(out=t3[:, :], in0=t3[:, :], in1=t1[:, :], op=ALU.mult),
                  reads=[t3, t1], writes=[t3])
            P.add("pool", lambda e: e.tensor_tensor(out=kh_t[:, :], in0=kh_t[:, :], in1=t1[:, :], op=ALU.mult),
                  reads=[kh_t, t1], writes=[kh_t])
            o = obf["bT"][0]
            P.add("pool", lambda e, fc=fc, o=o: e.tensor_copy(out=o[:, fc, :], in_=t3[:, :]), reads=[t3], writes=[o])
            o = obf["kT"][0]
            P.add("pool", lambda e, fc=fc, o=o: e.tensor_copy(out=o[:, fc, :], in_=kh_t[:, :]), reads=[kh_t],
                  writes=[o])
            gcv = t2[:, :].rearrange("p (n c) -> p n c", c=CH)[:, :, CH - 1:CH]
            P.add("pool", lambda e, fc=fc, gcv=gcv: e.tensor_copy(out=ogc[:, fc, :].unsqueeze(2), in_=gcv),
                  reads=[t2], writes=[ogc])
            gcb = gcv.to_broadcast([128, TT // CH, CH])
            o = obf["BhT"][0]
            P.add("dve", lambda e, fc=fc, o=o, gcb=gcb: e.tensor_tensor(
                out=o[:, fc, :].rearrange("p (n c) -> p n c", c=CH), in0=t3[:, :].rearrange("p (n c) -> p n c", c=CH),
                in1=gcb, op=ALU.mult), reads=[t3, t2], writes=[o])
            o = obf["KhT"][0]
            P.add("dve", lambda e, fc=fc, o=o, gcb=gcb: e.tensor_tensor(
                out=o[:, fc, :].rearrange("p (n c) -> p n c", c=CH), in0=kh_t[:, :].rearrange("p (n c) -> p n c", c=CH),
                in1=gcb, op=ALU.mult), reads=[kh_t, t2], writes=[o])
        sl = slice(tt * TT, (tt + 1) * TT)
        for n in RW_OUT_BF:
            P.dma("pool", ov[n][:, :, sl], obf[n][0][:, :, :], reads=[obf[n][0]], dma_buf=obf[n][0])
        for n in RW_OUT_F:
            P.dma("pool", ov[n][:, :, sl], of32[n][0][:, :, :], reads=[of32[n][0]], dma_buf=of32[n][0])
        P.dma("pool", gCv[:, :, tt * (TT // CH):(tt + 1) * (TT // CH)], ogc[:, :, :], reads=[ogc], dma_buf=ogc)
    P.finalize()
    return nc


GN_EPS = 64e-5


def scan_consts():
    s = np.arange(128)[:, None]
    t = np.arange(128)[None, :]
    same = (s // CH) == (t // CH)
    return {"mstrict": (same & (s < t)).astype(np.float32), "mincl": (same & (s <= t)).astype(np.float32),
            "mstrictT": (same & (t < s)).astype(np.float32), "identf": np.eye(128, dtype=np.float32)}


def build_rwkv_scan(S):
    NW = S // 128
    NCH = 128 // CH
    L = int(np.log2(CH))
    nc = bass.Bass("TRN2", target_bir_lowering=False)
    din = lambda name, shape, dt=F32: nc.dram_tensor(name, shape, dt, kind="ExternalInput").ap()
    fm = din("fm", [4, 64, NW, 512], SDT)
    tk = din("tk", [4, 128, NW, 256], SDT)
    gC = din("gC", [4, 64, S // CH])
    lnw = din("lnw", [4, 64, 64])
    lnb = din("lnb", [4, 64, 64])
    c_ms = din("mstrict", [128, 128]); c_mi = din("mincl", [128, 128]); c_mt = din("mstrictT", [128, 128])
    c_id = din("identf", [128, 128])
    yout = nc.dram_tensor("yn", [4, S, 64], F32, kind="ExternalOutput").ap()
    C = Ctx(nc)
    P = C.P
    banks = C.banks

    def small(name, src, shape):
        b = P.sb(name, shape, F32)
        P.dma("sp", b[tuple(slice(None) for _ in shape)], src, writes=[b], dma_buf=b)
        return b

    ms = small("ms", c_ms, [128, 128]); mi = small("mi", c_mi, [128, 128]); mt = small("mt", c_mt, [128, 128])
    idf = small("idf", c_id, [128, 128])
    gC_sb = [small(f"gC{h}", gC[h], [64, S // CH]) for h in range(4)]
    lnw_sb = [small(f"lnw{h}", lnw[h], [64, 64]) for h in range(4)]
    lnb_sb = [small(f"lnb{h}", lnb[h], [64, 64]) for h in range(4)]
    NBUF = 3
    fmb = [[P.sb(f"fm{h}_{i}", [64, 512], SDT) for i in range(NBUF)] for h in range(4)]
    tkb = [[P.sb(f"tk{h}_{i}", [128, 256], SDT) for i in range(NBUF)] for h in range(4)]

    def per_head(name, shape, dt, n=2):
        return [[P.sb(f"{name}{h}_{i}", shape, dt) for i in range(n)] for h in range(4)]

    Abr = per_head("Abr", [128, 128], SDT)
    Aak = per_head("Aak", [128, 128], SDT)
    Akr = per_head("Akr", [128, 128], SDT)
    Xn = per_head("Xn", [128, 128], SDT)
    Xt = per_head("Xt", [128, 128], SDT)
    Pf = per_head("Pf", [128, 128], F32, 1)
    Pb = per_head("Pb", [128, 128], SDT)
    axb = per_head("axb", [128, 128], SDT)
    wv = per_head("wv", [128, 128], SDT)
    qeff = per_head("qeff", [64, 128], F32)
    Tc = per_head("Tc", [64, 64], F32)
    ST = per_head("ST", [64, 64], F32)
    yc = per_head("yc", [64, 64], F32)
    ysq = per_head("ysq", [64, 64], F32, 1)
    st = per_head("st", [64, 4], F32)
    yo = per_head("yo", [64, 64], F32)
    for h in range(4):
        P.add("pool", lambda e, h=h: e.memset(ST[h][0][:, :], 0.0), writes=[ST[h][0]])
    nb = [0]

    def bank():
        b = banks[nb[0] % 8]
        nb[0] += 1
        return b

    eng_rr = [0]

    def ev():
        e = ["dve", "pool"][eng_rr[0] % 2]
        eng_rr[0] += 1
        return e

    nstate = [0, 0, 0, 0]
    def load_win(w):
        for h in range(4):
            f = fmb[h][w % NBUF]
            t = tkb[h][w % NBUF]
            P.dma("sp", f[:, :], fm[h, :, w, :], writes=[f], dma_buf=f)
            P.dma("sp", t[:, :], tk[h, :, w, :], writes=[t], dma_buf=t)

    load_win(0)
    for w in range(NW):
        i2 = w % 2
        if w + 1 < NW:
            load_win(w + 1)
        for h in range(4):
            f = fmb[h][w % NBUF]
            t = tkb[h][w % NBUF]
            aT, rT, bT, kT = f[:, 0:128], f[:, 128:256], f[:, 256:384], f[:, 384:512]
            a_tok, Bh, Kh, v_tok = t[:, 0:64], t[:, 64:128], t[:, 128:192], t[:, 192:256]
            abr, aak, akr = Abr[h][i2], Aak[h][i2], Akr[h][i2]
            b1 = bank()
            P.mm(b1[:, 0:256], bT, f[:, 0:256], start=True, stop=True, reads=[f], writes=[b1])
            xn, xt = Xn[h][0], Xt[h][0]
            P.add("dve", lambda e, b1=b1, xn=xn: e.tensor_tensor(out=xn[:, :], in0=b1[:, 0:128], in1=ms[:, :],
                                                                op=ALU.mult), reads=[b1, ms], writes=[xn])
            P.add("dve", lambda e, b1=b1, abr=abr: e.tensor_tensor(out=abr[:, :], in0=b1[:, 128:256], in1=mi[:, :],
                                                                  op=ALU.mult), reads=[b1, mi], writes=[abr])
            pf, pb = Pf[h][0], Pb[h][0]
            P.add("dve", lambda e, b1=b1, pf=pf: e.tensor_tensor(out=pf[:, :], in0=b1[:, 0:128], in1=ms[:, :],
                                                                op=ALU.mult), reads=[b1, ms], writes=[pf])
            P.add("pool", lambda e, pf=pf: e.tensor_tensor(out=pf[:, :], in0=pf[:, :], in1=idf[:, :], op=ALU.add),
                  reads=[pf, idf], writes=[pf])
            P.add("pool", lambda e, pf=pf, pb=pb: e.tensor_copy(out=pb[:, :], in_=pf[:, :]), reads=[pf], writes=[pb])
            b2 = bank()
            P.mm(b2[:, 0:256], kT, f[:, 0:256], start=True, stop=True, reads=[f], writes=[b2])
            P.add("dve", lambda e, b2=b2, aak=aak: e.tensor_tensor(out=aak[:, :], in0=b2[:, 0:128], in1=ms[:, :],
                                                                  op=ALU.mult), reads=[b2, ms], writes=[aak])
            P.add("dve", lambda e, b2=b2, akr=akr: e.tensor_tensor(out=akr[:, :], in0=b2[:, 128:256], in1=mi[:, :],
                                                                  op=ALU.mult), reads=[b2, mi], writes=[akr])
            b3 = bank()
            P.mm(b3[:, 0:128], aT, bT, start=True, stop=True, reads=[f], writes=[b3])
            P.add("dve", lambda e, b3=b3, xt=xt: e.tensor_tensor(out=xt[:, :], in0=b3[:, 0:128], in1=mt[:, :],
                                                                op=ALU.mult), reads=[b3, mt], writes=[xt])
            cur_n, cur_t = xn, xt
            for k in range(1, L):
                nxt_n, nxt_t = Xn[h][k % 2], Xt[h][k % 2]
                bt_ = bank()
                P.mm(bt_[:, 0:128], cur_n[:, :], cur_t[:, :], start=True, stop=True, reads=[cur_n, cur_t],
                     writes=[bt_])
                if k < L - 1:
                    bn_ = bank()
                    P.mm(bn_[:, 0:128], cur_t[:, :], cur_n[:, :], start=True, stop=True, reads=[cur_n, cur_t],
                         writes=[bn_])
                P.add("act", lambda e, bt_=bt_, nxt_t=nxt_t: e.activation(out=nxt_t[:, :], in_=bt_[:, 0:128],
                                                                         func=AF.Copy), reads=[bt_], writes=[nxt_t])
                if k < L - 1:
                    P.add("act", lambda e, bn_=bn_, nxt_n=nxt_n: e.activation(out=nxt_n[:, :], in_=bn_[:, 0:128],
                                                                             func=AF.Copy), reads=[bn_],
                          writes=[nxt_n])
                bp = bank()
                P.mm(bp[:, 0:128], nxt_t[:, :], pb[:, :], start=True, stop=True, reads=[nxt_t, pb], writes=[bp])
                P.add("dve", lambda e, bp=bp, pf=pf: e.tensor_tensor(out=pf[:, :], in0=pf[:, :], in1=bp[:, 0:128],
                                                                    op=ALU.add), reads=[pf, bp], writes=[pf])
                pb = Pb[h][k % 2]
                P.add("pool", lambda e, pf=pf, pb=pb: e.tensor_copy(out=pb[:, :], in_=pf[:, :]), reads=[pf],
                      writes=[pb])
                cur_n, cur_t = nxt_n, nxt_t
            tinv = pb
            ax = axb[h][i2]
            bx = bank()
            P.mm(bx[:, 0:64], aak[:, :], v_tok, start=True, stop=True, reads=[aak, t], writes=[bx])
            P.add("pool", lambda e, ax=ax, a_tok=a_tok: e.tensor_copy(out=ax[:, 0:64], in_=a_tok), reads=[t],
                  writes=[ax])
            P.add("act", lambda e, ax=ax, bx=bx: e.activation(out=ax[:, 64:128], in_=bx[:, 0:64], func=AF.Copy),
                  reads=[bx, ax], writes=[ax])
            wvb = wv[h][i2]
            bw = bank()
            P.mm(bw[:, 0:128], tinv[:, :], ax[:, :], start=True, stop=True, reads=[tinv, ax], writes=[bw])
            P.add("act", lambda e, wvb=wvb, bw=bw: e.activation(out=wvb[:, :], in_=bw[:, 0:128], func=AF.Copy),
                  reads=[bw], writes=[wvb])
            qe = qeff[h][i2]
            bq = bank()
            P.mm(bq[0:64, 0:128], wvb[:, 0:64], abr[:, :], start=True, stop=True, reads=[wvb, abr], writes=[bq])
            P.add("dve", lambda e, qe=qe, bq=bq, rT=rT: e.tensor_tensor(out=qe[:, :], in0=bq[0:64, 0:128], in1=rT,
                                                                       op=ALU.add), reads=[bq, f], writes=[qe])
            for c in range(NCH):
                ps = slice(c * CH, (c + 1) * CH)
                ci = nstate[h]
                nstate[h] += 1
                s_old, s_new = ST[h][ci % 2], ST[h][(ci + 1) % 2]
                tc = Tc[h][ci % 2]
                btc = bank()
                P.mm(btc[0:64, 0:64], wvb[ps, 0:64], Bh[ps, :], start=True, stop=True, reads=[wvb, t], writes=[btc])
                gidx = w * NCH + c
                P.add("dve", lambda e, tc=tc, btc=btc, h=h, gidx=gidx: e.scalar_tensor_tensor(
                    out=tc[:, :], in0=idf[0:64, 0:64], scalar=gC_sb[h][:, gidx:gidx + 1], in1=btc[0:64, 0:64],
                    op0=ALU.mult, op1=ALU.add), reads=[idf, gC_sb[h], btc], writes=[tc])
                by = bank()
                P.mm(by[0:64, 0:64], abr[ps, ps], wvb[ps, 64:128], start=True, stop=False, reads=[abr, wvb],
                     writes=[by])
                P.mm(by[0:64, 0:64], akr[ps, ps], v_tok[ps, :], start=False, stop=False, reads=[akr, t], writes=[by])
                bys = bank()
                P.mm(bys[0:64, 0:64], qe[:, ps], s_old[:, :], start=True, stop=True, reads=[qe, s_old], writes=[bys])
                bs = bank()
                P.mm(bs[0:64, 0:64], Bh[ps, :], wvb[ps, 64:128], start=True, stop=False, reads=[t, wvb], writes=[bs])
                P.mm(bs[0:64, 0:64], Kh[ps, :], v_tok[ps, :], start=False, stop=True, reads=[t], writes=[bs])
                bs2 = bank()
                P.mm(bs2[0:64, 0:64], tc[:, :], s_old[:, :], start=True, stop=True, reads=[tc, s_old], writes=[bs2])
                P.add("act", lambda e, s_new=s_new, bs=bs: e.activation(out=s_new[:, :], in_=bs[0:64, 0:64],
                                                                       func=AF.Copy), reads=[bs], writes=[s_new])
                P.add("dve", lambda e, s_new=s_new, bs2=bs2: e.tensor_tensor(
                    out=s_new[:, :], in0=s_new[:, :], in1=bs2[0:64, 0:64], op=ALU.add), reads=[s_new, bs2],
                    writes=[s_new])
                y = yc[h][ci % 2]
                s4 = st[h][ci % 2]
                P.add("act", lambda e, y=y, by=by: e.activation(out=y[:, :], in_=by[0:64, 0:64], func=AF.Copy),
                      reads=[by], writes=[y])
                P.add("dve", lambda e, y=y, bys=bys: e.tensor_tensor(out=y[:, :], in0=y[:, :], in1=bys[0:64, 0:64],
                                                                    op=ALU.add), reads=[y, bys], writes=[y])
                P.add("dve", lambda e, y=y, s4=s4: e.tensor_reduce(out=s4[:, 0:1], in_=y[:, :], axis=AX.X,
                                                                   op=ALU.add), reads=[y], writes=[s4])
                P.add("dve", lambda e, s4=s4: e.tensor_scalar(out=s4[:, 0:1], in0=s4[:, 0:1], scalar1=-1.0 / 64,
                                                              scalar2=None, op0=ALU.mult), reads=[s4], writes=[s4])
                P.add("act", lambda e, y=y, s4=s4: e.activation(out=y[:, :], in_=y[:, :], func=AF.Identity,
                                                                bias=s4[:, 0:1]), reads=[y, s4], writes=[y])
                sqb = ysq[h][0]
                P.add("pool", lambda e, y=y, sqb=sqb: e.tensor_tensor(out=sqb[:, :], in0=y[:, :], in1=y[:, :],
                                                                     op=ALU.mult), reads=[y], writes=[sqb])
                P.add("dve", lambda e, sqb=sqb, s4=s4: e.tensor_reduce(out=s4[:, 1:2], in_=sqb[:, :], axis=AX.X,
                                                                       op=ALU.add), reads=[sqb], writes=[s4])
                P.add("dve", lambda e, s4=s4: e.tensor_scalar(out=s4[:, 1:2], in0=s4[:, 1:2], scalar1=1.0 / 64,
                                                              scalar2=GN_EPS, op0=ALU.mult, op1=ALU.add),
                      reads=[s4], writes=[s4])
                P.add("act", lambda e, s4=s4: e.activation(out=s4[:, 2:3], in_=s4[:, 1:2], func=AF.Sqrt),
                      reads=[s4], writes=[s4])
                P.add("dve", lambda e, s4=s4: e.reciprocal(out=s4[:, 3:4], in_=s4[:, 2:3]), reads=[s4], writes=[s4])
                o = yo[h][ci % 2]
                P.add("dve", lambda e, o=o, y=y, s4=s4, h=h: e.scalar_tensor_tensor(
                    out=o[:, :], in0=y[:, :], scalar=s4[:, 3:4], in1=lnw_sb[h][:, :], op0=ALU.mult, op1=ALU.mult),
                    reads=[y, s4, lnw_sb[h]], writes=[o])
                P.add("pool", lambda e, o=o, h=h: e.tensor_tensor(out=o[:, :], in0=o[:, :], in1=lnb_sb[h][:, :],
                                                                  op=ALU.add), reads=[o, lnb_sb[h]], writes=[o])
                t0 = w * 128 + c * CH
                P.dma("pool", yout[h, t0:t0 + CH, :], o[:, :], reads=[o], dma_buf=o)
    P.finalize()
    return nc


def lay8(v):
    return np.ascontiguousarray(v.reshape(8, 128).T)


def run_rwkv_pre(x_tok, S, g, x_mix, w_rkv, w0, w1, w2, a0, a1, a2, g1, g2, k_k, k_a, r_k, vres, vfT_full, TT=128):
    ntok = x_tok.shape[0]
    T = ntok // NCORES
    has_vres = vres is not None
    nc = build_rwkv_pre(T, has_vres, TT=TT)
    v0 = vres[0] if has_vres else np.zeros(D, np.float32)
    prm = np.concatenate([lay8(x_mix[i]) for i in range(6)] + [lay8(w0), lay8(a0), lay8(k_k), lay8(k_a), lay8(r_k),
                                                                 lay8(v0)], axis=1)
    bones = np.kron(np.eye(2, dtype=np.float32), np.ones((64, 64), np.float32))
    rmask = np.ones((128, TT), np.float32)
    rmask[:, ::CH] = 0.0
    in_maps = []
    for c in range(NCORES):
        xs = np.zeros((T + 1, D), np.float32)
        xs[1:] = x_tok[c * T:(c + 1) * T]
        if (c * T) % S != 0:
            xs[0] = x_tok[c * T - 1]
        m = {"xT": np.ascontiguousarray(xs.T), "g": lay8(g), "prm": np.ascontiguousarray(prm),
             "wr": w_rkv[0], "wk": w_rkv[1], "wv": w_rkv[2], "w1": w1, "w2": w2, "a1": a1, "a2": a2, "g1": g1,
             "g2": g2, "bones": bones, "rmask": rmask}
        if has_vres:
            m.update({"v1": vres[1], "v2": vres[2], "vfT": np.ascontiguousarray(vfT_full[:, c * T:(c + 1) * T])})
        in_maps.append(m)
    res = run_bass_kernel_spmd(nc, in_maps, core_ids=list(range(NCORES)))
    out = {}
    for n in RW_OUT_BF + RW_OUT_F + ["gC"]:
        out[n] = np.concatenate([r[n] for r in res.results], axis=1)
    return out


def run_rwkv_scan(pre, ln_w, ln_b, B, S):
    NW = S // 128
    nc = build_rwkv_scan(S)
    consts = scan_consts()
    in_maps = []
    for c in range(NCORES):
        b, hg = c // 4, c % 4
        fm = np.zeros((4, 64, NW, 512), dtype=pre["aT"].dtype)
        tk = np.zeros((4, 128, NW, 256), dtype=pre["aT"].dtype)
        gC = np.zeros((4, 64, S // CH), np.float32)
        lnw = np.zeros((4, 64, 64), np.float32)
        lnb = np.zeros((4, 64, 64), np.float32)
        for hh in range(4):
            ch = slice((4 * hg + hh) * 64, (4 * hg + hh + 1) * 64)
            ts = slice(b * S, (b + 1) * S)
            pc = {n: pre[n][ch, ts].reshape(64, NW, 128) for n in RW_OUT_BF}
            for i, n in enumerate(["aT", "rT", "bT", "kT"]):
                fm[hh, :, :, i * 128:(i + 1) * 128] = pc[n]
            for i, n in enumerate(["aT", "BhT", "KhT", "vbT"]):
                tk[hh, :, :, i * 64:(i + 1) * 64] = pc[n].transpose(2, 1, 0)
            gC[hh] = pre["gC"][ch, b * (S // CH):(b + 1) * (S // CH)]
            lnw[hh] = np.broadcast_to(ln_w[ch][None, :], (64, 64))
            lnb[hh] = np.broadcast_to(ln_b[ch][None, :], (64, 64))
        m = {"fm": fm, "tk": tk, "gC": gC, "lnw": lnw, "lnb": lnb}
        m.update(consts)
        in_maps.append(m)
    res = run_bass_kernel_spmd(nc, in_maps, core_ids=list(range(NCORES)))
    ynT = np.zeros((D, B * S), np.float32)
    for c in range(NCORES):
        b, hg = c // 4, c % 4
        y = res.results[c]["yn"]
        for hh in range(4):
            ynT[(4 * hg + hh) * 64:(4 * hg + hh + 1) * 64, b * S:(b + 1) * S] = y[hh].T
    return ynT


def run_linres(xT_full, w, ins):
    ntok = xT_full.shape[1]
    T = ntok // NCORES
    nc = build_linres(T, n_in=len(ins))
    names = ["aT", "bT", "cT"]
    in_maps = []
    for c in range(NCORES):
        sl = slice(c * T, (c + 1) * T)
        m = {"xT": np.ascontiguousarray(xT_full[:, sl]), "w": w}
        for n, a in zip(names, ins):
            m[n] = np.ascontiguousarray(a[:, sl].astype(np.float32))
        in_maps.append(m)
    res = run_bass_kernel_spmd(nc, in_maps, core_ids=list(range(NCORES)))
    return np.concatenate([r["yT"] for r in res.results], axis=1)


def kernel(**inp):
    inp = {k: np.asarray(v) for k, v in inp.items()}
    x = inp["x"]
    B, S, _ = x.shape
    ntok = B * S
    xT = np.ascontiguousarray(x.reshape(ntok, D).T)
    vfT = None
    for i in range(4):
        j = i // 2
        if i % 2 == 0:
            qkv, gates = run_nsa_inproj(np.ascontiguousarray(xT.T), inp["norm_mix"][i], inp["nsa_w_in"][j], S)
            o = run_nsa_attn(qkv, gates, inp["nsa_cmp_w1_k"][j], inp["nsa_cmp_w2_k"][j], inp["nsa_cmp_pe_k"][j],
                             inp["nsa_cmp_w1_v"][j], inp["nsa_cmp_w2_v"][j], inp["nsa_cmp_pe_v"][j], B, S)
            xT = run_linres(xT, inp["nsa_w_out"][j], [np.ascontiguousarray(o.T)])
        else:
            vres = None if j == 0 else (inp["rwkv_v0"][j - 1], inp["rwkv_v1"][j - 1], inp["rwkv_v2"][j - 1])
            pre = run_rwkv_pre(np.ascontiguousarray(xT.T), S, inp["norm_mix"][i], inp["rwkv_x_mix"][j],
                               inp["rwkv_w_rkv"][j], inp["rwkv_w0"][j], inp["rwkv_w1"][j], inp["rwkv_w2"][j],
                               inp["rwkv_a0"][j], inp["rwkv_a1"][j], inp["rwkv_a2"][j], inp["rwkv_g1"][j],
                               inp["rwkv_g2"][j], inp["rwkv_k_k"][j], inp["rwkv_k_a"][j], inp["rwkv_r_k"][j],
                               vres, vfT)
            if j == 0:
                vfT = pre["vT"]
            ynT = run_rwkv_scan(pre, inp["rwkv_ln_w"][j], inp["rwkv_ln_b"][j], B, S)
            xT = run_linres(xT, inp["rwkv_w_out"][j], [ynT, pre["bonT"], pre["gT"]])
        xT = run_mlp_T(xT, inp["norm_mlp"][i], inp["mlp_w1"][i], inp["mlp_w2"][i],
                       gf=inp["norm_final"] if i == 3 else None)
    return np.ascontiguousarray(xT.T).reshape(B, S, D).astype(np.float32)
```

```python
import numpy as np
from contextlib import ExitStack
import concourse.bass as bass
import concourse.mybir as mybir
from concourse.bass_utils import run_bass_kernel_spmd

F32 = mybir.dt.float32
BF16 = mybir.dt.bfloat16
AF = mybir.ActivationFunctionType
ALU = mybir.AluOpType
AX = mybir.AxisListType

NCORES = 8
D = 1024
STG = 1024
SAME_SYNC = True


class Buf:
    def __init__(self, t, name):
        self.t = t
        self.name = name
        self.writers = {}
        self.readers = {}
        self.wgroup = None

    def __getitem__(self, idx):
        return self.t[idx]


class Op:
    __slots__ = ("eng", "fn", "stream", "pos", "waits", "needs_inc", "val", "is_dma")


class Prog:
    ENGS = ["pe", "act", "dve", "pool", "sp"]

    def __init__(self, nc):
        self.nc = nc
        self.stack = ExitStack()
        self.ops = {e: [] for e in self.ENGS}
        self.streams = {e: [] for e in self.ENGS}
        self.seen = {e: {} for e in self.ENGS}
        self.nbuf = 0

    def sb(self, name, shape, dt):
        self.nbuf += 1
        t = self.stack.enter_context(self.nc.sbuf_tensor(f"{name}_{self.nbuf}", list(shape), dt))
        return Buf(t, f"{name}_{self.nbuf}")

    def ps(self, name, shape, dt=F32):
        self.nbuf += 1
        t = self.stack.enter_context(self.nc.psum_tensor(f"{name}_{self.nbuf}", list(shape), dt))
        return Buf(t, f"{name}_{self.nbuf}")

    def add(self, eng, fn, reads=(), writes=(), dma_buf=None, group=None):
        op = Op()
        op.eng = eng
        op.fn = fn
        op.is_dma = dma_buf is not None
        op.needs_inc = op.is_dma
        op.val = None
        op.stream = ("dma", dma_buf.name) if op.is_dma else eng
        st = self.streams.setdefault(op.stream, [])
        op.pos = len(st)
        st.append(op)
        deps = {}

        def need(p):
            if p is op:
                return
            if (not p.is_dma) and p.stream == eng and not op.is_dma:
                if eng == "pe" or not SAME_SYNC:
                    return
            cur = deps.get(p.stream)
            if cur is None or cur.pos < p.pos:
                deps[p.stream] = p

        for b in reads:
            for p in b.writers.values():
                need(p)
        for b in writes:
            same_group = group is not None and b.wgroup == group
            if not same_group:
                for p in b.writers.values():
                    need(p)
            for p in b.readers.values():
                need(p)
        op.waits = []
        seen = self.seen[eng]
        for s, p in deps.items():
            if seen.get(s, -1) >= p.pos:
                continue
            seen[s] = p.pos
            p.needs_inc = True
            op.waits.append(p)
        for b in reads:
            b.readers[op.stream] = op
        for b in writes:
            same_group = group is not None and b.wgroup == group
            if same_group:
                b.writers[op.stream] = op
            else:
                b.writers = {op.stream: op}
                b.wgroup = group
            b.readers = {}
        self.ops[eng].append(op)
        return op

    def dma(self, q, out, in_, reads=(), writes=(), dma_buf=None, group=None, **kw):
        return self.add(q, lambda e: e.dma_start(out=out, in_=in_, **kw), reads, writes,
                        dma_buf=dma_buf, group=group)

    def mm(self, out, lhsT, rhs, start, stop, reads=(), writes=(), **kw):
        return self.add("pe", lambda e: e.matmul(out, lhsT, rhs, start=start, stop=stop, **kw), reads, writes)

    def finalize(self):
        nc = self.nc
        sems = {}
        for s, st in self.streams.items():
            if not any(o.needs_inc for o in st):
                continue
            nm = s if isinstance(s, str) else "d_" + s[1]
            sems[s] = self.stack.enter_context(nc.semaphore("s_" + nm))
            c = 0
            for o in st:
                if o.needs_inc:
                    c += 16 if o.is_dma else 1
                o.val = c
        self.nsem = len(sems)
        finals = [(sems[s], st[-1].val) for s, st in self.streams.items() if not isinstance(s, str) and st]
        block = self.stack.enter_context(nc.Block())
        engmap = {"pe": block.tensor, "act": block.scalar, "dve": block.vector, "pool": block.gpsimd,
                  "sp": block.sync}

        def make(engname):
            ops = self.ops[engname]

            def body(e):
                for o in ops:
                    for p in o.waits:
                        e.wait_ge(sems[p.stream], p.val)
                    inst = o.fn(e)
                    if o.needs_inc:
                        inst.then_inc(sems[o.stream], 16 if o.is_dma else 1)
                if engname == "sp":
                    for sem, v in finals:
                        e.wait_ge(sem, v)
            return body

        for en in self.ENGS:
            engmap[en](make(en))
        self.stack.close()


class Ctx:
    def __init__(self, nc):
        self.nc = nc
        self.P = Prog(nc)
        P = self.P
        self.ones32 = P.sb("ones32", [128, 128], F32)
        P.add("pool", lambda e: e.memset(self.ones32[:, :], 1.0), writes=[self.ones32])
        self.banks = [P.ps(f"bank{i}", [128, 512], F32) for i in range(8)]
        self.stg = [P.sb(f"stg{i}", [128, STG], F32) for i in range(2)]
        self.nstg = 0
        self.ncast = 0

    def load_cast(self, dst_buf, dst_ap, src_ap, ncols, eng=None, parts=128):
        P = self.P
        stg = self.stg[self.nstg % 2]
        q = "sp"
        self.nstg += 1
        P.dma(q, stg[0:parts, :ncols], src_ap, writes=[stg], dma_buf=stg)
        if eng is None:
            eng = ["pool", "dve"][self.ncast % 2]
            self.ncast += 1
        P.add(eng, lambda e: e.tensor_copy(out=dst_ap, in_=stg[0:parts, :ncols]), reads=[stg], writes=[dst_buf])

    def load_weight(self, name, w_dram, K, N, eng=None):
        P = self.P
        wv = w_dram.rearrange("(kc p) n -> p kc n", p=128)
        out = []
        for kc in range(K // 128):
            b = P.sb(f"{name}{kc}", [128, N], BF16)
            for c0 in range(0, N, STG):
                n = min(STG, N - c0)
                self.load_cast(b, b[:, c0:c0 + n], wv[:, kc, c0:c0 + n], n, eng=eng)
            out.append(b)
        return out

    def rmsnorm(self, xt, g_sb, hn, TT, sq, rstd, bank, eps_t):
        P = self.P
        P.add("act", lambda e: e.activation(out=sq[:, :, :], in_=xt[:, :, :TT], func=AF.Square),
              reads=[xt], writes=[sq])
        for c in range(8):
            P.mm(bank[:, :TT], self.ones32[:, :], sq[:, c, :], start=(c == 0), stop=(c == 7),
                 reads=[self.ones32, sq], writes=[bank])
        P.add("act", lambda e: e.activation(out=rstd[:, :], in_=bank[:, :TT], func=AF.Sqrt,
                                            bias=eps_t[:, 0:1], scale=1.0 / D),
              reads=[bank, eps_t], writes=[rstd])
        P.add("dve", lambda e: e.reciprocal(out=rstd[:, :], in_=rstd[:, :]), reads=[rstd], writes=[rstd])
        for c in range(8):
            P.add("dve", lambda e, c=c: e.scalar_tensor_tensor(
                out=hn[:, c, :], in0=xt[:, c, :TT], scalar=g_sb[:, c:c + 1], in1=rstd[:, :],
                op0=ALU.mult, op1=ALU.mult), reads=[xt, g_sb, rstd], writes=[hn])


def build_mlp(T, TT=256, final=False):
    nc = bass.Bass("TRN2", target_bir_lowering=False)
    xT = nc.dram_tensor("xT", [D, T], F32, kind="ExternalInput").ap()
    g = nc.dram_tensor("g", [128, 8], F32, kind="ExternalInput").ap()
    w1 = nc.dram_tensor("w1", [D, 4 * D], F32, kind="ExternalInput").ap()
    w2 = nc.dram_tensor("w2", [4 * D, D], F32, kind="ExternalInput").ap()
    yT = nc.dram_tensor("yT", [D, T], F32, kind="ExternalOutput").ap()
    C = Ctx(nc)
    P = C.P
    xv = xT.rearrange("(c p) t -> p c t", p=128)
    yv = yT.rearrange("(c p) t -> p c t", p=128)
    g_sb = P.sb("g", [128, 8], F32)
    P.dma("sp", g_sb[:, :], g, writes=[g_sb], dma_buf=g_sb)
    if final:
        gf = nc.dram_tensor("gf", [128, 8], F32, kind="ExternalInput").ap()
        gf_sb = P.sb("gf", [128, 8], F32)
        P.dma("sp", gf_sb[:, :], gf, writes=[gf_sb], dma_buf=gf_sb)
    eps_t = P.sb("eps", [128, 1], F32)
    P.add("pool", lambda e: e.memset(eps_t[:, :], 1e-5), writes=[eps_t])
    xb = [P.sb(f"x{i}", [128, 8, TT], F32) for i in range(2)]
    sq = P.sb("sq", [128, 8, TT], F32)
    rstd = P.sb("rstd", [128, TT], F32)
    hn = P.sb("hn", [128, 8, TT], BF16)
    h1 = [P.sb(f"h1_{i}", [128, 8, TT], BF16) for i in range(4)]
    rl = [P.sb(f"rl{i}", [128, TT], F32) for i in range(3)]
    P.dma("sp", xb[0][:, :, :], xv[:, :, 0:TT], writes=[xb[0]], dma_buf=xb[0])
    w1s = C.load_weight("w1_", w1, D, 4 * D)
    w2s = C.load_weight("w2_", w2, 4 * D, D)
    nt = T // TT
    nb = 0
    for tt in range(nt):
        xt = xb[tt % 2]
        if tt + 1 < nt:
            nx = xb[(tt + 1) % 2]
            P.dma("sp", nx[:, :, :], xv[:, :, (tt + 1) * TT:(tt + 2) * TT], writes=[nx], dma_buf=nx)
        C.rmsnorm(xt, g_sb, hn, TT, sq, rstd, C.banks[7], eps_t)
        for fc in range(32):
            bank = C.banks[nb % 4]
            nb += 1
            for kc in range(8):
                P.mm(bank[:, :TT], w1s[kc][:, fc * 128:(fc + 1) * 128], hn[:, kc, :], start=(kc == 0),
                     stop=(kc == 7), reads=[w1s[kc], hn], writes=[bank])
            r = rl[fc % 3]
            P.add("act", lambda e, r=r, bank=bank: e.activation(out=r[:, :], in_=bank[:, :TT], func=AF.Relu),
                  reads=[bank], writes=[r])
            hb = h1[fc // 8]
            P.add("pool", lambda e, r=r, hb=hb, fc=fc: e.tensor_tensor(
                out=hb[:, fc % 8, :], in0=r[:, :], in1=r[:, :], op=ALU.mult), reads=[r], writes=[hb])
        for fc in range(8):
            bank = C.banks[4 + fc % 2]
            for kc in range(32):
                P.mm(bank[:, :TT], w2s[kc][:, fc * 128:(fc + 1) * 128], h1[kc // 8][:, kc % 8, :],
                     start=(kc == 0), stop=(kc == 31), reads=[w2s[kc], h1[kc // 8]], writes=[bank])
            P.add("dve", lambda e, xt=xt, bank=bank, fc=fc: e.tensor_tensor(
                out=xt[:, fc, :], in0=xt[:, fc, :], in1=bank[:, :TT], op=ALU.add), reads=[xt, bank], writes=[xt])
        if final:
            C.rmsnorm(xt, gf_sb, xt, TT, sq, rstd, C.banks[7], eps_t)
        P.dma("pool", yv[:, :, tt * TT:(tt + 1) * TT], xt[:, :, :], reads=[xt], dma_buf=xt)
    P.finalize()
    return nc


def run_mlp_T(xT_full, g, w1, w2, gf=None):
    ntok = xT_full.shape[1]
    T = ntok // NCORES
    nc = build_mlp(T, final=gf is not None)
    in_maps = []
    for c in range(NCORES):
        m = {"xT": np.ascontiguousarray(xT_full[:, c * T:(c + 1) * T]), "g": lay8(g), "w1": w1, "w2": w2}
        if gf is not None:
            m["gf"] = lay8(gf)
        in_maps.append(m)
    res = run_bass_kernel_spmd(nc, in_maps, core_ids=list(range(NCORES)))
    return np.concatenate([r["yT"] for r in res.results], axis=1)


def run_mlp(x_tok, g, w1, w2):
    ntok = x_tok.shape[0]
    T = ntok // NCORES
    nc = build_mlp(T)
    g_l = np.ascontiguousarray(g.reshape(8, 128).T)
    in_maps = [{"xT": np.ascontiguousarray(x_tok[c * T:(c + 1) * T].T), "g": g_l, "w1": w1, "w2": w2}
               for c in range(NCORES)]
    res = run_bass_kernel_spmd(nc, in_maps, core_ids=list(range(NCORES)))
    return np.concatenate([r["yT"].T for r in res.results], axis=0)


def build_linres(T, n_in=1, TT=512):
    TT = min(TT, T)
    nc = bass.Bass("TRN2", target_bir_lowering=False)
    xT = nc.dram_tensor("xT", [D, T], F32, kind="ExternalInput").ap()
    aT = nc.dram_tensor("aT", [D, T], F32, kind="ExternalInput").ap()
    if n_in == 3:
        bT = nc.dram_tensor("bT", [D, T], F32, kind="ExternalInput").ap()
        cT = nc.dram_tensor("cT", [D, T], F32, kind="ExternalInput").ap()
    w = nc.dram_tensor("w", [D, D], F32, kind="ExternalInput").ap()
    yT = nc.dram_tensor("yT", [D, T], F32, kind="ExternalOutput").ap()
    C = Ctx(nc)
    P = C.P
    v3 = lambda ap: ap.rearrange("(c p) t -> p c t", p=128)
    xv, av, yv = v3(xT), v3(aT), v3(yT)
    ws = C.load_weight("w_", w, D, D)
    xb = [P.sb(f"x{i}", [128, 8, TT], F32) for i in range(2)]
    ab = [P.sb(f"a{i}", [128, 8, TT], F32) for i in range(2)]
    if n_in == 3:
        bv, cv = v3(bT), v3(cT)
        bb = [P.sb(f"b{i}", [128, 8, TT], F32) for i in range(2)]
        cb = [P.sb(f"c{i}", [128, 8, TT], F32) for i in range(2)]
    z = P.sb("z", [128, 8, TT], BF16)
    nt = T // TT
    for tt in range(nt):
        sl = slice(tt * TT, (tt + 1) * TT)
        xt, at = xb[tt % 2], ab[tt % 2]
        P.dma("sp", xt[:, :, :], xv[:, :, sl], writes=[xt], dma_buf=xt)
        P.dma("sp", at[:, :, :], av[:, :, sl], writes=[at], dma_buf=at)
        if n_in == 3:
            bt, ct = bb[tt % 2], cb[tt % 2]
            P.dma("sp", bt[:, :, :], bv[:, :, sl], writes=[bt], dma_buf=bt)
            P.dma("sp", ct[:, :, :], cv[:, :, sl], writes=[ct], dma_buf=ct)
            P.add("pool", lambda e, at=at, bt=bt: e.tensor_tensor(out=at[:, :, :], in0=at[:, :, :], in1=bt[:, :, :],
                                                                  op=ALU.add), reads=[at, bt], writes=[at])
            P.add("dve", lambda e, at=at, ct=ct: e.tensor_tensor(out=z[:, :, :], in0=at[:, :, :], in1=ct[:, :, :],
                                                                 op=ALU.mult), reads=[at, ct], writes=[z])
        else:
            P.add("pool", lambda e, at=at: e.tensor_copy(out=z[:, :, :], in_=at[:, :, :]), reads=[at], writes=[z])
        for fc in range(8):
            bank = C.banks[fc % 4]
            for kc in range(8):
                P.mm(bank[:, :TT], ws[kc][:, fc * 128:(fc + 1) * 128], z[:, kc, :], start=(kc == 0),
                     stop=(kc == 7), reads=[ws[kc], z], writes=[bank])
            P.add("dve", lambda e, xt=xt, bank=bank, fc=fc: e.tensor_tensor(
                out=xt[:, fc, :], in0=xt[:, fc, :], in1=bank[:, :TT], op=ALU.add), reads=[xt, bank], writes=[xt])
        P.dma("pool", yv[:, :, sl], xt[:, :, :], reads=[xt], dma_buf=xt)
    P.finalize()
    return nc


NSA_PROJ = 2608


def build_nsa_inproj(T, TT=256):
    nc = bass.Bass("TRN2", target_bir_lowering=False)
    xT = nc.dram_tensor("xT", [D, T], F32, kind="ExternalInput").ap()
    g = nc.dram_tensor("g", [128, 8], F32, kind="ExternalInput").ap()
    w = nc.dram_tensor("w", [D, NSA_PROJ], F32, kind="ExternalInput").ap()
    cosx = nc.dram_tensor("cosx", [T, 64], F32, kind="ExternalInput").ap()
    sinx = nc.dram_tensor("sinx", [T, 64], F32, kind="ExternalInput").ap()
    qkv = nc.dram_tensor("qkv", [T, 2560], BF16, kind="ExternalOutput").ap()
    gates = nc.dram_tensor("gates", [T, 48], F32, kind="ExternalOutput").ap()
    C = Ctx(nc)
    P = C.P
    xv = xT.rearrange("(c p) t -> p c t", p=128)
    g_sb = P.sb("g", [128, 8], F32)
    P.dma("sp", g_sb[:, :], g, writes=[g_sb], dma_buf=g_sb)
    eps_t = P.sb("eps", [128, 1], F32)
    P.add("pool", lambda e: e.memset(eps_t[:, :], 1e-5), writes=[eps_t])
    nsub = T // 128
    cs = P.sb("cs", [128, nsub, 8, 8], F32)
    sn = P.sb("sn", [128, nsub, 8, 8], F32)
    P.dma("sp", cs[:, :, :, :], cosx.rearrange("(n p) (h d) -> p n h d", p=128, d=8), writes=[cs], dma_buf=cs)
    P.dma("sp", sn[:, :, :, :], sinx.rearrange("(n p) (h d) -> p n h d", p=128, d=8), writes=[sn], dma_buf=sn)
    xb = [P.sb(f"x{i}", [128, 8, TT], F32) for i in range(2)]
    sq = P.sb("sq", [128, 8, TT], F32)
    rstd = P.sb("rstd", [128, TT], F32)
    hn = P.sb("hn", [128, 8, TT], BF16)
    ob = [P.sb(f"ob{i}", [128, 8, 64], F32) for i in range(3)]
    tmp = [P.sb(f"tmp{i}", [128, 8, 8], F32) for i in range(4)]
    obig = [P.sb(f"obig{i}", [128, 2560], BF16) for i in range(2)]
    gsb = [P.sb(f"gsb{i}", [128, 48], F32) for i in range(2)]
    P.dma("sp", xb[0][:, :, :], xv[:, :, 0:TT], writes=[xb[0]], dma_buf=xb[0])
    ws = C.load_weight("w_", w, D, NSA_PROJ)
    nt = T // TT
    nb = 0
    for tt in range(nt):
        xt = xb[tt % 2]
        if tt + 1 < nt:
            nx = xb[(tt + 1) % 2]
            P.dma("sp", nx[:, :, :], xv[:, :, (tt + 1) * TT:(tt + 2) * TT], writes=[nx], dma_buf=nx)
        C.rmsnorm(xt, g_sb, hn, TT, sq, rstd, C.banks[7], eps_t)
        for ts in range(TT // 128):
            isub = tt * (TT // 128) + ts
            og = obig[isub % 2]
            for cc in range(6):
                c0 = cc * 512
                n = min(512, NSA_PROJ - c0)
                bank = C.banks[nb % 4]
                nb += 1
                for kc in range(8):
                    P.mm(bank[:, :n], hn[:, kc, ts * 128:(ts + 1) * 128], ws[kc][:, c0:c0 + n], start=(kc == 0),
                         stop=(kc == 7), reads=[ws[kc], hn], writes=[bank])
                if cc == 5:
                    gs = gsb[isub % 2]
                    P.add("act", lambda e, gs=gs, bank=bank: e.activation(out=gs[:, :], in_=bank[:, :48],
                                                                         func=AF.Sigmoid), reads=[bank], writes=[gs])
                    P.dma("pool", gates[isub * 128:(isub + 1) * 128, :], gs[:, :], reads=[gs], dma_buf=gs)
                    continue
                o = ob[nb % 3]
                P.add("act", lambda e, o=o, bank=bank: e.activation(
                    out=o[:, :, :], in_=bank[:, :512].rearrange("p (h d) -> p h d", d=64), func=AF.Copy),
                    reads=[bank], writes=[o])
                nh = 8 if cc < 2 else 4
                A = o[:, 0:nh, 0:8]
                B = o[:, 0:nh, 8:16]
                cst = cs[:, isub, 0:nh, :]
                snt = sn[:, isub, 0:nh, :]
                t1, t2, t3, t4 = [t[:, 0:nh, :] for t in tmp]
                for (dst, a, b) in ((t1, A, cst), (t2, B, snt), (t3, B, cst), (t4, A, snt)):
                    P.add("dve", lambda e, dst=dst, a=a, b=b: e.tensor_tensor(out=dst, in0=a, in1=b, op=ALU.mult),
                          reads=[o, cs, sn], writes=[tmp[0]])
                P.add("dve", lambda e, A=A, t1=t1, t2=t2: e.tensor_tensor(out=A, in0=t1, in1=t2, op=ALU.subtract),
                      reads=[tmp[0]], writes=[o])
                P.add("dve", lambda e, B=B, t3=t3, t4=t4: e.tensor_tensor(out=B, in0=t3, in1=t4, op=ALU.add),
                      reads=[tmp[0]], writes=[o])
                P.add("pool", lambda e, og=og, o=o, c0=c0: e.tensor_copy(
                    out=og[:, c0:c0 + 512].rearrange("p (h d) -> p h d", d=64), in_=o[:, :, :]),
                    reads=[o], writes=[og])
            P.dma("pool", qkv[isub * 128:(isub + 1) * 128, :], og[:, :], reads=[og], dma_buf=og)
    P.finalize()
    return nc


def rope_tables(S):
    inv = (500000.0 ** (-np.arange(0, 16, 2, dtype=np.float32) / 16.0)).astype(np.float32)
    ang = np.arange(S, dtype=np.float32)[:, None] * inv[None, :]
    return np.cos(ang).astype(np.float32), np.sin(ang).astype(np.float32)


def run_nsa_inproj(x_tok, g, w_in, S):
    ntok = x_tok.shape[0]
    T = ntok // NCORES
    nc = build_nsa_inproj(T)
    g_l = np.ascontiguousarray(g.reshape(8, 128).T)
    cos, sin = rope_tables(S)
    cosx = np.tile(cos, (1, 8))
    sinx = np.tile(sin, (1, 8))
    in_maps = []
    for c in range(NCORES):
        pos = (np.arange(c * T, (c + 1) * T)) % S
        in_maps.append({"xT": np.ascontiguousarray(x_tok[c * T:(c + 1) * T].T), "g": g_l, "w": w_in,
                        "cosx": np.ascontiguousarray(cosx[pos]), "sinx": np.ascontiguousarray(sinx[pos])})
    res = run_bass_kernel_spmd(nc, in_maps, core_ids=list(range(NCORES)))
    qkv = np.concatenate([r["qkv"] for r in res.results], axis=0)
    gates = np.concatenate([r["gates"] for r in res.results], axis=0)
    return qkv, gates


NEGV = -30000.0
SCALE = 0.125


def nsa_consts():
    tl = np.arange(128)[:, None]
    kl = np.arange(128)[None, :]
    c = {}
    c["ident"] = np.eye(128, dtype=np.float32)
    c["i4"] = np.tile(np.eye(128, dtype=np.float32), (1, 4))
    c["causal"] = np.where(kl > tl, NEGV, 0.0).astype(np.float32)
    c["winneg"] = np.where(kl > tl, 0.0, NEGV).astype(np.float32)
    negc = np.zeros((128, 17, 128), np.float32)
    for dl in range(17):
        negc[:, dl, :] = np.where(16 * kl - tl <= 128 * dl - 31, 0.0, NEGV)
    c["negc"] = negc.reshape(128, 17 * 128)
    cc = np.arange(8)[None, :]
    c["cnc"] = np.where(16 * cc + 15 <= tl, 0.0, NEGV).astype(np.float32)
    f = np.zeros((128, 3), np.float32)
    f[:64] = [1e30, 2e30, -1e30]
    f[64:] = [0.0, 1e30, 2e30]
    c["force"] = f
    return c


def build_nsa_attn(S, debug=False):
    NT = S // 128
    NCMP = S // 16
    nc = bass.Bass("TRN2", target_bir_lowering=False)
    dt_in = lambda name, shape, dt: nc.dram_tensor(name, shape, dt, kind="ExternalInput").ap()
    QT = dt_in("QT", [64, 4, S], BF16)
    KcT = dt_in("KcT", [64, S], BF16)
    VcT = dt_in("VcT", [64, S], BF16)
    KsT = dt_in("KsT", [64, S], BF16)
    KwT = dt_in("KwT", [64, S], BF16)
    Vs1 = dt_in("Vs1", [128, NT, 65], BF16)
    Vw1 = dt_in("Vw1", [128, NT, 65], BF16)
    gat = dt_in("gat", [128, NT, 12], F32)
    w1k = dt_in("w1k", [64, 32 * 256], F32)
    w1v = dt_in("w1v", [64, 32 * 256], F32)
    w2k = dt_in("w2k", [128, 2 * 64], F32)
    w2v = dt_in("w2v", [128, 2 * 64], F32)
    pekT = dt_in("pekT", [64, 32], F32)
    pevT = dt_in("pevT", [64, 32], F32)
    c_ident = dt_in("ident", [128, 128], F32)
    c_i4 = dt_in("i4", [128, 512], F32)
    c_causal = dt_in("causal", [128, 128], F32)
    c_winneg = dt_in("winneg", [128, 128], F32)
    c_negc = dt_in("negc", [128, 17 * 128], F32)
    c_cnc = dt_in("cnc", [128, 8], F32)
    c_force = dt_in("force", [128, 3], F32)
    out = nc.dram_tensor("o", [S, 256], F32, kind="ExternalOutput").ap()
    C = Ctx(nc)
    P = C.P
    banks = C.banks

    def resident(name, src, shape, dt):
        b = P.sb(name, shape, dt)
        P.dma("sp", b[tuple(slice(None) for _ in shape)], src, writes=[b], dma_buf=b)
        return b

    KsT_sb = resident("KsT", KsT, [64, S], BF16)
    KwT_sb = resident("KwT", KwT, [64, S], BF16)
    Vs_sb = resident("Vs1", Vs1, [128, NT, 65], BF16)
    Vw_sb = resident("Vw1", Vw1, [128, NT, 65], BF16)
    gat_sb = resident("gat", gat, [128, NT, 12], F32)
    force_sb = resident("force", c_force, [128, 3], F32)

    def const_bf(name, src, n):
        b = P.sb(name, [128, n], BF16)
        for c0 in range(0, n, STG):
            m = min(STG, n - c0)
            C.load_cast(b, b[:, c0:c0 + m], src[:, c0:c0 + m], m)
        return b

    ident = const_bf("ident", c_ident, 128)
    i4 = const_bf("i4", c_i4, 512)
    causal = const_bf("causal", c_causal, 128)
    winneg = const_bf("winneg", c_winneg, 128)
    negc = const_bf("negc", c_negc, 17 * 128)
    cnc = const_bf("cnc", c_cnc, 8)

    KcC = P.sb("KcC", [64, NCMP], BF16)
    Vc1 = P.sb("Vc1", [128, NCMP // 128, 65], BF16)
    P.add("pool", lambda e: e.memset(KcC[:, :], 0.0), writes=[KcC])
    P.add("pool", lambda e: e.memset(Vc1[:, :, :], 0.0), writes=[Vc1])
    P.add("pool", lambda e: e.memset(Vc1[:, :, 64:65], 1.0), writes=[Vc1])
    src_sb = P.sb("cmpsrc", [64, S], BF16)
    w1_sb = P.sb("w1c", [64, 32 * 256], BF16)
    w2_sb = P.sb("w2c", [128, 128], BF16)
    pe_sb = P.sb("pec", [64, 32], BF16)
    bias_sb = P.sb("biasc", [128, 2], F32)
    hid = [P.sb(f"hid{i}", [128, NCMP], BF16) for i in range(2)]
    xs = P.sb("xs", [128, 512], F32)
    x2 = P.sb("x2", [128, 512], F32)
    sg = P.sb("sg", [128, 512], F32)
    NB = NCMP - 1
    for which, (srcT, w1d, w2d, ped) in enumerate(((KcT, w1k, w2k, pekT), (VcT, w1v, w2v, pevT))):
        P.dma("sp", src_sb[:, :], srcT, writes=[src_sb], dma_buf=src_sb)
        for c0 in range(0, 32 * 256, STG):
            C.load_cast(w1_sb, w1_sb[:, c0:c0 + STG], w1d[:, c0:c0 + STG], STG, parts=64)
        C.load_cast(w2_sb, w2_sb[:, :], w2d[:, :], 128)
        stg = C.stg[C.nstg % 2]
        C.nstg += 1
        P.dma("sp", stg[0:64, 0:32], ped, writes=[stg], dma_buf=stg)
        P.add("dve", lambda e, stg=stg: e.tensor_copy(out=pe_sb[:, :], in_=stg[0:64, 0:32]), reads=[stg],
              writes=[pe_sb])
        for hc in range(2):
            bb = banks[6]
            for j in range(32):
                P.mm(bb[:, 0:1], w1_sb[:, j * 256 + hc * 128: j * 256 + (hc + 1) * 128], pe_sb[:, j:j + 1],
                     start=(j == 0), stop=(j == 31), reads=[w1_sb, pe_sb], writes=[bb])
            P.add("dve", lambda e, bb=bb, hc=hc: e.tensor_copy(out=bias_sb[:, hc:hc + 1], in_=bb[:, 0:1]),
                  reads=[bb], writes=[bias_sb])
            for b0 in range(0, NB, 512):
                n = min(512, NB - b0)
                bank = banks[(b0 // 512) % 2]
                for j in range(32):
                    rhs = src_sb[:, j + 16 * b0: j + 16 * (b0 + n - 1) + 1: 16]
                    P.mm(bank[:, :n], w1_sb[:, j * 256 + hc * 128: j * 256 + (hc + 1) * 128], rhs,
                         start=(j == 0), stop=(j == 31), reads=[w1_sb, src_sb], writes=[bank])
                P.add("act", lambda e, bank=bank, n=n, hc=hc: e.activation(
                    out=xs[:, :n], in_=bank[:, :n], func=AF.Identity, bias=bias_sb[:, hc:hc + 1]),
                    reads=[bank, bias_sb], writes=[xs])
                P.add("dve", lambda e, n=n: e.tensor_tensor(out=x2[:, :n], in0=xs[:, :n], in1=xs[:, :n],
                                                            op=ALU.mult), reads=[xs], writes=[x2])
                P.add("dve", lambda e, n=n: e.tensor_scalar(out=x2[:, :n], in0=x2[:, :n], scalar1=0.044715,
                                                            scalar2=1.0, op0=ALU.mult, op1=ALU.add),
                      reads=[x2], writes=[x2])
                P.add("dve", lambda e, n=n: e.tensor_tensor(out=x2[:, :n], in0=x2[:, :n], in1=xs[:, :n],
                                                            op=ALU.mult), reads=[x2, xs], writes=[x2])
                P.add("act", lambda e, n=n: e.activation(out=sg[:, :n], in_=x2[:, :n], func=AF.Sigmoid,
                                                         scale=1.5957691216057308), reads=[x2], writes=[sg])
                P.add("dve", lambda e, n=n, b0=b0, hc=hc: e.tensor_tensor(
                    out=hid[hc][:, b0:b0 + n], in0=xs[:, :n], in1=sg[:, :n], op=ALU.mult),
                    reads=[xs, sg], writes=[hid[hc]])
        if which == 0:
            for b0 in range(0, NB, 512):
                n = min(512, NB - b0)
                bank = banks[2 + (b0 // 512) % 2]
                for hc in range(2):
                    P.mm(bank[0:64, :n], w2_sb[:, hc * 64:(hc + 1) * 64], hid[hc][:, b0:b0 + n], start=(hc == 0),
                         stop=(hc == 1), reads=[w2_sb, hid[hc]], writes=[bank])
                P.add("act", lambda e, bank=bank, n=n, b0=b0: e.activation(
                    out=KcC[:, b0:b0 + n], in_=bank[0:64, :n], func=AF.Copy), reads=[bank], writes=[KcC])
        else:
            for nt in range(NCMP // 128):
                n = min(128, NB - nt * 128)
                bank = banks[2 + nt % 2]
                for hc in range(2):
                    P.mm(bank[0:n, 0:64], hid[hc][:, nt * 128: nt * 128 + n], w2_sb[:, hc * 64:(hc + 1) * 64],
                         start=(hc == 0), stop=(hc == 1), reads=[w2_sb, hid[hc]], writes=[bank])
                P.add("act", lambda e, bank=bank, n=n, nt=nt: e.activation(
                    out=Vc1[0:n, nt, 0:64], in_=bank[0:n, 0:64], func=AF.Copy), reads=[bank], writes=[Vc1])

    qsb = [P.sb(f"q{i}", [64, 4, 128], BF16) for i in range(3)]
    PT = [P.sb(f"PT{i}", [128, 512], BF16) for i in range(3)]
    Eh = [P.sb(f"Eh{i}", [128, NCMP], F32) for i in range(2)]
    acc = P.sb("acc", [128, NCMP + 8], F32)
    imp = P.sb("imp", [128, 256], F32)
    work = P.sb("work", [128, 256], F32)
    m8 = P.sb("m8", [128, 16], F32)
    thr = P.sb("thr", [128, 1], F32)
    negm = P.sb("negm", [128, 256], BF16)
    nmx = [P.sb(f"nmx{i}", [128, 2, 64], BF16) for i in range(4)]
    osb = P.sb("osb", [128, 3, 260], F32)
    ocs = P.sb("ocs", [128, 260], F32)
    rinv = P.sb("rinv", [128, 4], F32)
    rden = P.sb("rden", [128, 3, 4], F32)
    coef = P.sb("coef", [128, 4, 3], F32)
    ot = [P.sb(f"ot{i}", [128, 4, 64], F32) for i in range(2)]
    P.add("pool", lambda e: e.memset(acc[:, :], 0.0), writes=[acc])
    bS = [banks[0], banks[1], banks[2]]
    bOc, bOs, bOw = banks[3], banks[4], banks[5]
    bE = [banks[6], banks[7]]
    cnt = {"s": 0, "e": 0, "x": 0}

    def issue_scores(KT_sb, kt, Q2, q_buf, mask_fn):
        bank = bS[cnt["s"] % 3]
        pt = PT[cnt["s"] % 3]
        cnt["s"] += 1
        masks = mask_fn()
        P.mm(bank[:, :], KT_sb[:, kt * 128:(kt + 1) * 128], Q2, start=True, stop=(len(masks) == 0),
             reads=[KT_sb, q_buf], writes=[bank])
        for mi, (m_ap, m_bufs) in enumerate(masks):
            P.mm(bank[:, :], m_ap, i4[:, :], start=False, stop=(mi == len(masks) - 1),
                 reads=list(m_bufs) + [i4], writes=[bank])
        P.add("act", lambda e: e.activation(out=pt[:, :], in_=bank[:, :], func=AF.Exp, scale=SCALE),
              reads=[bank], writes=[pt])
        return pt

    def issue_pv(pt, V_sb, v_idx, Obank, first):
        for h in range(4):
            P.mm(Obank[:, h * 65:(h + 1) * 65], pt[:, h * 128:(h + 1) * 128], V_sb[:, v_idx, :],
                 start=(first and h == 0), stop=True, reads=[pt, V_sb], writes=[Obank], skip_group_check=True)

    LOOK = 2

    def run_branch(tiles, KT_sb, Q2, q_buf, V_sb, Obank):
        pend = []
        for i, (kt, v_idx, mask_fn) in enumerate(tiles):
            pend.append((issue_scores(KT_sb, kt, Q2, q_buf, mask_fn), v_idx, i == 0))
            if len(pend) > LOOK:
                pt, vi, first = pend.pop(0)
                issue_pv(pt, V_sb, vi, Obank, first)
        for pt, vi, first in pend:
            issue_pv(pt, V_sb, vi, Obank, first)

    for qt in range(NT):
        qb = qsb[qt % 3]
        P.dma("sp", qb[:, :, :], QT[:, :, qt * 128:(qt + 1) * 128], writes=[qb], dma_buf=qb)
        Q2 = qb[:, :, :].rearrange("p h t -> p (h t)")
        ncols = min(8 * qt + 7, NB)
        nnt = (ncols + 127) // 128
        tiles = []
        for nt in range(nnt):
            dl = qt - 16 * nt
            masks = [(negc[:, dl * 128:(dl + 1) * 128], [negc])] if dl <= 16 else []
            tiles.append((nt, nt, lambda masks=masks: masks))
        run_branch(tiles, KcC, Q2, qb, Vc1, bOc)
        P.add("act", lambda e: e.activation(out=ocs[:, :], in_=bOc[:, 0:260], func=AF.Copy), reads=[bOc],
              writes=[ocs])
        P.add("pool", lambda e: e.tensor_copy(out=osb[:, 0, :], in_=ocs[:, :]), reads=[ocs], writes=[osb])
        sel = qt >= 8
        if sel:
            ocv = ocs[:, :].rearrange("p (h c) -> p h c", c=65)
            P.add("dve", lambda e, ocv=ocv: e.tensor_scalar(out=rinv[:, :], in0=ocv[:, :, 64], scalar1=1e-30,
                                                            scalar2=None, op0=ALU.add), reads=[ocs], writes=[rinv])
            P.add("dve", lambda e: e.reciprocal(out=rinv[:, :], in_=rinv[:, :]), reads=[rinv], writes=[rinv])
            w0 = 8 * qt - 1
            for h in range(4):
                eh = Eh[cnt["e"] % 2]
                for c0 in range(0, ncols, 512):
                    n = min(512, ncols - c0)
                    bank = bE[cnt["e"] % 2]
                    cnt["e"] += 1
                    lo = max(w0, c0)
                    hi = min(w0 + 8, c0 + n)
                    P.mm(bank[:, :n], qb[:, h, :], KcC[:, c0:c0 + n], start=True, stop=(hi <= lo),
                         reads=[qb, KcC], writes=[bank])
                    if hi > lo:
                        P.mm(bank[:, lo - c0:hi - c0], ident[:, :], cnc[:, lo - w0:hi - w0], start=False, stop=True,
                             reads=[ident, cnc], writes=[bank], skip_group_check=True)
                    P.add("act", lambda e, eh=eh, bank=bank, c0=c0, n=n: e.activation(
                        out=eh[:, c0:c0 + n], in_=bank[:, :n], func=AF.Exp, scale=SCALE), reads=[bank], writes=[eh])
                if h == 0:
                    P.add("dve", lambda e, eh=eh, ncols=ncols: e.tensor_scalar(
                        out=acc[:, :ncols], in0=eh[:, :ncols], scalar1=rinv[:, 0:1], scalar2=None, op0=ALU.mult),
                        reads=[eh, rinv], writes=[acc])
                else:
                    P.add("dve", lambda e, eh=eh, ncols=ncols, h=h: e.scalar_tensor_tensor(
                        out=acc[:, :ncols], in0=eh[:, :ncols], scalar=rinv[:, h:h + 1], in1=acc[:, :ncols],
                        op0=ALU.mult, op1=ALU.add), reads=[eh, rinv, acc], writes=[acc])
            nbk = 2 * qt + 2
            P.add("dve", lambda e, nbk=nbk: e.tensor_reduce(
                out=imp[:, 0:nbk], in_=acc[:, 0:4 * nbk].rearrange("p (s r) -> p s r", r=4), axis=AX.X, op=ALU.add),
                reads=[acc], writes=[imp])
            P.add("dve", lambda e, nbk=nbk: e.tensor_tensor(
                out=imp[:, 1:nbk], in0=imp[:, 1:nbk], in1=acc[:, 3:4 * (nbk - 1):4], op=ALU.add),
                reads=[imp, acc], writes=[imp])
            P.add("dve", lambda e, qt=qt: e.tensor_tensor(
                out=imp[:, 2 * qt - 1:2 * qt + 2], in0=imp[:, 2 * qt - 1:2 * qt + 2], in1=force_sb[:, :], op=ALU.add),
                reads=[imp, force_sb], writes=[imp])
            P.add("dve", lambda e: e.memset(imp[:, 0:1], 3e30), reads=[imp], writes=[imp])
            P.add("dve", lambda e, nbk=nbk: e.max(out=m8[:, 0:8], in_=imp[:, 0:nbk]), reads=[imp], writes=[m8])
            P.add("dve", lambda e, nbk=nbk: e.match_replace(out=work[:, 0:nbk], in_to_replace=m8[:, 0:8],
                                                            in_values=imp[:, 0:nbk], imm_value=-3e38),
                  reads=[imp, m8], writes=[work])
            P.add("dve", lambda e, nbk=nbk: e.max(out=m8[:, 8:16], in_=work[:, 0:nbk]), reads=[work], writes=[m8])
            P.add("dve", lambda e: e.tensor_reduce(out=thr[:, :], in_=m8[:, 8:16], axis=AX.X, op=ALU.min),
                  reads=[m8], writes=[thr])
            P.add("dve", lambda e, nbk=nbk: e.tensor_scalar(out=work[:, 0:nbk], in0=imp[:, 0:nbk], scalar1=thr[:, 0:1],
                                                            scalar2=None, op0=ALU.is_ge),
                  reads=[imp, thr], writes=[work])
            P.add("dve", lambda e, nbk=nbk: e.tensor_scalar(out=negm[:, 0:nbk], in0=work[:, 0:nbk], scalar1=-NEGV,
                                                            scalar2=NEGV, op0=ALU.mult, op1=ALU.add),
                  reads=[work], writes=[negm])
        k0 = max(0, qt - 4)
        tiles = []
        for kt in range(k0, qt + 1):
            masks = []
            if kt == qt:
                masks.append((causal[:, :], [causal]))
            if kt == qt - 4:
                masks.append((winneg[:, :], [winneg]))
            tiles.append((kt, kt, lambda masks=masks: masks))
        run_branch(tiles, KwT_sb, Q2, qb, Vw_sb, bOw)
        P.add("act", lambda e: e.activation(out=osb[:, 2, :], in_=bOw[:, 0:260], func=AF.Copy), reads=[bOw],
              writes=[osb])
        tiles = []
        for kt in range(0, qt + 1):
            def mask_fn(kt=kt, qt=qt, sel=sel):
                masks = []
                if sel:
                    nx = nmx[cnt["x"] % 4]
                    cnt["x"] += 1
                    P.add("pool", lambda e, nx=nx, kt=kt: e.tensor_copy(
                        out=nx[:, :, :], in_=negm[:, 2 * kt:2 * kt + 2].unsqueeze(2).to_broadcast([128, 2, 64])),
                        reads=[negm], writes=[nx])
                    masks.append((nx[:, :, :].rearrange("p b k -> p (b k)"), [nx]))
                if kt == qt:
                    masks.append((causal[:, :], [causal]))
                return masks
            tiles.append((kt, kt, mask_fn))
        run_branch(tiles, KsT_sb, Q2, qb, Vs_sb, bOs)
        P.add("act", lambda e: e.activation(out=osb[:, 1, :], in_=bOs[:, 0:260], func=AF.Copy), reads=[bOs],
              writes=[osb])
        o = ot[qt % 2]
        ov = osb[:, :, :].rearrange("p b (h c) -> p b h c", c=65)
        P.add("dve", lambda e, ov=ov: e.tensor_scalar(out=rden[:, :, :], in0=ov[:, :, :, 64], scalar1=1e-30,
                                                      scalar2=None, op0=ALU.add), reads=[osb], writes=[rden])
        P.add("dve", lambda e: e.reciprocal(out=rden[:, :, :], in_=rden[:, :, :]), reads=[rden], writes=[rden])
        P.add("dve", lambda e, qt=qt: e.tensor_tensor(
            out=coef[:, :, :], in0=gat_sb[:, qt, :].rearrange("p (h b) -> p h b", b=3),
            in1=rden[:, :, :].rearrange("p b h -> p h b"), op=ALU.mult), reads=[gat_sb, rden], writes=[coef])
        for h in range(4):
            for br in range(3):
                if br == 0:
                    P.add("dve", lambda e, o=o, h=h: e.tensor_scalar(
                        out=o[:, h, :], in0=osb[:, 0, h * 65:h * 65 + 64], scalar1=coef[:, h, 0:1], scalar2=None,
                        op0=ALU.mult), reads=[osb, coef], writes=[o])
                else:
                    P.add("dve", lambda e, o=o, h=h, br=br: e.scalar_tensor_tensor(
                        out=o[:, h, :], in0=osb[:, br, h * 65:h * 65 + 64], scalar=coef[:, h, br:br + 1],
                        in1=o[:, h, :], op0=ALU.mult, op1=ALU.add), reads=[osb, coef, o], writes=[o])
        P.dma("pool", out[qt * 128:(qt + 1) * 128, :], o[:, :, :].rearrange("p h d -> p (h d)"), reads=[o],
              dma_buf=o)
    P.finalize()
    return nc


def run_nsa_attn(qkv, gates, w1k, w2k, pek, w1v, w2v, pev, B, S):
    NT = S // 128
    nc = build_nsa_attn(S)
    consts = nsa_consts()

    def w1l(w):
        return np.ascontiguousarray(w.reshape(32, 64, 256).transpose(1, 0, 2).reshape(64, 8192))

    def w2l(w):
        return np.ascontiguousarray(w.reshape(2, 128, 64).transpose(1, 0, 2).reshape(128, 128))

    shared = dict(consts)
    shared.update({"w1k": w1l(w1k), "w1v": w1l(w1v), "w2k": w2l(w2k), "w2v": w2l(w2v),
                   "pekT": np.ascontiguousarray(pek.T), "pevT": np.ascontiguousarray(pev.T)})
    in_maps = []
    for c in range(NCORES):
        b, g = c // 4, c % 4
        blk = qkv[b * S:(b + 1) * S]
        q = blk[:, 0:1024].reshape(S, 16, 64)[:, 4 * g:4 * g + 4, :]
        kv = [blk[:, 1024 + i * 256 + g * 64: 1024 + i * 256 + (g + 1) * 64] for i in range(6)]

        def v1(v):
            o = np.ones((128, NT, 65), dtype=qkv.dtype)
            o[:, :, :64] = v.reshape(NT, 128, 64).transpose(1, 0, 2)
            return o

        gt = gates[b * S:(b + 1) * S].reshape(S, 4, 12)[:, g, :]
        m = {"QT": np.ascontiguousarray(q.transpose(2, 1, 0)),
             "KcT": np.ascontiguousarray(kv[0].T), "VcT": np.ascontiguousarray(kv[1].T),
             "KsT": np.ascontiguousarray(kv[2].T), "Vs1": v1(kv[3]),
             "KwT": np.ascontiguousarray(kv[4].T), "Vw1": v1(kv[5]),
             "gat": np.ascontiguousarray(gt.reshape(NT, 128, 12).transpose(1, 0, 2))}
        m.update(shared)
        in_maps.append(m)
    res = run_bass_kernel_spmd(nc, in_maps, core_ids=list(range(NCORES)))
    o = np.zeros((B * S, 1024), np.float32)
    for c in range(NCORES):
        b, g = c // 4, c % 4
        o[b * S:(b + 1) * S, g * 256:(g + 1) * 256] = res.results[c]["o"]
    return o


CH = 64
SDT = F32
E05 = float(np.exp(-0.5))
RW_OUT_BF = ["aT", "rT", "bT", "kT", "BhT", "KhT", "vbT"]
RW_OUT_F = ["vT", "gT", "bonT"]


def build_rwkv_pre(T, has_vres, TT=128):
    nc = bass.Bass("TRN2", target_bir_lowering=False)
    din = lambda name, shape, dt=F32: nc.dram_tensor(name, shape, dt, kind="ExternalInput").ap()
    xT = din("xT", [D, T + 1])
    g = din("g", [128, 8])
    prm = din("prm", [128, 12 * 8])
    wrkv = [din(f"w{n}", [D, D]) for n in "rkv"]
    w1 = din("w1", [D, 64]); w2 = din("w2", [64, D])
    a1 = din("a1", [D, 64]); a2 = din("a2", [64, D])
    g1 = din("g1", [D, 160]); g2 = din("g2", [160, D])
    if has_vres:
        v1 = din("v1", [D, 32]); v2 = din("v2", [32, D])
        vfT = din("vfT", [D, T])
    bones = din("bones", [128, 128])
    rmask = din("rmask", [128, TT])
    outs = {n: nc.dram_tensor(n, [D, T], SDT, kind="ExternalOutput").ap() for n in RW_OUT_BF}
    outs.update({n: nc.dram_tensor(n, [D, T], F32, kind="ExternalOutput").ap() for n in RW_OUT_F})
    gC = nc.dram_tensor("gC", [D, T // CH], F32, kind="ExternalOutput").ap()
    C = Ctx(nc)
    P = C.P
    banks = C.banks
    v3 = lambda ap: ap.rearrange("(c p) t -> p c t", p=128)
    xv = v3(xT)
    ov = {n: v3(a) for n, a in outs.items()}
    gCv = v3(gC)

    def small(name, src, shape):
        b = P.sb(name, shape, F32)
        P.dma("sp", b[tuple(slice(None) for _ in shape)], src, writes=[b], dma_buf=b)
        return b

    g_sb = small("g", g, [128, 8])
    prm_sb = small("prm", prm, [128, 96])
    bones_sb = small("bones", bones, [128, 128])
    rmask_sb = small("rmask", rmask, [128, TT])
    pr = lambda i, c: prm_sb[:, i * 8 + c: i * 8 + c + 1]
    eps_t = P.sb("eps", [128, 1], F32)
    P.add("pool", lambda e: e.memset(eps_t[:, :], 1e-5), writes=[eps_t])
    Wr, Wk, Wv = [C.load_weight(f"W{n}_", w, D, D) for n, w in zip("rkv", wrkv)]
    W1 = C.load_weight("w1_", w1, D, 64)
    A1 = C.load_weight("a1_", a1, D, 64)
    G1 = C.load_weight("g1_", g1, D, 160)

    def load_rows(name, src, r0, nr):
        b = P.sb(name, [nr, D], BF16)
        C.load_cast(b, b[:, :], src[r0:r0 + nr, :], D, parts=nr)
        return b

    W2 = load_rows("w2_", w2, 0, 64)
    A2 = load_rows("a2_", a2, 0, 64)
    G2a = load_rows("g2a_", g2, 0, 128)
    G2b = load_rows("g2b_", g2, 128, 32)
    if has_vres:
        V1 = C.load_weight("v1_", v1, D, 32)
        V2 = load_rows("v2_", v2, 0, 32)
        vfv = v3(vfT)

    TH = TT + 1
    xb = [P.sb(f"x{i}", [128, 8, TH], F32) for i in range(2)]
    sq = P.sb("sq", [128, 8, TH], F32)
    rstd = P.sb("rstd", [128, TH], F32)
    hn = P.sb("hn", [128, 8, TH], F32)
    dx = P.sb("dx", [128, 8, TT], F32)
    xm = [P.sb(f"xm{i}", [128, 8, TT], BF16) for i in range(6)]
    lw = P.sb("lw", [64, TT], BF16)
    la = P.sb("la", [64, TT], BF16)
    lg = [P.sb("lga", [128, TT], BF16), P.sb("lgb", [32, TT], BF16)]
    lv = P.sb("lv", [32, TT], BF16)
    F = lambda name: P.sb(name, [128, TT], F32)
    r_t, k_t, v_t, a_t, dl_t, cum_t, kk_t, kh_t = [F(n) for n in ("r", "k", "v", "a", "dl", "cum", "kk", "kh")]
    t1, t2, t3, t4 = [F(n) for n in ("t1", "t2", "t3", "t4")]
    vf_t = F("vf")
    NO = len(RW_OUT_BF)
    obf = {n: [P.sb(f"o_{n}{i}", [128, 8, TT], SDT) for i in range(1)] for n in RW_OUT_BF}
    of32 = {n: [P.sb(f"o_{n}{i}", [128, 8, TT], F32) for i in range(1)] for n in RW_OUT_F}
    ogc = P.sb("ogc", [128, 8, TT // CH], F32)
    nt = T // TT
    nbk = [0]

    def proj(Wl, xin, fc, K=8):
        bank = banks[nbk[0] % 6]
        nbk[0] += 1
        for kc in range(K):
            P.mm(bank[:, :TT], Wl[kc][:, fc * 128:(fc + 1) * 128], xin[:, kc, :], start=(kc == 0), stop=(kc == K - 1),
                 reads=[Wl[kc], xin], writes=[bank])
        return bank

    def lora_down(Wl, xin, n):
        bank = banks[nbk[0] % 6]
        nbk[0] += 1
        for kc in range(8):
            P.mm(bank[0:n, :TT], Wl[kc][:, 0:n], xin[:, kc, :], start=(kc == 0), stop=(kc == 7),
                 reads=[Wl[kc], xin], writes=[bank])
        return bank

    def lora_up(parts, fc):
        bank = banks[nbk[0] % 6]
        nbk[0] += 1
        for i, (Wb, hb, n) in enumerate(parts):
            P.mm(bank[:, :TT], Wb[0:n, fc * 128:(fc + 1) * 128], hb[0:n, :], start=(i == 0),
                 stop=(i == len(parts) - 1), reads=[Wb, hb], writes=[bank])
        return bank

    for tt in range(nt):
        xt = xb[tt % 2]
        P.dma("sp", xt[:, :, :], xv[:, :, tt * TT: tt * TT + TH], writes=[xt], dma_buf=xt)
        P.add("act", lambda e, xt=xt: e.activation(out=sq[:, :, :], in_=xt[:, :, :], func=AF.Square),
              reads=[xt], writes=[sq])
        bk = banks[7]
        for c in range(8):
            P.mm(bk[:, :TH], C.ones32[:, :], sq[:, c, :], start=(c == 0), stop=(c == 7), reads=[C.ones32, sq],
                 writes=[bk])
        P.add("act", lambda e: e.activation(out=rstd[:, :], in_=bk[:, :TH], func=AF.Sqrt, bias=eps_t[:, 0:1],
                                            scale=1.0 / D), reads=[bk, eps_t], writes=[rstd])
        P.add("dve", lambda e: e.reciprocal(out=rstd[:, :], in_=rstd[:, :]), reads=[rstd], writes=[rstd])
        for c in range(8):
            P.add("dve", lambda e, c=c, xt=xt: e.scalar_tensor_tensor(
                out=hn[:, c, :], in0=xt[:, c, :], scalar=g_sb[:, c:c + 1], in1=rstd[:, :], op0=ALU.mult,
                op1=ALU.mult), reads=[xt, g_sb, rstd], writes=[hn])
        P.add("pool", lambda e: e.tensor_tensor(out=dx[:, :, :], in0=hn[:, :, 0:TT], in1=hn[:, :, 1:TH],
                                                op=ALU.subtract), reads=[hn], writes=[dx])
        for i in range(6):
            for c in range(8):
                P.add("dve", lambda e, i=i, c=c: e.scalar_tensor_tensor(
                    out=xm[i][:, c, :], in0=dx[:, c, :], scalar=pr(i, c), in1=hn[:, c, 1:TH], op0=ALU.mult,
                    op1=ALU.add), reads=[dx, hn, prm_sb], writes=[xm[i]])
        xr, xw, xk, xvv, xa, xg = xm
        bw = lora_down(W1, xw, 64)
        P.add("act", lambda e, bw=bw: e.activation(out=lw[:, :], in_=bw[0:64, :TT], func=AF.Tanh), reads=[bw],
              writes=[lw])
        ba = lora_down(A1, xa, 64)
        P.add("act", lambda e, ba=ba: e.activation(out=la[:, :], in_=ba[0:64, :TT], func=AF.Copy), reads=[ba],
              writes=[la])
        bg = lora_down(G1, xg, 128)
        P.add("act", lambda e, bg=bg: e.activation(out=lg[0][:, :], in_=bg[:, :TT], func=AF.Sigmoid), reads=[bg],
              writes=[lg[0]])
        bg2 = banks[nbk[0] % 6]
        nbk[0] += 1
        for kc in range(8):
            P.mm(bg2[0:32, :TT], G1[kc][:, 128:160], xg[:, kc, :], start=(kc == 0), stop=(kc == 7),
                 reads=[G1[kc], xg], writes=[bg2])
        P.add("act", lambda e, bg2=bg2: e.activation(out=lg[1][:, :], in_=bg2[0:32, :TT], func=AF.Sigmoid),
              reads=[bg2], writes=[lg[1]])
        if has_vres:
            bv = lora_down(V1, xvv, 32)
            P.add("act", lambda e, bv=bv: e.activation(out=lv[:, :], in_=bv[0:32, :TT], func=AF.Copy), reads=[bv],
                  writes=[lv])
        for fc in range(8):
            sl = slice(tt * TT, (tt + 1) * TT)
            b = proj(Wr, xr, fc)
            P.add("act", lambda e, b=b: e.activation(out=r_t[:, :], in_=b[:, :TT], func=AF.Copy), reads=[b],
                  writes=[r_t])
            b = proj(Wk, xk, fc)
            P.add("act", lambda e, b=b: e.activation(out=k_t[:, :], in_=b[:, :TT], func=AF.Copy), reads=[b],
                  writes=[k_t])
            b = proj(Wv, xvv, fc)
            P.add("act", lambda e, b=b: e.activation(out=v_t[:, :], in_=b[:, :TT], func=AF.Copy), reads=[b],
                  writes=[v_t])
            b = lora_up([(W2, lw, 64)], fc)
            P.add("act", lambda e, b=b, fc=fc: e.activation(out=dl_t[:, :], in_=b[:, :TT], func=AF.Sigmoid,
                                                            bias=pr(6, fc)), reads=[b, prm_sb], writes=[dl_t])
            P.add("pool", lambda e: e.tensor_scalar(out=dl_t[:, :], in0=dl_t[:, :], scalar1=-E05, scalar2=None,
                                                    op0=ALU.mult), reads=[dl_t], writes=[dl_t])
            b = lora_up([(A2, la, 64)], fc)
            P.add("act", lambda e, b=b, fc=fc: e.activation(out=a_t[:, :], in_=b[:, :TT], func=AF.Sigmoid,
                                                            bias=pr(7, fc)), reads=[b, prm_sb], writes=[a_t])
            b = lora_up([(G2a, lg[0], 128), (G2b, lg[1], 32)], fc)
            og = of32["gT"][0]
            P.add("act", lambda e, b=b, fc=fc, og=og: e.activation(out=og[:, fc, :], in_=b[:, :TT], func=AF.Copy),
                  reads=[b], writes=[og])
            if has_vres:
                b = lora_up([(V2, lv, 32)], fc)
                P.add("act", lambda e, b=b, fc=fc: e.activation(out=t1[:, :], in_=b[:, :TT], func=AF.Sigmoid,
                                                                bias=pr(11, fc)), reads=[b, prm_sb], writes=[t1])
                P.dma("sp", vf_t[:, :], vfv[:, fc, sl], writes=[vf_t], dma_buf=vf_t)
                P.add("pool", lambda e: e.tensor_tensor(out=vf_t[:, :], in0=vf_t[:, :], in1=v_t[:, :],
                                                        op=ALU.subtract), reads=[vf_t, v_t], writes=[vf_t])
                P.add("pool", lambda e: e.tensor_tensor(out=vf_t[:, :], in0=vf_t[:, :], in1=t1[:, :], op=ALU.mult),
                      reads=[vf_t, t1], writes=[vf_t])
                P.add("pool", lambda e: e.tensor_tensor(out=v_t[:, :], in0=v_t[:, :], in1=vf_t[:, :], op=ALU.add),
                      reads=[vf_t, v_t], writes=[v_t])
            ovf = of32["vT"][0]
            ovb = obf["vbT"][0]
            P.add("pool", lambda e, fc=fc, ovf=ovf: e.tensor_copy(out=ovf[:, fc, :], in_=v_t[:, :]), reads=[v_t],
                  writes=[ovf])
            P.add("pool", lambda e, fc=fc, ovb=ovb: e.tensor_copy(out=ovb[:, fc, :], in_=v_t[:, :]), reads=[v_t],
                  writes=[ovb])
            P.add("dve", lambda e, fc=fc: e.tensor_scalar(out=kk_t[:, :], in0=k_t[:, :], scalar1=pr(8, fc),
                                                          scalar2=None, op0=ALU.mult), reads=[k_t, prm_sb],
                  writes=[kk_t])
            P.add("dve", lambda e: e.tensor_tensor(out=t2[:, :], in0=kk_t[:, :], in1=kk_t[:, :], op=ALU.mult),
                  reads=[kk_t], writes=[t2])
            bn = banks[6]
            P.mm(bn[:, :TT], bones_sb[:, :], t2[:, :], start=True, stop=True, reads=[bones_sb, t2], writes=[bn])
            P.add("act", lambda e, bn=bn: e.activation(out=t2[:, :], in_=bn[:, :TT], func=AF.Sqrt), reads=[bn],
                  writes=[t2])
            P.add("dve", lambda e: e.tensor_scalar(out=t2[:, :], in0=t2[:, :], scalar1=1e-12, scalar2=None,
                                                   op0=ALU.max), reads=[t2], writes=[t2])
            P.add("dve", lambda e: e.reciprocal(out=t2[:, :], in_=t2[:, :]), reads=[t2], writes=[t2])
            P.add("dve", lambda e: e.tensor_tensor(out=kk_t[:, :], in0=kk_t[:, :], in1=t2[:, :], op=ALU.mult),
                  reads=[kk_t, t2], writes=[kk_t])
            P.add("dve", lambda e, fc=fc: e.tensor_scalar(out=kh_t[:, :], in0=a_t[:, :], scalar1=-1.0,
                                                          scalar2=pr(9, fc), op0=ALU.add, op1=ALU.mult),
                  reads=[a_t, prm_sb], writes=[kh_t])
            P.add("dve", lambda e: e.scalar_tensor_tensor(out=kh_t[:, :], in0=kh_t[:, :], scalar=1.0, in1=k_t[:, :],
                                                          op0=ALU.add, op1=ALU.mult), reads=[kh_t, k_t],
                  writes=[kh_t])
            P.add("dve", lambda e, fc=fc: e.scalar_tensor_tensor(out=t3[:, :], in0=r_t[:, :], scalar=pr(10, fc),
                                                                 in1=kh_t[:, :], op0=ALU.mult, op1=ALU.mult),
                  reads=[r_t, kh_t, prm_sb], writes=[t3])
            bn2 = banks[7]
            P.mm(bn2[:, :TT], bones_sb[:, :], t3[:, :], start=True, stop=True, reads=[bones_sb, t3], writes=[bn2])
            ob = of32["bonT"][0]
            P.add("dve", lambda e, fc=fc, ob=ob, bn2=bn2: e.tensor_tensor(out=ob[:, fc, :], in0=bn2[:, :TT],
                                                                         in1=v_t[:, :], op=ALU.mult),
                  reads=[bn2, v_t], writes=[ob])
            P.add("dve", lambda e: e.tensor_tensor_scan(out=cum_t[:, :], data0=rmask_sb[:, :], data1=dl_t[:, :],
                                                        initial=0.0, op0=ALU.mult, op1=ALU.add),
                  reads=[rmask_sb, dl_t], writes=[cum_t])
            P.add("act", lambda e: e.activation(out=t1[:, :], in_=cum_t[:, :], func=AF.Exp, scale=-1.0),
                  reads=[cum_t], writes=[t1])
            P.add("act", lambda e: e.activation(out=t2[:, :], in_=cum_t[:, :], func=AF.Exp), reads=[cum_t],
                  writes=[t2])
            P.add("pool", lambda e: e.tensor_tensor(out=t4[:, :], in0=cum_t[:, :], in1=dl_t[:, :], op=ALU.subtract),
                  reads=[cum_t, dl_t], writes=[t4])
            P.add("act", lambda e: e.activation(out=t4[:, :], in_=t4[:, :], func=AF.Exp), reads=[t4], writes=[t4])
            o = obf["aT"][0]
            P.add("dve", lambda e, fc=fc, o=o: e.scalar_tensor_tensor(out=o[:, fc, :], in0=kk_t[:, :], scalar=-1.0,
                                                                     in1=t4[:, :], op0=ALU.mult, op1=ALU.mult),
                  reads=[kk_t, t4], writes=[o])
            o = obf["rT"][0]
            P.add("pool", lambda e, fc=fc, o=o: e.tensor_tensor(out=o[:, fc, :], in0=r_t[:, :], in1=t2[:, :],
                                                               op=ALU.mult), reads=[r_t, t2], writes=[o])
            P.add("dve", lambda e: e.tensor_tensor(out=t3[:, :], in0=kk_t[:, :], in1=a_t[:, :], op=ALU.mult),
                  reads=[kk_t, a_t], writes=[t3])
            P.add("dve", lambda e: e.tensor_tensor(out=t3[:, :], in0=t3[:, :], in1=t1[:, :], op=ALU.mult),
                  reads=[t3, t1], writes=[t3])
            P.add("pool", lambda e: e.tensor_tensor(out=kh_t[:, :], in0=kh_t[:, :], in1=t1[:, :], op=ALU.mult),
                  reads=[kh_t, t1], writes=[kh_t])
            o = obf["bT"][0]
            P.add("pool", lambda e, fc=fc, o=o: e.tensor_copy(out=o[:, fc, :], in_=t3[:, :]), reads=[t3], writes=[o])
            o = obf["kT"][0]
            P.add("pool", lambda e, fc=fc, o=o: e.tensor_copy(out=o[:, fc, :], in_=kh_t[:, :]), reads=[kh_t],
                  writes=[o])
            gcv = t2[:, :].rearrange("p (n c) -> p n c", c=CH)[:, :, CH - 1:CH]
            P.add("pool", lambda e, fc=fc, gcv=gcv: e.tensor_copy(out=ogc[:, fc, :].unsqueeze(2), in_=gcv),
                  reads=[t2], writes=[ogc])
            gcb = gcv.to_broadcast([128, TT // CH, CH])
            o = obf["BhT"][0]
            P.add("dve", lambda e, fc=fc, o=o, gcb=gcb: e.tensor_tensor(
                out=o[:, fc, :].rearrange("p (n c) -> p n c", c=CH), in0=t3[:, :].rearrange("p (n c) -> p n c", c=CH),
                in1=gcb, op=ALU.mult), reads=[t3, t2], writes=[o])
            o = obf["KhT"][0]
            P.add("dve", lambda e, fc=fc, o=o, gcb=gcb: e.tensor_tensor(
                out=o[:, fc, :].rearrange("p (n c) -> p n c", c=CH), in0=kh_t[:, :].rearrange("p (n c) -> p n c", c=CH),
                in1=gcb, op=ALU.mult), reads=[kh_t, t2], writes=[o])
        sl = slice(tt * TT, (tt + 1) * TT)
        for n in RW_OUT_BF:
            P.dma("pool", ov[n][:, :, sl], obf[n][0][:, :, :], reads=[obf[n][0]], dma_buf=obf[n][0])
        for n in RW_OUT_F:
            P.dma("pool", ov[n][:, :, sl], of32[n][0][:, :, :], reads=[of32[n][0]], dma_buf=of32[n][0])
        P.dma("pool", gCv[:, :, tt * (TT // CH):(tt + 1) * (TT // CH)], ogc[:, :, :], reads=[ogc], dma_buf=ogc)
    P.finalize()
    return nc


GN_EPS = 64e-5


def scan_consts():
    s = np.arange(128)[:, None]
    t = np.arange(128)[None, :]
    same = (s // CH) == (t // CH)
    return {"mstrict": (same & (s < t)).astype(np.float32), "mincl": (same & (s <= t)).astype(np.float32),
            "mstrictT": (same & (t < s)).astype(np.float32), "identf": np.eye(128, dtype=np.float32)}


def build_rwkv_scan(S):
    NW = S // 128
    NCH = 128 // CH
    L = int(np.log2(CH))
    nc = bass.Bass("TRN2", target_bir_lowering=False)
    din = lambda name, shape, dt=F32: nc.dram_tensor(name, shape, dt, kind="ExternalInput").ap()
    fm = din("fm", [4, 64, NW, 512], SDT)
    tk = din("tk", [4, 128, NW, 256], SDT)
    gC = din("gC", [4, 64, S // CH])
    lnw = din("lnw", [4, 64, 64])
    lnb = din("lnb", [4, 64, 64])
    c_ms = din("mstrict", [128, 128]); c_mi = din("mincl", [128, 128]); c_mt = din("mstrictT", [128, 128])
    c_id = din("identf", [128, 128])
    yout = nc.dram_tensor("yn", [4, S, 64], F32, kind="ExternalOutput").ap()
    C = Ctx(nc)
    P = C.P
    banks = C.banks

    def small(name, src, shape):
        b = P.sb(name, shape, F32)
        P.dma("sp", b[tuple(slice(None) for _ in shape)], src, writes=[b], dma_buf=b)
        return b

    ms = small("ms", c_ms, [128, 128]); mi = small("mi", c_mi, [128, 128]); mt = small("mt", c_mt, [128, 128])
    idf = small("idf", c_id, [128, 128])
    gC_sb = [small(f"gC{h}", gC[h], [64, S // CH]) for h in range(4)]
    lnw_sb = [small(f"lnw{h}", lnw[h], [64, 64]) for h in range(4)]
    lnb_sb = [small(f"lnb{h}", lnb[h], [64, 64]) for h in range(4)]
    NBUF = 3
    fmb = [[P.sb(f"fm{h}_{i}", [64, 512], SDT) for i in range(NBUF)] for h in range(4)]
    tkb = [[P.sb(f"tk{h}_{i}", [128, 256], SDT) for i in range(NBUF)] for h in range(4)]

    def per_head(name, shape, dt, n=2):
        return [[P.sb(f"{name}{h}_{i}", shape, dt) for i in range(n)] for h in range(4)]

    Abr = per_head("Abr", [128, 128], SDT)
    Aak = per_head("Aak", [128, 128], SDT)
    Akr = per_head("Akr", [128, 128], SDT)
    Xn = per_head("Xn", [128, 128], SDT)
    Xt = per_head("Xt", [128, 128], SDT)
    Pf = per_head("Pf", [128, 128], F32, 1)
    Pb = per_head("Pb", [128, 128], SDT)
    axb = per_head("axb", [128, 128], SDT)
    wv = per_head("wv", [128, 128], SDT)
    qeff = per_head("qeff", [64, 128], F32)
    Tc = per_head("Tc", [64, 64], F32)
    ST = per_head("ST", [64, 64], F32)
    yc = per_head("yc", [64, 64], F32)
    ysq = per_head("ysq", [64, 64], F32, 1)
    st = per_head("st", [64, 4], F32)
    yo = per_head("yo", [64, 64], F32)
    for h in range(4):
        P.add("pool", lambda e, h=h: e.memset(ST[h][0][:, :], 0.0), writes=[ST[h][0]])
    nb = [0]

    def bank():
        b = banks[nb[0] % 8]
        nb[0] += 1
        return b

    eng_rr = [0]

    def ev():
        e = ["dve", "pool"][eng_rr[0] % 2]
        eng_rr[0] += 1
        return e

    nstate = [0, 0, 0, 0]
    def load_win(w):
        for h in range(4):
            f = fmb[h][w % NBUF]
            t = tkb[h][w % NBUF]
            P.dma("sp", f[:, :], fm[h, :, w, :], writes=[f], dma_buf=f)
            P.dma("sp", t[:, :], tk[h, :, w, :], writes=[t], dma_buf=t)

    load_win(0)
    for w in range(NW):
        i2 = w % 2
        if w + 1 < NW:
            load_win(w + 1)
        def head_gen(h, w=w, i2=i2):
            f = fmb[h][w % NBUF]
            t = tkb[h][w % NBUF]
            aT, rT, bT, kT = f[:, 0:128], f[:, 128:256], f[:, 256:384], f[:, 384:512]
            a_tok, Bh, Kh, v_tok = t[:, 0:64], t[:, 64:128], t[:, 128:192], t[:, 192:256]
            abr, aak, akr = Abr[h][i2], Aak[h][i2], Akr[h][i2]
            b1 = bank()
            P.mm(b1[:, 0:256], bT, f[:, 0:256], start=True, stop=True, reads=[f], writes=[b1])
            xn, xt = Xn[h][0], Xt[h][0]
            P.add("dve", lambda e, b1=b1, xn=xn: e.tensor_tensor(out=xn[:, :], in0=b1[:, 0:128], in1=ms[:, :],
                                                                op=ALU.mult), reads=[b1, ms], writes=[xn])
            P.add("dve", lambda e, b1=b1, abr=abr: e.tensor_tensor(out=abr[:, :], in0=b1[:, 128:256], in1=mi[:, :],
                                                                  op=ALU.mult), reads=[b1, mi], writes=[abr])
            pf, pb = Pf[h][0], Pb[h][0]
            P.add("dve", lambda e, b1=b1, pf=pf: e.tensor_tensor(out=pf[:, :], in0=b1[:, 0:128], in1=ms[:, :],
                                                                op=ALU.mult), reads=[b1, ms], writes=[pf])
            P.add("pool", lambda e, pf=pf: e.tensor_tensor(out=pf[:, :], in0=pf[:, :], in1=idf[:, :], op=ALU.add),
                  reads=[pf, idf], writes=[pf])
            P.add("pool", lambda e, pf=pf, pb=pb: e.tensor_copy(out=pb[:, :], in_=pf[:, :]), reads=[pf], writes=[pb])
            yield
            b2 = bank()
            P.mm(b2[:, 0:256], kT, f[:, 0:256], start=True, stop=True, reads=[f], writes=[b2])
            P.add("dve", lambda e, b2=b2, aak=aak: e.tensor_tensor(out=aak[:, :], in0=b2[:, 0:128], in1=ms[:, :],
                                                                  op=ALU.mult), reads=[b2, ms], writes=[aak])
            P.add("dve", lambda e, b2=b2, akr=akr: e.tensor_tensor(out=akr[:, :], in0=b2[:, 128:256], in1=mi[:, :],
                                                                  op=ALU.mult), reads=[b2, mi], writes=[akr])
            yield
            b3 = bank()
            P.mm(b3[:, 0:128], aT, bT, start=True, stop=True, reads=[f], writes=[b3])
            P.add("dve", lambda e, b3=b3, xt=xt: e.tensor_tensor(out=xt[:, :], in0=b3[:, 0:128], in1=mt[:, :],
                                                                op=ALU.mult), reads=[b3, mt], writes=[xt])
            yield
            cur_n, cur_t = xn, xt
            for k in range(1, L):
                nxt_n, nxt_t = Xn[h][k % 2], Xt[h][k % 2]
                bt_ = bank()
                P.mm(bt_[:, 0:128], cur_n[:, :], cur_t[:, :], start=True, stop=True, reads=[cur_n, cur_t],
                     writes=[bt_])
                if k < L - 1:
                    bn_ = bank()
                    P.mm(bn_[:, 0:128], cur_t[:, :], cur_n[:, :], start=True, stop=True, reads=[cur_n, cur_t],
                         writes=[bn_])
                P.add("act", lambda e, bt_=bt_, nxt_t=nxt_t: e.activation(out=nxt_t[:, :], in_=bt_[:, 0:128],
                                                                         func=AF.Copy), reads=[bt_], writes=[nxt_t])
                if k < L - 1:
                    P.add("act", lambda e, bn_=bn_, nxt_n=nxt_n: e.activation(out=nxt_n[:, :], in_=bn_[:, 0:128],
                                                                             func=AF.Copy), reads=[bn_],
                          writes=[nxt_n])
                yield
                bp = bank()
                P.mm(bp[:, 0:128], nxt_t[:, :], pb[:, :], start=True, stop=True, reads=[nxt_t, pb], writes=[bp])
                P.add("dve", lambda e, bp=bp, pf=pf: e.tensor_tensor(out=pf[:, :], in0=pf[:, :], in1=bp[:, 0:128],
                                                                    op=ALU.add), reads=[pf, bp], writes=[pf])
                pb = Pb[h][k % 2]
                P.add("pool", lambda e, pf=pf, pb=pb: e.tensor_copy(out=pb[:, :], in_=pf[:, :]), reads=[pf],
                      writes=[pb])
                cur_n, cur_t = nxt_n, nxt_t
                yield
            tinv = pb
            ax = axb[h][i2]
            bx = bank()
            P.mm(bx[:, 0:64], aak[:, :], v_tok, start=True, stop=True, reads=[aak, t], writes=[bx])
            P.add("pool", lambda e, ax=ax, a_tok=a_tok: e.tensor_copy(out=ax[:, 0:64], in_=a_tok), reads=[t],
                  writes=[ax])
            P.add("act", lambda e, ax=ax, bx=bx: e.activation(out=ax[:, 64:128], in_=bx[:, 0:64], func=AF.Copy),
                  reads=[bx, ax], writes=[ax])
            yield
            wvb = wv[h][i2]
            bw = bank()
            P.mm(bw[:, 0:128], tinv[:, :], ax[:, :], start=True, stop=True, reads=[tinv, ax], writes=[bw])
            P.add("act", lambda e, wvb=wvb, bw=bw: e.activation(out=wvb[:, :], in_=bw[:, 0:128], func=AF.Copy),
                  reads=[bw], writes=[wvb])
            yield
            qe = qeff[h][i2]
            bq = bank()
            P.mm(bq[0:64, 0:128], wvb[:, 0:64], abr[:, :], start=True, stop=True, reads=[wvb, abr], writes=[bq])
            P.add("dve", lambda e, qe=qe, bq=bq, rT=rT: e.tensor_tensor(out=qe[:, :], in0=bq[0:64, 0:128], in1=rT,
                                                                       op=ALU.add), reads=[bq, f], writes=[qe])
            yield
            for c in range(NCH):
                ps = slice(c * CH, (c + 1) * CH)
                ci = nstate[h]
                nstate[h] += 1
                s_old, s_new = ST[h][ci % 2], ST[h][(ci + 1) % 2]
                tc = Tc[h][ci % 2]
                btc = bank()
                P.mm(btc[0:64, 0:64], wvb[ps, 0:64], Bh[ps, :], start=True, stop=True, reads=[wvb, t], writes=[btc])
                gidx = w * NCH + c
                P.add("dve", lambda e, tc=tc, btc=btc, h=h, gidx=gidx: e.scalar_tensor_tensor(
                    out=tc[:, :], in0=idf[0:64, 0:64], scalar=gC_sb[h][:, gidx:gidx + 1], in1=btc[0:64, 0:64],
                    op0=ALU.mult, op1=ALU.add), reads=[idf, gC_sb[h], btc], writes=[tc])
                yield
                by = bank()
                P.mm(by[0:64, 0:64], abr[ps, ps], wvb[ps, 64:128], start=True, stop=False, reads=[abr, wvb],
                     writes=[by])
                P.mm(by[0:64, 0:64], akr[ps, ps], v_tok[ps, :], start=False, stop=False, reads=[akr, t], writes=[by])
                bys = bank()
                P.mm(bys[0:64, 0:64], qe[:, ps], s_old[:, :], start=True, stop=True, reads=[qe, s_old], writes=[bys])
                y = yc[h][ci % 2]
                s4 = st[h][ci % 2]
                P.add("act", lambda e, y=y, by=by: e.activation(out=y[:, :], in_=by[0:64, 0:64], func=AF.Copy),
                      reads=[by], writes=[y])
                P.add("dve", lambda e, y=y, bys=bys: e.tensor_tensor(out=y[:, :], in0=y[:, :], in1=bys[0:64, 0:64],
                                                                    op=ALU.add), reads=[y, bys], writes=[y])
                yield
                bs = bank()
                P.mm(bs[0:64, 0:64], Bh[ps, :], wvb[ps, 64:128], start=True, stop=False, reads=[t, wvb], writes=[bs])
                P.mm(bs[0:64, 0:64], Kh[ps, :], v_tok[ps, :], start=False, stop=True, reads=[t], writes=[bs])
                bs2 = bank()
                P.mm(bs2[0:64, 0:64], tc[:, :], s_old[:, :], start=True, stop=True, reads=[tc, s_old], writes=[bs2])
                P.add("act", lambda e, s_new=s_new, bs=bs: e.activation(out=s_new[:, :], in_=bs[0:64, 0:64],
                                                                       func=AF.Copy), reads=[bs], writes=[s_new])
                P.add("dve", lambda e, s_new=s_new, bs2=bs2: e.tensor_tensor(
                    out=s_new[:, :], in0=s_new[:, :], in1=bs2[0:64, 0:64], op=ALU.add), reads=[s_new, bs2],
                    writes=[s_new])
                yield
                P.add("dve", lambda e, y=y, s4=s4: e.tensor_reduce(out=s4[:, 0:1], in_=y[:, :], axis=AX.X,
                                                                   op=ALU.add), reads=[y], writes=[s4])
                P.add("dve", lambda e, s4=s4: e.tensor_scalar(out=s4[:, 0:1], in0=s4[:, 0:1], scalar1=-1.0 / 64,
                                                              scalar2=None, op0=ALU.mult), reads=[s4], writes=[s4])
                P.add("act", lambda e, y=y, s4=s4: e.activation(out=y[:, :], in_=y[:, :], func=AF.Identity,
                                                                bias=s4[:, 0:1]), reads=[y, s4], writes=[y])
                yield
                sqb = ysq[h][0]
                P.add("pool", lambda e, y=y, sqb=sqb: e.tensor_tensor(out=sqb[:, :], in0=y[:, :], in1=y[:, :],
                                                                     op=ALU.mult), reads=[y], writes=[sqb])
                P.add("dve", lambda e, sqb=sqb, s4=s4: e.tensor_reduce(out=s4[:, 1:2], in_=sqb[:, :], axis=AX.X,
                                                                       op=ALU.add), reads=[sqb], writes=[s4])
                P.add("dve", lambda e, s4=s4: e.tensor_scalar(out=s4[:, 1:2], in0=s4[:, 1:2], scalar1=1.0 / 64,
                                                              scalar2=GN_EPS, op0=ALU.mult, op1=ALU.add),
                      reads=[s4], writes=[s4])
                P.add("act", lambda e, s4=s4: e.activation(out=s4[:, 2:3], in_=s4[:, 1:2], func=AF.Sqrt),
                      reads=[s4], writes=[s4])
                P.add("dve", lambda e, s4=s4: e.reciprocal(out=s4[:, 3:4], in_=s4[:, 2:3]), reads=[s4], writes=[s4])
                yield
                o = yo[h][ci % 2]
                P.add("dve", lambda e, o=o, y=y, s4=s4, h=h: e.scalar_tensor_tensor(
                    out=o[:, :], in0=y[:, :], scalar=s4[:, 3:4], in1=lnw_sb[h][:, :], op0=ALU.mult, op1=ALU.mult),
                    reads=[y, s4, lnw_sb[h]], writes=[o])
                P.add("pool", lambda e, o=o, h=h: e.tensor_tensor(out=o[:, :], in0=o[:, :], in1=lnb_sb[h][:, :],
                                                                  op=ALU.add), reads=[o, lnb_sb[h]], writes=[o])
                t0 = w * 128 + c * CH
                P.dma("pool", yout[h, t0:t0 + CH, :], o[:, :], reads=[o], dma_buf=o)
        alive = [head_gen(h) for h in range(4)]
        while alive:
            for g_ in list(alive):
                try:
                    next(g_)
                except StopIteration:
                    alive.remove(g_)
    P.finalize()
    return nc


def lay8(v):
    return np.ascontiguousarray(v.reshape(8, 128).T)


def run_rwkv_pre(x_tok, S, g, x_mix, w_rkv, w0, w1, w2, a0, a1, a2, g1, g2, k_k, k_a, r_k, vres, vfT_full, TT=128):
    ntok = x_tok.shape[0]
    T = ntok // NCORES
    has_vres = vres is not None
    nc = build_rwkv_pre(T, has_vres, TT=TT)
    v0 = vres[0] if has_vres else np.zeros(D, np.float32)
    prm = np.concatenate([lay8(x_mix[i]) for i in range(6)] + [lay8(w0), lay8(a0), lay8(k_k), lay8(k_a), lay8(r_k),
                                                                 lay8(v0)], axis=1)
    bones = np.kron(np.eye(2, dtype=np.float32), np.ones((64, 64), np.float32))
    rmask = np.ones((128, TT), np.float32)
    rmask[:, ::CH] = 0.0
    in_maps = []
    for c in range(NCORES):
        xs = np.zeros((T + 1, D), np.float32)
        xs[1:] = x_tok[c * T:(c + 1) * T]
        if (c * T) % S != 0:
            xs[0] = x_tok[c * T - 1]
        m = {"xT": np.ascontiguousarray(xs.T), "g": lay8(g), "prm": np.ascontiguousarray(prm),
             "wr": w_rkv[0], "wk": w_rkv[1], "wv": w_rkv[2], "w1": w1, "w2": w2, "a1": a1, "a2": a2, "g1": g1,
             "g2": g2, "bones": bones, "rmask": rmask}
        if has_vres:
            m.update({"v1": vres[1], "v2": vres[2], "vfT": np.ascontiguousarray(vfT_full[:, c * T:(c + 1) * T])})
        in_maps.append(m)
    res = run_bass_kernel_spmd(nc, in_maps, core_ids=list(range(NCORES)))
    out = {}
    for n in RW_OUT_BF + RW_OUT_F + ["gC"]:
        out[n] = np.concatenate([r[n] for r in res.results], axis=1)
    return out


def run_rwkv_scan(pre, ln_w, ln_b, B, S):
    NW = S // 128
    nc = build_rwkv_scan(S)
    consts = scan_consts()
    in_maps = []
    for c in range(NCORES):
        b, hg = c // 4, c % 4
        fm = np.zeros((4, 64, NW, 512), dtype=pre["aT"].dtype)
        tk = np.zeros((4, 128, NW, 256), dtype=pre["aT"].dtype)
        gC = np.zeros((4, 64, S // CH), np.float32)
        lnw = np.zeros((4, 64, 64), np.float32)
        lnb = np.zeros((4, 64, 64), np.float32)
        for hh in range(4):
            ch = slice((4 * hg + hh) * 64, (4 * hg + hh + 1) * 64)
            ts = slice(b * S, (b + 1) * S)
            pc = {n: pre[n][ch, ts].reshape(64, NW, 128) for n in RW_OUT_BF}
            for i, n in enumerate(["aT", "rT", "bT", "kT"]):
                fm[hh, :, :, i * 128:(i + 1) * 128] = pc[n]
            for i, n in enumerate(["aT", "BhT", "KhT", "vbT"]):
                tk[hh, :, :, i * 64:(i + 1) * 64] = pc[n].transpose(2, 1, 0)
            gC[hh] = pre["gC"][ch, b * (S // CH):(b + 1) * (S // CH)]
            lnw[hh] = np.broadcast_to(ln_w[ch][None, :], (64, 64))
            lnb[hh] = np.broadcast_to(ln_b[ch][None, :], (64, 64))
        m = {"fm": fm, "tk": tk, "gC": gC, "lnw": lnw, "lnb": lnb}
        m.update(consts)
        in_maps.append(m)
    res = run_bass_kernel_spmd(nc, in_maps, core_ids=list(range(NCORES)))
    ynT = np.zeros((D, B * S), np.float32)
    for c in range(NCORES):
        b, hg = c // 4, c % 4
        y = res.results[c]["yn"]
        for hh in range(4):
            ynT[(4 * hg + hh) * 64:(4 * hg + hh + 1) * 64, b * S:(b + 1) * S] = y[hh].T
    return ynT


def run_linres(xT_full, w, ins):
    ntok = xT_full.shape[1]
    T = ntok // NCORES
    nc = build_linres(T, n_in=len(ins))
    names = ["aT", "bT", "cT"]
    in_maps = []
    for c in range(NCORES):
        sl = slice(c * T, (c + 1) * T)
        m = {"xT": np.ascontiguousarray(xT_full[:, sl]), "w": w}
        for n, a in zip(names, ins):
            m[n] = np.ascontiguousarray(a[:, sl].astype(np.float32))
        in_maps.append(m)
    res = run_bass_kernel_spmd(nc, in_maps, core_ids=list(range(NCORES)))
    return np.concatenate([r["yT"] for r in res.results], axis=1)


def kernel(**inp):
    inp = {k: np.asarray(v) for k, v in inp.items()}
    x = inp["x"]
    B, S, _ = x.shape
    ntok = B * S
    xT = np.ascontiguousarray(x.reshape(ntok, D).T)
    vfT = None
    for i in range(4):
        j = i // 2
        if i % 2 == 0:
            qkv, gates = run_nsa_inproj(np.ascontiguousarray(xT.T), inp["norm_mix"][i], inp["nsa_w_in"][j], S)
            o = run_nsa_attn(qkv, gates, inp["nsa_cmp_w1_k"][j], inp["nsa_cmp_w2_k"][j], inp["nsa_cmp_pe_k"][j],
                             inp["nsa_cmp_w1_v"][j], inp["nsa_cmp_w2_v"][j], inp["nsa_cmp_pe_v"][j], B, S)
            xT = run_linres(xT, inp["nsa_w_out"][j], [np.ascontiguousarray(o.T)])
        else:
            vres = None if j == 0 else (inp["rwkv_v0"][j - 1], inp["rwkv_v1"][j - 1], inp["rwkv_v2"][j - 1])
            pre = run_rwkv_pre(np.ascontiguousarray(xT.T), S, inp["norm_mix"][i], inp["rwkv_x_mix"][j],
                               inp["rwkv_w_rkv"][j], inp["rwkv_w0"][j], inp["rwkv_w1"][j], inp["rwkv_w2"][j],
                               inp["rwkv_a0"][j], inp["rwkv_a1"][j], inp["rwkv_a2"][j], inp["rwkv_g1"][j],
                               inp["rwkv_g2"][j], inp["rwkv_k_k"][j], inp["rwkv_k_a"][j], inp["rwkv_r_k"][j],
                               vres, vfT)
            if j == 0:
                vfT = pre["vT"]
            ynT = run_rwkv_scan(pre, inp["rwkv_ln_w"][j], inp["rwkv_ln_b"][j], B, S)
            xT = run_linres(xT, inp["rwkv_w_out"][j], [ynT, pre["bonT"], pre["gT"]])
        xT = run_mlp_T(xT, inp["norm_mlp"][i], inp["mlp_w1"][i], inp["mlp_w2"][i],
                       gf=inp["norm_final"] if i == 3 else None)
    return np.ascontiguousarray(xT.T).reshape(B, S, D).astype(np.float32)
```

```python
import numpy as np
from contextlib import ExitStack
import concourse.bass as bass
import concourse.mybir as mybir
from concourse.bass_utils import run_bass_kernel_spmd

F32 = mybir.dt.float32
BF16 = mybir.dt.bfloat16
AF = mybir.ActivationFunctionType
ALU = mybir.AluOpType
AX = mybir.AxisListType

NCORES = 8
D = 1024
STG = 1024
SAME_SYNC = True


class Buf:
    def __init__(self, t, name):
        self.t = t
        self.name = name
        self.writers = {}
        self.readers = {}
        self.wgroup = None

    def __getitem__(self, idx):
        return self.t[idx]


class Op:
    __slots__ = ("eng", "fn", "stream", "pos", "waits", "needs_inc", "val", "is_dma")


class Prog:
    ENGS = ["pe", "act", "dve", "pool", "sp"]

    def __init__(self, nc):
        self.nc = nc
        self.stack = ExitStack()
        self.ops = {e: [] for e in self.ENGS}
        self.streams = {e: [] for e in self.ENGS}
        self.seen = {e: {} for e in self.ENGS}
        self.nbuf = 0

    def sb(self, name, shape, dt):
        self.nbuf += 1
        t = self.stack.enter_context(self.nc.sbuf_tensor(f"{name}_{self.nbuf}", list(shape), dt))
        return Buf(t, f"{name}_{self.nbuf}")

    def ps(self, name, shape, dt=F32):
        self.nbuf += 1
        t = self.stack.enter_context(self.nc.psum_tensor(f"{name}_{self.nbuf}", list(shape), dt))
        return Buf(t, f"{name}_{self.nbuf}")

    def add(self, eng, fn, reads=(), writes=(), dma_buf=None, group=None):
        op = Op()
        op.eng = eng
        op.fn = fn
        op.is_dma = dma_buf is not None
        op.needs_inc = op.is_dma
        op.val = None
        op.stream = ("dma", dma_buf.name) if op.is_dma else eng
        st = self.streams.setdefault(op.stream, [])
        op.pos = len(st)
        st.append(op)
        deps = {}

        def need(p):
            if p is op:
                return
            if (not p.is_dma) and p.stream == eng and not op.is_dma:
                if eng == "pe" or not SAME_SYNC:
                    return
            cur = deps.get(p.stream)
            if cur is None or cur.pos < p.pos:
                deps[p.stream] = p

        for b in reads:
            for p in b.writers.values():
                need(p)
        for b in writes:
            same_group = group is not None and b.wgroup == group
            if not same_group:
                for p in b.writers.values():
                    need(p)
            for p in b.readers.values():
                need(p)
        op.waits = []
        seen = self.seen[eng]
        for s, p in deps.items():
            if seen.get(s, -1) >= p.pos:
                continue
            seen[s] = p.pos
            p.needs_inc = True
            op.waits.append(p)
        for b in reads:
            b.readers[op.stream] = op
        for b in writes:
            same_group = group is not None and b.wgroup == group
            if same_group:
                b.writers[op.stream] = op
            else:
                b.writers = {op.stream: op}
                b.wgroup = group
            b.readers = {}
        self.ops[eng].append(op)
        return op

    def dma(self, q, out, in_, reads=(), writes=(), dma_buf=None, group=None, **kw):
        return self.add(q, lambda e: e.dma_start(out=out, in_=in_, **kw), reads, writes,
                        dma_buf=dma_buf, group=group)

    def mm(self, out, lhsT, rhs, start, stop, reads=(), writes=(), **kw):
        return self.add("pe", lambda e: e.matmul(out, lhsT, rhs, start=start, stop=stop, **kw), reads, writes)

    def finalize(self):
        nc = self.nc
        sems = {}
        for s, st in self.streams.items():
            if not any(o.needs_inc for o in st):
                continue
            nm = s if isinstance(s, str) else "d_" + s[1]
            sems[s] = self.stack.enter_context(nc.semaphore("s_" + nm))
            c = 0
            for o in st:
                if o.needs_inc:
                    c += 16 if o.is_dma else 1
                o.val = c
        self.nsem = len(sems)
        finals = [(sems[s], st[-1].val) for s, st in self.streams.items() if not isinstance(s, str) and st]
        block = self.stack.enter_context(nc.Block())
        engmap = {"pe": block.tensor, "act": block.scalar, "dve": block.vector, "pool": block.gpsimd,
                  "sp": block.sync}

        def make(engname):
            ops = self.ops[engname]

            def body(e):
                for o in ops:
                    for p in o.waits:
                        e.wait_ge(sems[p.stream], p.val)
                    inst = o.fn(e)
                    if o.needs_inc:
                        inst.then_inc(sems[o.stream], 16 if o.is_dma else 1)
                if engname == "sp":
                    for sem, v in finals:
                        e.wait_ge(sem, v)
            return body

        for en in self.ENGS:
            engmap[en](make(en))
        self.stack.close()


class Ctx:
    def __init__(self, nc):
        self.nc = nc
        self.P = Prog(nc)
        P = self.P
        self.ones32 = P.sb("ones32", [128, 128], F32)
        P.add("pool", lambda e: e.memset(self.ones32[:, :], 1.0), writes=[self.ones32])
        self.banks = [P.ps(f"bank{i}", [128, 512], F32) for i in range(8)]
        self.stg = [P.sb(f"stg{i}", [128, STG], F32) for i in range(2)]
        self.nstg = 0
        self.ncast = 0

    def load_cast(self, dst_buf, dst_ap, src_ap, ncols, eng=None, parts=128):
        P = self.P
        stg = self.stg[self.nstg % 2]
        q = "sp"
        self.nstg += 1
        P.dma(q, stg[0:parts, :ncols], src_ap, writes=[stg], dma_buf=stg)
        if eng is None:
            eng = ["pool", "dve"][self.ncast % 2]
            self.ncast += 1
        P.add(eng, lambda e: e.tensor_copy(out=dst_ap, in_=stg[0:parts, :ncols]), reads=[stg], writes=[dst_buf])

    def load_weight(self, name, w_dram, K, N, eng=None):
        P = self.P
        wv = w_dram.rearrange("(kc p) n -> p kc n", p=128)
        out = []
        for kc in range(K // 128):
            b = P.sb(f"{name}{kc}", [128, N], BF16)
            for c0 in range(0, N, STG):
                n = min(STG, N - c0)
                self.load_cast(b, b[:, c0:c0 + n], wv[:, kc, c0:c0 + n], n, eng=eng)
            out.append(b)
        return out

    def rmsnorm(self, xt, g_sb, hn, TT, sq, rstd, bank, eps_t):
        P = self.P
        P.add("act", lambda e: e.activation(out=sq[:, :, :], in_=xt[:, :, :TT], func=AF.Square),
              reads=[xt], writes=[sq])
        for c in range(8):
            P.mm(bank[:, :TT], self.ones32[:, :], sq[:, c, :], start=(c == 0), stop=(c == 7),
                 reads=[self.ones32, sq], writes=[bank])
        P.add("act", lambda e: e.activation(out=rstd[:, :], in_=bank[:, :TT], func=AF.Sqrt,
                                            bias=eps_t[:, 0:1], scale=1.0 / D),
              reads=[bank, eps_t], writes=[rstd])
        P.add("dve", lambda e: e.reciprocal(out=rstd[:, :], in_=rstd[:, :]), reads=[rstd], writes=[rstd])
        for c in range(8):
            P.add("dve", lambda e, c=c: e.scalar_tensor_tensor(
                out=hn[:, c, :], in0=xt[:, c, :TT], scalar=g_sb[:, c:c + 1], in1=rstd[:, :],
                op0=ALU.mult, op1=ALU.mult), reads=[xt, g_sb, rstd], writes=[hn])


def build_mlp(T, TT=256, final=False):
    nc = bass.Bass("TRN2", target_bir_lowering=False)
    xT = nc.dram_tensor("xT", [D, T], F32, kind="ExternalInput").ap()
    g = nc.dram_tensor("g", [128, 8], F32, kind="ExternalInput").ap()
    w1 = nc.dram_tensor("w1", [D, 4 * D], F32, kind="ExternalInput").ap()
    w2 = nc.dram_tensor("w2", [4 * D, D], F32, kind="ExternalInput").ap()
    yT = nc.dram_tensor("yT", [D, T], F32, kind="ExternalOutput").ap()
    C = Ctx(nc)
    P = C.P
    xv = xT.rearrange("(c p) t -> p c t", p=128)
    yv = yT.rearrange("(c p) t -> p c t", p=128)
    g_sb = P.sb("g", [128, 8], F32)
    P.dma("sp", g_sb[:, :], g, writes=[g_sb], dma_buf=g_sb)
    if final:
        gf = nc.dram_tensor("gf", [128, 8], F32, kind="ExternalInput").ap()
        gf_sb = P.sb("gf", [128, 8], F32)
        P.dma("sp", gf_sb[:, :], gf, writes=[gf_sb], dma_buf=gf_sb)
    eps_t = P.sb("eps", [128, 1], F32)
    P.add("pool", lambda e: e.memset(eps_t[:, :], 1e-5), writes=[eps_t])
    xb = [P.sb(f"x{i}", [128, 8, TT], F32) for i in range(2)]
    sq = P.sb("sq", [128, 8, TT], F32)
    rstd = P.sb("rstd", [128, TT], F32)
    hn = P.sb("hn", [128, 8, TT], BF16)
    h1 = [P.sb(f"h1_{i}", [128, 8, TT], BF16) for i in range(4)]
    rl = [P.sb(f"rl{i}", [128, TT], F32) for i in range(3)]
    P.dma("sp", xb[0][:, :, :], xv[:, :, 0:TT], writes=[xb[0]], dma_buf=xb[0])
    w1s = C.load_weight("w1_", w1, D, 4 * D)
    w2s = C.load_weight("w2_", w2, 4 * D, D)
    nt = T // TT
    nb = 0
    for tt in range(nt):
        xt = xb[tt % 2]
        if tt + 1 < nt:
            nx = xb[(tt + 1) % 2]
            P.dma("sp", nx[:, :, :], xv[:, :, (tt + 1) * TT:(tt + 2) * TT], writes=[nx], dma_buf=nx)
        C.rmsnorm(xt, g_sb, hn, TT, sq, rstd, C.banks[7], eps_t)
        for fc in range(32):
            bank = C.banks[nb % 4]
            nb += 1
            for kc in range(8):
                P.mm(bank[:, :TT], w1s[kc][:, fc * 128:(fc + 1) * 128], hn[:, kc, :], start=(kc == 0),
                     stop=(kc == 7), reads=[w1s[kc], hn], writes=[bank])
            r = rl[fc % 3]
            P.add("act", lambda e, r=r, bank=bank: e.activation(out=r[:, :], in_=bank[:, :TT], func=AF.Relu),
                  reads=[bank], writes=[r])
            hb = h1[fc // 8]
            P.add("pool", lambda e, r=r, hb=hb, fc=fc: e.tensor_tensor(
                out=hb[:, fc % 8, :], in0=r[:, :], in1=r[:, :], op=ALU.mult), reads=[r], writes=[hb])
        for fc in range(8):
            bank = C.banks[4 + fc % 2]
            for kc in range(32):
                P.mm(bank[:, :TT], w2s[kc][:, fc * 128:(fc + 1) * 128], h1[kc // 8][:, kc % 8, :],
                     start=(kc == 0), stop=(kc == 31), reads=[w2s[kc], h1[kc // 8]], writes=[bank])
            P.add("dve", lambda e, xt=xt, bank=bank, fc=fc: e.tensor_tensor(
                out=xt[:, fc, :], in0=xt[:, fc, :], in1=bank[:, :TT], op=ALU.add), reads=[xt, bank], writes=[xt])
        if final:
            C.rmsnorm(xt, gf_sb, xt, TT, sq, rstd, C.banks[7], eps_t)
        P.dma("pool", yv[:, :, tt * TT:(tt + 1) * TT], xt[:, :, :], reads=[xt], dma_buf=xt)
    P.finalize()
    return nc


def run_mlp_T(xT_full, g, w1, w2, gf=None):
    ntok = xT_full.shape[1]
    T = ntok // NCORES
    nc = build_mlp(T, final=gf is not None)
    in_maps = []
    for c in range(NCORES):
        m = {"xT": np.ascontiguousarray(xT_full[:, c * T:(c + 1) * T]), "g": lay8(g), "w1": w1, "w2": w2}
        if gf is not None:
            m["gf"] = lay8(gf)
        in_maps.append(m)
    res = run_bass_kernel_spmd(nc, in_maps, core_ids=list(range(NCORES)))
    return np.concatenate([r["yT"] for r in res.results], axis=1)


def run_mlp(x_tok, g, w1, w2):
    ntok = x_tok.shape[0]
    T = ntok // NCORES
    nc = build_mlp(T)
    g_l = np.ascontiguousarray(g.reshape(8, 128).T)
    in_maps = [{"xT": np.ascontiguousarray(x_tok[c * T:(c + 1) * T].T), "g": g_l, "w1": w1, "w2": w2}
               for c in range(NCORES)]
    res = run_bass_kernel_spmd(nc, in_maps, core_ids=list(range(NCORES)))
    return np.concatenate([r["yT"].T for r in res.results], axis=0)


def build_linres(T, n_in=1, TT=512):
    TT = min(TT, T)
    nc = bass.Bass("TRN2", target_bir_lowering=False)
    xT = nc.dram_tensor("xT", [D, T], F32, kind="ExternalInput").ap()
    aT = nc.dram_tensor("aT", [D, T], F32, kind="ExternalInput").ap()
    if n_in == 3:
        bT = nc.dram_tensor("bT", [D, T], F32, kind="ExternalInput").ap()
        cT = nc.dram_tensor("cT", [D, T], F32, kind="ExternalInput").ap()
    w = nc.dram_tensor("w", [D, D], F32, kind="ExternalInput").ap()
    yT = nc.dram_tensor("yT", [D, T], F32, kind="ExternalOutput").ap()
    C = Ctx(nc)
    P = C.P
    v3 = lambda ap: ap.rearrange("(c p) t -> p c t", p=128)
    xv, av, yv = v3(xT), v3(aT), v3(yT)
    ws = C.load_weight("w_", w, D, D)
    xb = [P.sb(f"x{i}", [128, 8, TT], F32) for i in range(2)]
    ab = [P.sb(f"a{i}", [128, 8, TT], F32) for i in range(2)]
    if n_in == 3:
        bv, cv = v3(bT), v3(cT)
        bb = [P.sb(f"b{i}", [128, 8, TT], F32) for i in range(2)]
        cb = [P.sb(f"c{i}", [128, 8, TT], F32) for i in range(2)]
    z = P.sb("z", [128, 8, TT], BF16)
    nt = T // TT
    for tt in range(nt):
        sl = slice(tt * TT, (tt + 1) * TT)
        xt, at = xb[tt % 2], ab[tt % 2]
        P.dma("sp", xt[:, :, :], xv[:, :, sl], writes=[xt], dma_buf=xt)
        P.dma("sp", at[:, :, :], av[:, :, sl], writes=[at], dma_buf=at)
        if n_in == 3:
            bt, ct = bb[tt % 2], cb[tt % 2]
            P.dma("sp", bt[:, :, :], bv[:, :, sl], writes=[bt], dma_buf=bt)
            P.dma("sp", ct[:, :, :], cv[:, :, sl], writes=[ct], dma_buf=ct)
            P.add("pool", lambda e, at=at, bt=bt: e.tensor_tensor(out=at[:, :, :], in0=at[:, :, :], in1=bt[:, :, :],
                                                                  op=ALU.add), reads=[at, bt], writes=[at])
            P.add("dve", lambda e, at=at, ct=ct: e.tensor_tensor(out=z[:, :, :], in0=at[:, :, :], in1=ct[:, :, :],
                                                                 op=ALU.mult), reads=[at, ct], writes=[z])
        else:
            P.add("pool", lambda e, at=at: e.tensor_copy(out=z[:, :, :], in_=at[:, :, :]), reads=[at], writes=[z])
        for fc in range(8):
            bank = C.banks[fc % 4]
            for kc in range(8):
                P.mm(bank[:, :TT], ws[kc][:, fc * 128:(fc + 1) * 128], z[:, kc, :], start=(kc == 0),
                     stop=(kc == 7), reads=[ws[kc], z], writes=[bank])
            P.add("dve", lambda e, xt=xt, bank=bank, fc=fc: e.tensor_tensor(
                out=xt[:, fc, :], in0=xt[:, fc, :], in1=bank[:, :TT], op=ALU.add), reads=[xt, bank], writes=[xt])
        P.dma("pool", yv[:, :, sl], xt[:, :, :], reads=[xt], dma_buf=xt)
    P.finalize()
    return nc


NSA_PROJ = 2608


def build_nsa_inproj(T, TT=256):
    nc = bass.Bass("TRN2", target_bir_lowering=False)
    xT = nc.dram_tensor("xT", [D, T], F32, kind="ExternalInput").ap()
    g = nc.dram_tensor("g", [128, 8], F32, kind="ExternalInput").ap()
    w = nc.dram_tensor("w", [D, NSA_PROJ], F32, kind="ExternalInput").ap()
    cosx = nc.dram_tensor("cosx", [T, 64], F32, kind="ExternalInput").ap()
    sinx = nc.dram_tensor("sinx", [T, 64], F32, kind="ExternalInput").ap()
    qkv = nc.dram_tensor("qkv", [T, 2560], BF16, kind="ExternalOutput").ap()
    gates = nc.dram_tensor("gates", [T, 48], F32, kind="ExternalOutput").ap()
    C = Ctx(nc)
    P = C.P
    xv = xT.rearrange("(c p) t -> p c t", p=128)
    g_sb = P.sb("g", [128, 8], F32)
    P.dma("sp", g_sb[:, :], g, writes=[g_sb], dma_buf=g_sb)
    eps_t = P.sb("eps", [128, 1], F32)
    P.add("pool", lambda e: e.memset(eps_t[:, :], 1e-5), writes=[eps_t])
    nsub = T // 128
    cs = P.sb("cs", [128, nsub, 8, 8], F32)
    sn = P.sb("sn", [128, nsub, 8, 8], F32)
    P.dma("sp", cs[:, :, :, :], cosx.rearrange("(n p) (h d) -> p n h d", p=128, d=8), writes=[cs], dma_buf=cs)
    P.dma("sp", sn[:, :, :, :], sinx.rearrange("(n p) (h d) -> p n h d", p=128, d=8), writes=[sn], dma_buf=sn)
    xb = [P.sb(f"x{i}", [128, 8, TT], F32) for i in range(2)]
    sq = P.sb("sq", [128, 8, TT], F32)
    rstd = P.sb("rstd", [128, TT], F32)
    hn = P.sb("hn", [128, 8, TT], BF16)
    ob = [P.sb(f"ob{i}", [128, 8, 64], F32) for i in range(3)]
    tmp = [P.sb(f"tmp{i}", [128, 8, 8], F32) for i in range(4)]
    obig = [P.sb(f"obig{i}", [128, 2560], BF16) for i in range(2)]
    gsb = [P.sb(f"gsb{i}", [128, 48], F32) for i in range(2)]
    P.dma("sp", xb[0][:, :, :], xv[:, :, 0:TT], writes=[xb[0]], dma_buf=xb[0])
    ws = C.load_weight("w_", w, D, NSA_PROJ)
    nt = T // TT
    nb = 0
    for tt in range(nt):
        xt = xb[tt % 2]
        if tt + 1 < nt:
            nx = xb[(tt + 1) % 2]
            P.dma("sp", nx[:, :, :], xv[:, :, (tt + 1) * TT:(tt + 2) * TT], writes=[nx], dma_buf=nx)
        C.rmsnorm(xt, g_sb, hn, TT, sq, rstd, C.banks[7], eps_t)
        for ts in range(TT // 128):
            isub = tt * (TT // 128) + ts
            og = obig[isub % 2]
            for cc in range(6):
                c0 = cc * 512
                n = min(512, NSA_PROJ - c0)
                bank = C.banks[nb % 4]
                nb += 1
                for kc in range(8):
                    P.mm(bank[:, :n], hn[:, kc, ts * 128:(ts + 1) * 128], ws[kc][:, c0:c0 + n], start=(kc == 0),
                         stop=(kc == 7), reads=[ws[kc], hn], writes=[bank])
                if cc == 5:
                    gs = gsb[isub % 2]
                    P.add("act", lambda e, gs=gs, bank=bank: e.activation(out=gs[:, :], in_=bank[:, :48],
                                                                         func=AF.Sigmoid), reads=[bank], writes=[gs])
                    P.dma("pool", gates[isub * 128:(isub + 1) * 128, :], gs[:, :], reads=[gs], dma_buf=gs)
                    continue
                o = ob[nb % 3]
                P.add("act", lambda e, o=o, bank=bank: e.activation(
                    out=o[:, :, :], in_=bank[:, :512].rearrange("p (h d) -> p h d", d=64), func=AF.Copy),
                    reads=[bank], writes=[o])
                nh = 8 if cc < 2 else 4
                A = o[:, 0:nh, 0:8]
                B = o[:, 0:nh, 8:16]
                cst = cs[:, isub, 0:nh, :]
                snt = sn[:, isub, 0:nh, :]
                t1, t2, t3, t4 = [t[:, 0:nh, :] for t in tmp]
                for (dst, a, b) in ((t1, A, cst), (t2, B, snt), (t3, B, cst), (t4, A, snt)):
                    P.add("dve", lambda e, dst=dst, a=a, b=b: e.tensor_tensor(out=dst, in0=a, in1=b, op=ALU.mult),
                          reads=[o, cs, sn], writes=[tmp[0]])
                P.add("dve", lambda e, A=A, t1=t1, t2=t2: e.tensor_tensor(out=A, in0=t1, in1=t2, op=ALU.subtract),
                      reads=[tmp[0]], writes=[o])
                P.add("dve", lambda e, B=B, t3=t3, t4=t4: e.tensor_tensor(out=B, in0=t3, in1=t4, op=ALU.add),
                      reads=[tmp[0]], writes=[o])
                P.add("pool", lambda e, og=og, o=o, c0=c0: e.tensor_copy(
                    out=og[:, c0:c0 + 512].rearrange("p (h d) -> p h d", d=64), in_=o[:, :, :]),
                    reads=[o], writes=[og])
            P.dma("pool", qkv[isub * 128:(isub + 1) * 128, :], og[:, :], reads=[og], dma_buf=og)
    P.finalize()
    return nc


def rope_tables(S):
    inv = (500000.0 ** (-np.arange(0, 16, 2, dtype=np.float32) / 16.0)).astype(np.float32)
    ang = np.arange(S, dtype=np.float32)[:, None] * inv[None, :]
    return np.cos(ang).astype(np.float32), np.sin(ang).astype(np.float32)


def run_nsa_inproj(x_tok, g, w_in, S):
    ntok = x_tok.shape[0]
    T = ntok // NCORES
    nc = build_nsa_inproj(T)
    g_l = np.ascontiguousarray(g.reshape(8, 128).T)
    cos, sin = rope_tables(S)
    cosx = np.tile(cos, (1, 8))
    sinx = np.tile(sin, (1, 8))
    in_maps = []
    for c in range(NCORES):
        pos = (np.arange(c * T, (c + 1) * T)) % S
        in_maps.append({"xT": np.ascontiguousarray(x_tok[c * T:(c + 1) * T].T), "g": g_l, "w": w_in,
                        "cosx": np.ascontiguousarray(cosx[pos]), "sinx": np.ascontiguousarray(sinx[pos])})
    res = run_bass_kernel_spmd(nc, in_maps, core_ids=list(range(NCORES)))
    qkv = np.concatenate([r["qkv"] for r in res.results], axis=0)
    gates = np.concatenate([r["gates"] for r in res.results], axis=0)
    return qkv, gates


NEGV = -30000.0
SCALE = 0.125


def nsa_consts():
    tl = np.arange(128)[:, None]
    kl = np.arange(128)[None, :]
    c = {}
    c["ident"] = np.eye(128, dtype=np.float32)
    c["i4"] = np.tile(np.eye(128, dtype=np.float32), (1, 4))
    c["causal"] = np.where(kl > tl, NEGV, 0.0).astype(np.float32)
    c["winneg"] = np.where(kl > tl, 0.0, NEGV).astype(np.float32)
    negc = np.zeros((128, 17, 128), np.float32)
    for dl in range(17):
        negc[:, dl, :] = np.where(16 * kl - tl <= 128 * dl - 31, 0.0, NEGV)
    c["negc"] = negc.reshape(128, 17 * 128)
    cc = np.arange(8)[None, :]
    c["cnc"] = np.where(16 * cc + 15 <= tl, 0.0, NEGV).astype(np.float32)
    f = np.zeros((128, 3), np.float32)
    f[:64] = [1e30, 2e30, -1e30]
    f[64:] = [0.0, 1e30, 2e30]
    c["force"] = f
    return c


def build_nsa_attn(S, debug=False):
    NT = S // 128
    NCMP = S // 16
    nc = bass.Bass("TRN2", target_bir_lowering=False)
    dt_in = lambda name, shape, dt: nc.dram_tensor(name, shape, dt, kind="ExternalInput").ap()
    QT = dt_in("QT", [64, 4, S], BF16)
    KcT = dt_in("KcT", [64, S], BF16)
    VcT = dt_in("VcT", [64, S], BF16)
    KsT = dt_in("KsT", [64, S], BF16)
    KwT = dt_in("KwT", [64, S], BF16)
    Vs1 = dt_in("Vs1", [128, NT, 65], BF16)
    Vw1 = dt_in("Vw1", [128, NT, 65], BF16)
    gat = dt_in("gat", [128, NT, 12], F32)
    w1k = dt_in("w1k", [64, 32 * 256], F32)
    w1v = dt_in("w1v", [64, 32 * 256], F32)
    w2k = dt_in("w2k", [128, 2 * 64], F32)
    w2v = dt_in("w2v", [128, 2 * 64], F32)
    pekT = dt_in("pekT", [64, 32], F32)
    pevT = dt_in("pevT", [64, 32], F32)
    c_ident = dt_in("ident", [128, 128], F32)
    c_i4 = dt_in("i4", [128, 512], F32)
    c_causal = dt_in("causal", [128, 128], F32)
    c_winneg = dt_in("winneg", [128, 128], F32)
    c_negc = dt_in("negc", [128, 17 * 128], F32)
    c_cnc = dt_in("cnc", [128, 8], F32)
    c_force = dt_in("force", [128, 3], F32)
    out = nc.dram_tensor("o", [S, 256], F32, kind="ExternalOutput").ap()
    C = Ctx(nc)
    P = C.P
    banks = C.banks

    def resident(name, src, shape, dt):
        b = P.sb(name, shape, dt)
        P.dma("sp", b[tuple(slice(None) for _ in shape)], src, writes=[b], dma_buf=b)
        return b

    KsT_sb = resident("KsT", KsT, [64, S], BF16)
    KwT_sb = resident("KwT", KwT, [64, S], BF16)
    Vs_sb = resident("Vs1", Vs1, [128, NT, 65], BF16)
    Vw_sb = resident("Vw1", Vw1, [128, NT, 65], BF16)
    gat_sb = resident("gat", gat, [128, NT, 12], F32)
    force_sb = resident("force", c_force, [128, 3], F32)

    def const_bf(name, src, n):
        b = P.sb(name, [128, n], BF16)
        for c0 in range(0, n, STG):
            m = min(STG, n - c0)
            C.load_cast(b, b[:, c0:c0 + m], src[:, c0:c0 + m], m)
        return b

    ident = const_bf("ident", c_ident, 128)
    i4 = const_bf("i4", c_i4, 512)
    causal = const_bf("causal", c_causal, 128)
    winneg = const_bf("winneg", c_winneg, 128)
    negc = const_bf("negc", c_negc, 17 * 128)
    cnc = const_bf("cnc", c_cnc, 8)

    KcC = P.sb("KcC", [64, NCMP], BF16)
    Vc1 = P.sb("Vc1", [128, NCMP // 128, 65], BF16)
    P.add("pool", lambda e: e.memset(KcC[:, :], 0.0), writes=[KcC])
    P.add("pool", lambda e: e.memset(Vc1[:, :, :], 0.0), writes=[Vc1])
    P.add("pool", lambda e: e.memset(Vc1[:, :, 64:65], 1.0), writes=[Vc1])
    src_sb = P.sb("cmpsrc", [64, S], BF16)
    w1_sb = P.sb("w1c", [64, 32 * 256], BF16)
    w2_sb = P.sb("w2c", [128, 128], BF16)
    pe_sb = P.sb("pec", [64, 32], BF16)
    bias_sb = P.sb("biasc", [128, 2], F32)
    hid = [P.sb(f"hid{i}", [128, NCMP], BF16) for i in range(2)]
    xs = P.sb("xs", [128, 512], F32)
    x2 = P.sb("x2", [128, 512], F32)
    sg = P.sb("sg", [128, 512], F32)
    NB = NCMP - 1
    for which, (srcT, w1d, w2d, ped) in enumerate(((KcT, w1k, w2k, pekT), (VcT, w1v, w2v, pevT))):
        P.dma("sp", src_sb[:, :], srcT, writes=[src_sb], dma_buf=src_sb)
        for c0 in range(0, 32 * 256, STG):
            C.load_cast(w1_sb, w1_sb[:, c0:c0 + STG], w1d[:, c0:c0 + STG], STG, parts=64)
        C.load_cast(w2_sb, w2_sb[:, :], w2d[:, :], 128)
        stg = C.stg[C.nstg % 2]
        C.nstg += 1
        P.dma("sp", stg[0:64, 0:32], ped, writes=[stg], dma_buf=stg)
        P.add("dve", lambda e, stg=stg: e.tensor_copy(out=pe_sb[:, :], in_=stg[0:64, 0:32]), reads=[stg],
              writes=[pe_sb])
        for hc in range(2):
            bb = banks[6]
            for j in range(32):
                P.mm(bb[:, 0:1], w1_sb[:, j * 256 + hc * 128: j * 256 + (hc + 1) * 128], pe_sb[:, j:j + 1],
                     start=(j == 0), stop=(j == 31), reads=[w1_sb, pe_sb], writes=[bb])
            P.add("dve", lambda e, bb=bb, hc=hc: e.tensor_copy(out=bias_sb[:, hc:hc + 1], in_=bb[:, 0:1]),
                  reads=[bb], writes=[bias_sb])
            for b0 in range(0, NB, 512):
                n = min(512, NB - b0)
                bank = banks[(b0 // 512) % 2]
                for j in range(32):
                    rhs = src_sb[:, j + 16 * b0: j + 16 * (b0 + n - 1) + 1: 16]
                    P.mm(bank[:, :n], w1_sb[:, j * 256 + hc * 128: j * 256 + (hc + 1) * 128], rhs,
                         start=(j == 0), stop=(j == 31), reads=[w1_sb, src_sb], writes=[bank])
                P.add("act", lambda e, bank=bank, n=n, hc=hc: e.activation(
                    out=xs[:, :n], in_=bank[:, :n], func=AF.Identity, bias=bias_sb[:, hc:hc + 1]),
                    reads=[bank, bias_sb], writes=[xs])
                P.add("dve", lambda e, n=n: e.tensor_tensor(out=x2[:, :n], in0=xs[:, :n], in1=xs[:, :n],
                                                            op=ALU.mult), reads=[xs], writes=[x2])
                P.add("dve", lambda e, n=n: e.tensor_scalar(out=x2[:, :n], in0=x2[:, :n], scalar1=0.044715,
                                                            scalar2=1.0, op0=ALU.mult, op1=ALU.add),
                      reads=[x2], writes=[x2])
                P.add("dve", lambda e, n=n: e.tensor_tensor(out=x2[:, :n], in0=x2[:, :n], in1=xs[:, :n],
                                                            op=ALU.mult), reads=[x2, xs], writes=[x2])
                P.add("act", lambda e, n=n: e.activation(out=sg[:, :n], in_=x2[:, :n], func=AF.Sigmoid,
                                                         scale=1.5957691216057308), reads=[x2], writes=[sg])
                P.add("dve", lambda e, n=n, b0=b0, hc=hc: e.tensor_tensor(
                    out=hid[hc][:, b0:b0 + n], in0=xs[:, :n], in1=sg[:, :n], op=ALU.mult),
                    reads=[xs, sg], writes=[hid[hc]])
        if which == 0:
            for b0 in range(0, NB, 512):
                n = min(512, NB - b0)
                bank = banks[2 + (b0 // 512) % 2]
                for hc in range(2):
                    P.mm(bank[0:64, :n], w2_sb[:, hc * 64:(hc + 1) * 64], hid[hc][:, b0:b0 + n], start=(hc == 0),
                         stop=(hc == 1), reads=[w2_sb, hid[hc]], writes=[bank])
                P.add("act", lambda e, bank=bank, n=n, b0=b0: e.activation(
                    out=KcC[:, b0:b0 + n], in_=bank[0:64, :n], func=AF.Copy), reads=[bank], writes=[KcC])
        else:
            for nt in range(NCMP // 128):
                n = min(128, NB - nt * 128)
                bank = banks[2 + nt % 2]
                for hc in range(2):
                    P.mm(bank[0:n, 0:64], hid[hc][:, nt * 128: nt * 128 + n], w2_sb[:, hc * 64:(hc + 1) * 64],
                         start=(hc == 0), stop=(hc == 1), reads=[w2_sb, hid[hc]], writes=[bank])
                P.add("act", lambda e, bank=bank, n=n, nt=nt: e.activation(
                    out=Vc1[0:n, nt, 0:64], in_=bank[0:n, 0:64], func=AF.Copy), reads=[bank], writes=[Vc1])

    qsb = [P.sb(f"q{i}", [64, 4, 128], BF16) for i in range(3)]
    PT = [P.sb(f"PT{i}", [128, 512], BF16) for i in range(3)]
    Eh = [P.sb(f"Eh{i}", [128, NCMP], F32) for i in range(2)]
    acc = P.sb("acc", [128, NCMP + 8], F32)
    imp = P.sb("imp", [128, 256], F32)
    work = P.sb("work", [128, 256], F32)
    m8 = P.sb("m8", [128, 16], F32)
    thr = P.sb("thr", [128, 1], F32)
    negm = P.sb("negm", [128, 256], BF16)
    nmx = [P.sb(f"nmx{i}", [128, 2, 64], BF16) for i in range(4)]
    osb = P.sb("osb", [128, 3, 260], F32)
    ocs = P.sb("ocs", [128, 260], F32)
    rinv = P.sb("rinv", [128, 4], F32)
    rden = P.sb("rden", [128, 3, 4], F32)
    coef = P.sb("coef", [128, 4, 3], F32)
    ot = [P.sb(f"ot{i}", [128, 4, 64], F32) for i in range(2)]
    P.add("pool", lambda e: e.memset(acc[:, :], 0.0), writes=[acc])
    bS = [banks[0], banks[1], banks[2]]
    bOc, bOs, bOw = banks[3], banks[4], banks[5]
    bE = [banks[6], banks[7]]
    cnt = {"s": 0, "e": 0, "x": 0}

    def issue_scores(KT_sb, kt, Q2, q_buf, mask_fn):
        bank = bS[cnt["s"] % 3]
        pt = PT[cnt["s"] % 3]
        cnt["s"] += 1
        masks = mask_fn()
        P.mm(bank[:, :], KT_sb[:, kt * 128:(kt + 1) * 128], Q2, start=True, stop=(len(masks) == 0),
             reads=[KT_sb, q_buf], writes=[bank])
        for mi, (m_ap, m_bufs) in enumerate(masks):
            P.mm(bank[:, :], m_ap, i4[:, :], start=False, stop=(mi == len(masks) - 1),
                 reads=list(m_bufs) + [i4], writes=[bank])
        P.add("act", lambda e: e.activation(out=pt[:, :], in_=bank[:, :], func=AF.Exp, scale=SCALE),
              reads=[bank], writes=[pt])
        return pt

    def issue_pv(pt, V_sb, v_idx, Obank, first):
        for h in range(4):
            P.mm(Obank[:, h * 65:(h + 1) * 65], pt[:, h * 128:(h + 1) * 128], V_sb[:, v_idx, :],
                 start=(first and h == 0), stop=True, reads=[pt, V_sb], writes=[Obank], skip_group_check=True)

    LOOK = 2

    def run_branch(tiles, KT_sb, Q2, q_buf, V_sb, Obank):
        pend = []
        for i, (kt, v_idx, mask_fn) in enumerate(tiles):
            pend.append((issue_scores(KT_sb, kt, Q2, q_buf, mask_fn), v_idx, i == 0))
            if len(pend) > LOOK:
                pt, vi, first = pend.pop(0)
                issue_pv(pt, V_sb, vi, Obank, first)
        for pt, vi, first in pend:
            issue_pv(pt, V_sb, vi, Obank, first)

    for qt in range(NT):
        qb = qsb[qt % 3]
        P.dma("sp", qb[:, :, :], QT[:, :, qt * 128:(qt + 1) * 128], writes=[qb], dma_buf=qb)
        Q2 = qb[:, :, :].rearrange("p h t -> p (h t)")
        ncols = min(8 * qt + 7, NB)
        nnt = (ncols + 127) // 128
        tiles = []
        for nt in range(nnt):
            dl = qt - 16 * nt
            masks = [(negc[:, dl * 128:(dl + 1) * 128], [negc])] if dl <= 16 else []
            tiles.append((nt, nt, lambda masks=masks: masks))
        run_branch(tiles, KcC, Q2, qb, Vc1, bOc)
        P.add("act", lambda e: e.activation(out=ocs[:, :], in_=bOc[:, 0:260], func=AF.Copy), reads=[bOc],
              writes=[ocs])
        P.add("pool", lambda e: e.tensor_copy(out=osb[:, 0, :], in_=ocs[:, :]), reads=[ocs], writes=[osb])
        sel = qt >= 8
        if sel:
            ocv = ocs[:, :].rearrange("p (h c) -> p h c", c=65)
            P.add("dve", lambda e, ocv=ocv: e.tensor_scalar(out=rinv[:, :], in0=ocv[:, :, 64], scalar1=1e-30,
                                                            scalar2=None, op0=ALU.add), reads=[ocs], writes=[rinv])
            P.add("dve", lambda e: e.reciprocal(out=rinv[:, :], in_=rinv[:, :]), reads=[rinv], writes=[rinv])
            w0 = 8 * qt - 1
            for h in range(4):
                eh = Eh[cnt["e"] % 2]
                for c0 in range(0, ncols, 512):
                    n = min(512, ncols - c0)
                    bank = bE[cnt["e"] % 2]
                    cnt["e"] += 1
                    lo = max(w0, c0)
                    hi = min(w0 + 8, c0 + n)
                    P.mm(bank[:, :n], qb[:, h, :], KcC[:, c0:c0 + n], start=True, stop=(hi <= lo),
                         reads=[qb, KcC], writes=[bank])
                    if hi > lo:
                        P.mm(bank[:, lo - c0:hi - c0], ident[:, :], cnc[:, lo - w0:hi - w0], start=False, stop=True,
                             reads=[ident, cnc], writes=[bank], skip_group_check=True)
                    P.add("act", lambda e, eh=eh, bank=bank, c0=c0, n=n: e.activation(
                        out=eh[:, c0:c0 + n], in_=bank[:, :n], func=AF.Exp, scale=SCALE), reads=[bank], writes=[eh])
                if h == 0:
                    P.add("dve", lambda e, eh=eh, ncols=ncols: e.tensor_scalar(
                        out=acc[:, :ncols], in0=eh[:, :ncols], scalar1=rinv[:, 0:1], scalar2=None, op0=ALU.mult),
                        reads=[eh, rinv], writes=[acc])
                else:
                    P.add("dve", lambda e, eh=eh, ncols=ncols, h=h: e.scalar_tensor_tensor(
                        out=acc[:, :ncols], in0=eh[:, :ncols], scalar=rinv[:, h:h + 1], in1=acc[:, :ncols],
                        op0=ALU.mult, op1=ALU.add), reads=[eh, rinv, acc], writes=[acc])
            nbk = 2 * qt + 2
            P.add("dve", lambda e, nbk=nbk: e.tensor_reduce(
                out=imp[:, 0:nbk], in_=acc[:, 0:4 * nbk].rearrange("p (s r) -> p s r", r=4), axis=AX.X, op=ALU.add),
                reads=[acc], writes=[imp])
            P.add("dve", lambda e, nbk=nbk: e.tensor_tensor(
                out=imp[:, 1:nbk], in0=imp[:, 1:nbk], in1=acc[:, 3:4 * (nbk - 1):4], op=ALU.add),
                reads=[imp, acc], writes=[imp])
            P.add("dve", lambda e, qt=qt: e.tensor_tensor(
                out=imp[:, 2 * qt - 1:2 * qt + 2], in0=imp[:, 2 * qt - 1:2 * qt + 2], in1=force_sb[:, :], op=ALU.add),
                reads=[imp, force_sb], writes=[imp])
            P.add("dve", lambda e: e.memset(imp[:, 0:1], 3e30), reads=[imp], writes=[imp])
            P.add("dve", lambda e, nbk=nbk: e.max(out=m8[:, 0:8], in_=imp[:, 0:nbk]), reads=[imp], writes=[m8])
            P.add("dve", lambda e, nbk=nbk: e.match_replace(out=work[:, 0:nbk], in_to_replace=m8[:, 0:8],
                                                            in_values=imp[:, 0:nbk], imm_value=-3e38),
                  reads=[imp, m8], writes=[work])
            P.add("dve", lambda e, nbk=nbk: e.max(out=m8[:, 8:16], in_=work[:, 0:nbk]), reads=[work], writes=[m8])
            P.add("dve", lambda e: e.tensor_reduce(out=thr[:, :], in_=m8[:, 8:16], axis=AX.X, op=ALU.min),
                  reads=[m8], writes=[thr])
            P.add("dve", lambda e, nbk=nbk: e.tensor_scalar(out=work[:, 0:nbk], in0=imp[:, 0:nbk], scalar1=thr[:, 0:1],
                                                            scalar2=None, op0=ALU.is_ge),
                  reads=[imp, thr], writes=[work])
            P.add("dve", lambda e, nbk=nbk: e.tensor_scalar(out=negm[:, 0:nbk], in0=work[:, 0:nbk], scalar1=-NEGV,
                                                            scalar2=NEGV, op0=ALU.mult, op1=ALU.add),
                  reads=[work], writes=[negm])
        k0 = max(0, qt - 4)
        tiles = []
        for kt in range(k0, qt + 1):
            masks = []
            if kt == qt:
                masks.append((causal[:, :], [causal]))
            if kt == qt - 4:
                masks.append((winneg[:, :], [winneg]))
            tiles.append((kt, kt, lambda masks=masks: masks))
        run_branch(tiles, KwT_sb, Q2, qb, Vw_sb, bOw)
        P.add("act", lambda e: e.activation(out=osb[:, 2, :], in_=bOw[:, 0:260], func=AF.Copy), reads=[bOw],
              writes=[osb])
        tiles = []
        for kt in range(0, qt + 1):
            def mask_fn(kt=kt, qt=qt, sel=sel):
                masks = []
                if sel:
                    nx = nmx[cnt["x"] % 4]
                    cnt["x"] += 1
                    P.add("pool", lambda e, nx=nx, kt=kt: e.tensor_copy(
                        out=nx[:, :, :], in_=negm[:, 2 * kt:2 * kt + 2].unsqueeze(2).to_broadcast([128, 2, 64])),
                        reads=[negm], writes=[nx])
                    masks.append((nx[:, :, :].rearrange("p b k -> p (b k)"), [nx]))
                if kt == qt:
                    masks.append((causal[:, :], [causal]))
                return masks
            tiles.append((kt, kt, mask_fn))
        run_branch(tiles, KsT_sb, Q2, qb, Vs_sb, bOs)
        P.add("act", lambda e: e.activation(out=osb[:, 1, :], in_=bOs[:, 0:260], func=AF.Copy), reads=[bOs],
              writes=[osb])
        o = ot[qt % 2]
        ov = osb[:, :, :].rearrange("p b (h c) -> p b h c", c=65)
        P.add("dve", lambda e, ov=ov: e.tensor_scalar(out=rden[:, :, :], in0=ov[:, :, :, 64], scalar1=1e-30,
                                                      scalar2=None, op0=ALU.add), reads=[osb], writes=[rden])
        P.add("dve", lambda e: e.reciprocal(out=rden[:, :, :], in_=rden[:, :, :]), reads=[rden], writes=[rden])
        P.add("dve", lambda e, qt=qt: e.tensor_tensor(
            out=coef[:, :, :], in0=gat_sb[:, qt, :].rearrange("p (h b) -> p h b", b=3),
            in1=rden[:, :, :].rearrange("p b h -> p h b"), op=ALU.mult), reads=[gat_sb, rden], writes=[coef])
        for h in range(4):
            for br in range(3):
                if br == 0:
                    P.add("dve", lambda e, o=o, h=h: e.tensor_scalar(
                        out=o[:, h, :], in0=osb[:, 0, h * 65:h * 65 + 64], scalar1=coef[:, h, 0:1], scalar2=None,
                        op0=ALU.mult), reads=[osb, coef], writes=[o])
                else:
                    P.add("dve", lambda e, o=o, h=h, br=br: e.scalar_tensor_tensor(
                        out=o[:, h, :], in0=osb[:, br, h * 65:h * 65 + 64], scalar=coef[:, h, br:br + 1],
                        in1=o[:, h, :], op0=ALU.mult, op1=ALU.add), reads=[osb, coef, o], writes=[o])
        P.dma("pool", out[qt * 128:(qt + 1) * 128, :], o[:, :, :].rearrange("p h d -> p (h d)"), reads=[o],
              dma_buf=o)
    P.finalize()
    return nc


def run_nsa_attn(qkv, gates, w1k, w2k, pek, w1v, w2v, pev, B, S):
    NT = S // 128
    nc = build_nsa_attn(S)
    consts = nsa_consts()

    def w1l(w):
        return np.ascontiguousarray(w.reshape(32, 64, 256).transpose(1, 0, 2).reshape(64, 8192))

    def w2l(w):
        return np.ascontiguousarray(w.reshape(2, 128, 64).transpose(1, 0, 2).reshape(128, 128))

    shared = dict(consts)
    shared.update({"w1k": w1l(w1k), "w1v": w1l(w1v), "w2k": w2l(w2k), "w2v": w2l(w2v),
                   "pekT": np.ascontiguousarray(pek.T), "pevT": np.ascontiguousarray(pev.T)})
    in_maps = []
    for c in range(NCORES):
        b, g = c // 4, c % 4
        blk = qkv[b * S:(b + 1) * S]
        q = blk[:, 0:1024].reshape(S, 16, 64)[:, 4 * g:4 * g + 4, :]
        kv = [blk[:, 1024 + i * 256 + g * 64: 1024 + i * 256 + (g + 1) * 64] for i in range(6)]

        def v1(v):
            o = np.ones((128, NT, 65), dtype=qkv.dtype)
            o[:, :, :64] = v.reshape(NT, 128, 64).transpose(1, 0, 2)
            return o

        gt = gates[b * S:(b + 1) * S].reshape(S, 4, 12)[:, g, :]
        m = {"QT": np.ascontiguousarray(q.transpose(2, 1, 0)),
             "KcT": np.ascontiguousarray(kv[0].T), "VcT": np.ascontiguousarray(kv[1].T),
             "KsT": np.ascontiguousarray(kv[2].T), "Vs1": v1(kv[3]),
             "KwT": np.ascontiguousarray(kv[4].T), "Vw1": v1(kv[5]),
             "gat": np.ascontiguousarray(gt.reshape(NT, 128, 12).transpose(1, 0, 2))}
        m.update(shared)
        in_maps.append(m)
    res = run_bass_kernel_spmd(nc, in_maps, core_ids=list(range(NCORES)))
    o = np.zeros((B * S, 1024), np.float32)
    for c in range(NCORES):
        b, g = c // 4, c % 4
        o[b * S:(b + 1) * S, g * 256:(g + 1) * 256] = res.results[c]["o"]
    return o


CH = 64
SDT = F32
E05 = float(np.exp(-0.5))
RW_OUT_BF = ["aT", "rT", "bT", "kT", "BhT", "KhT", "vbT"]
RW_OUT_F = ["vT", "gT", "bonT"]


def build_rwkv_pre(T, has_vres, TT=128):
    nc = bass.Bass("TRN2", target_bir_lowering=False)
    din = lambda name, shape, dt=F32: nc.dram_tensor(name, shape, dt, kind="ExternalInput").ap()
    xT = din("xT", [D, T + 1])
    g = din("g", [128, 8])
    prm = din("prm", [128, 12 * 8])
    wrkv = [din(f"w{n}", [D, D]) for n in "rkv"]
    w1 = din("w1", [D, 64]); w2 = din("w2", [64, D])
    a1 = din("a1", [D, 64]); a2 = din("a2", [64, D])
    g1 = din("g1", [D, 160]); g2 = din("g2", [160, D])
    if has_vres:
        v1 = din("v1", [D, 32]); v2 = din("v2", [32, D])
        vfT = din("vfT", [D, T])
    bones = din("bones", [128, 128])
    rmask = din("rmask", [128, TT])
    outs = {n: nc.dram_tensor(n, [D, T], SDT, kind="ExternalOutput").ap() for n in RW_OUT_BF}
    outs.update({n: nc.dram_tensor(n, [D, T], F32, kind="ExternalOutput").ap() for n in RW_OUT_F})
    gC = nc.dram_tensor("gC", [D, T // CH], F32, kind="ExternalOutput").ap()
    C = Ctx(nc)
    P = C.P
    banks = C.banks
    v3 = lambda ap: ap.rearrange("(c p) t -> p c t", p=128)
    xv = v3(xT)
    ov = {n: v3(a) for n, a in outs.items()}
    gCv = v3(gC)

    def small(name, src, shape):
        b = P.sb(name, shape, F32)
        P.dma("sp", b[tuple(slice(None) for _ in shape)], src, writes=[b], dma_buf=b)
        return b

    g_sb = small("g", g, [128, 8])
    prm_sb = small("prm", prm, [128, 96])
    bones_sb = small("bones", bones, [128, 128])
    rmask_sb = small("rmask", rmask, [128, TT])
    pr = lambda i, c: prm_sb[:, i * 8 + c: i * 8 + c + 1]
    eps_t = P.sb("eps", [128, 1], F32)
    P.add("pool", lambda e: e.memset(eps_t[:, :], 1e-5), writes=[eps_t])
    Wr, Wk, Wv = [C.load_weight(f"W{n}_", w, D, D) for n, w in zip("rkv", wrkv)]
    W1 = C.load_weight("w1_", w1, D, 64)
    A1 = C.load_weight("a1_", a1, D, 64)
    G1 = C.load_weight("g1_", g1, D, 160)

    def load_rows(name, src, r0, nr):
        b = P.sb(name, [nr, D], BF16)
        C.load_cast(b, b[:, :], src[r0:r0 + nr, :], D, parts=nr)
        return b

    W2 = load_rows("w2_", w2, 0, 64)
    A2 = load_rows("a2_", a2, 0, 64)
    G2a = load_rows("g2a_", g2, 0, 128)
    G2b = load_rows("g2b_", g2, 128, 32)
    if has_vres:
        V1 = C.load_weight("v1_", v1, D, 32)
        V2 = load_rows("v2_", v2, 0, 32)
        vfv = v3(vfT)

    TH = TT + 1
    xb = [P.sb(f"x{i}", [128, 8, TH], F32) for i in range(2)]
    sq = P.sb("sq", [128, 8, TH], F32)
    rstd = P.sb("rstd", [128, TH], F32)
    hn = P.sb("hn", [128, 8, TH], F32)
    dx = P.sb("dx", [128, 8, TT], F32)
    xm = [P.sb(f"xm{i}", [128, 8, TT], BF16) for i in range(6)]
    lw = P.sb("lw", [64, TT], BF16)
    la = P.sb("la", [64, TT], BF16)
    lg = [P.sb("lga", [128, TT], BF16), P.sb("lgb", [32, TT], BF16)]
    lv = P.sb("lv", [32, TT], BF16)
    F = lambda name: P.sb(name, [128, TT], F32)
    r_t, k_t, v_t, a_t, dl_t, cum_t, kk_t, kh_t = [F(n) for n in ("r", "k", "v", "a", "dl", "cum", "kk", "kh")]
    t1, t2, t3, t4 = [F(n) for n in ("t1", "t2", "t3", "t4")]
    vf_t = F("vf")
    NO = len(RW_OUT_BF)
    obf = {n: [P.sb(f"o_{n}{i}", [128, 8, TT], SDT) for i in range(1)] for n in RW_OUT_BF}
    of32 = {n: [P.sb(f"o_{n}{i}", [128, 8, TT], F32) for i in range(1)] for n in RW_OUT_F}
    ogc = P.sb("ogc", [128, 8, TT // CH], F32)
    nt = T // TT
    nbk = [0]

    def proj(Wl, xin, fc, K=8):
        bank = banks[nbk[0] % 6]
        nbk[0] += 1
        for kc in range(K):
            P.mm(bank[:, :TT], Wl[kc][:, fc * 128:(fc + 1) * 128], xin[:, kc, :], start=(kc == 0), stop=(kc == K - 1),
                 reads=[Wl[kc], xin], writes=[bank])
        return bank

    def lora_down(Wl, xin, n):
        bank = banks[nbk[0] % 6]
        nbk[0] += 1
        for kc in range(8):
            P.mm(bank[0:n, :TT], Wl[kc][:, 0:n], xin[:, kc, :], start=(kc == 0), stop=(kc == 7),
                 reads=[Wl[kc], xin], writes=[bank])
        return bank

    def lora_up(parts, fc):
        bank = banks[nbk[0] % 6]
        nbk[0] += 1
        for i, (Wb, hb, n) in enumerate(parts):
            P.mm(bank[:, :TT], Wb[0:n, fc * 128:(fc + 1) * 128], hb[0:n, :], start=(i == 0),
                 stop=(i == len(parts) - 1), reads=[Wb, hb], writes=[bank])
        return bank

    for tt in range(nt):
        xt = xb[tt % 2]
        P.dma("sp", xt[:, :, :], xv[:, :, tt * TT: tt * TT + TH], writes=[xt], dma_buf=xt)
        P.add("act", lambda e, xt=xt: e.activation(out=sq[:, :, :], in_=xt[:, :, :], func=AF.Square),
              reads=[xt], writes=[sq])
        bk = banks[7]
        for c in range(8):
            P.mm(bk[:, :TH], C.ones32[:, :], sq[:, c, :], start=(c == 0), stop=(c == 7), reads=[C.ones32, sq],
                 writes=[bk])
        P.add("act", lambda e: e.activation(out=rstd[:, :], in_=bk[:, :TH], func=AF.Sqrt, bias=eps_t[:, 0:1],
                                            scale=1.0 / D), reads=[bk, eps_t], writes=[rstd])
        P.add("dve", lambda e: e.reciprocal(out=rstd[:, :], in_=rstd[:, :]), reads=[rstd], writes=[rstd])
        for c in range(8):
            P.add("dve", lambda e, c=c, xt=xt: e.scalar_tensor_tensor(
                out=hn[:, c, :], in0=xt[:, c, :], scalar=g_sb[:, c:c + 1], in1=rstd[:, :], op0=ALU.mult,
                op1=ALU.mult), reads=[xt, g_sb, rstd], writes=[hn])
        P.add("pool", lambda e: e.tensor_tensor(out=dx[:, :, :], in0=hn[:, :, 0:TT], in1=hn[:, :, 1:TH],
                                                op=ALU.subtract), reads=[hn], writes=[dx])
        for i in range(6):
            for c in range(8):
                P.add("dve", lambda e, i=i, c=c: e.scalar_tensor_tensor(
                    out=xm[i][:, c, :], in0=dx[:, c, :], scalar=pr(i, c), in1=hn[:, c, 1:TH], op0=ALU.mult,
                    op1=ALU.add), reads=[dx, hn, prm_sb], writes=[xm[i]])
        xr, xw, xk, xvv, xa, xg = xm
        bw = lora_down(W1, xw, 64)
        P.add("act", lambda e, bw=bw: e.activation(out=lw[:, :], in_=bw[0:64, :TT], func=AF.Tanh), reads=[bw],
              writes=[lw])
        ba = lora_down(A1, xa, 64)
        P.add("act", lambda e, ba=ba: e.activation(out=la[:, :], in_=ba[0:64, :TT], func=AF.Copy), reads=[ba],
              writes=[la])
        bg = lora_down(G1, xg, 128)
        P.add("act", lambda e, bg=bg: e.activation(out=lg[0][:, :], in_=bg[:, :TT], func=AF.Sigmoid), reads=[bg],
              writes=[lg[0]])
        bg2 = banks[nbk[0] % 6]
        nbk[0] += 1
        for kc in range(8):
            P.mm(bg2[0:32, :TT], G1[kc][:, 128:160], xg[:, kc, :], start=(kc == 0), stop=(kc == 7),
                 reads=[G1[kc], xg], writes=[bg2])
        P.add("act", lambda e, bg2=bg2: e.activation(out=lg[1][:, :], in_=bg2[0:32, :TT], func=AF.Sigmoid),
              reads=[bg2], writes=[lg[1]])
        if has_vres:
            bv = lora_down(V1, xvv, 32)
            P.add("act", lambda e, bv=bv: e.activation(out=lv[:, :], in_=bv[0:32, :TT], func=AF.Copy), reads=[bv],
                  writes=[lv])
        for fc in range(8):
            sl = slice(tt * TT, (tt + 1) * TT)
            b = proj(Wr, xr, fc)
            P.add("act", lambda e, b=b: e.activation(out=r_t[:, :], in_=b[:, :TT], func=AF.Copy), reads=[b],
                  writes=[r_t])
            b = proj(Wk, xk, fc)
            P.add("act", lambda e, b=b: e.activation(out=k_t[:, :], in_=b[:, :TT], func=AF.Copy), reads=[b],
                  writes=[k_t])
            b = proj(Wv, xvv, fc)
            P.add("act", lambda e, b=b: e.activation(out=v_t[:, :], in_=b[:, :TT], func=AF.Copy), reads=[b],
                  writes=[v_t])
            b = lora_up([(W2, lw, 64)], fc)
            P.add("act", lambda e, b=b, fc=fc: e.activation(out=dl_t[:, :], in_=b[:, :TT], func=AF.Sigmoid,
                                                            bias=pr(6, fc)), reads=[b, prm_sb], writes=[dl_t])
            P.add("pool", lambda e: e.tensor_scalar(out=dl_t[:, :], in0=dl_t[:, :], scalar1=-E05, scalar2=None,
                                                    op0=ALU.mult), reads=[dl_t], writes=[dl_t])
            b = lora_up([(A2, la, 64)], fc)
            P.add("act", lambda e, b=b, fc=fc: e.activation(out=a_t[:, :], in_=b[:, :TT], func=AF.Sigmoid,
                                                            bias=pr(7, fc)), reads=[b, prm_sb], writes=[a_t])
            b = lora_up([(G2a, lg[0], 128), (G2b, lg[1], 32)], fc)
            og = of32["gT"][0]
            P.add("act", lambda e, b=b, fc=fc, og=og: e.activation(out=og[:, fc, :], in_=b[:, :TT], func=AF.Copy),
                  reads=[b], writes=[og])
            if has_vres:
                b = lora_up([(V2, lv, 32)], fc)
                P.add("act", lambda e, b=b, fc=fc: e.activation(out=t1[:, :], in_=b[:, :TT], func=AF.Sigmoid,
                                                                bias=pr(11, fc)), reads=[b, prm_sb], writes=[t1])
                P.dma("sp", vf_t[:, :], vfv[:, fc, sl], writes=[vf_t], dma_buf=vf_t)
                P.add("pool", lambda e: e.tensor_tensor(out=vf_t[:, :], in0=vf_t[:, :], in1=v_t[:, :],
                                                        op=ALU.subtract), reads=[vf_t, v_t], writes=[vf_t])
                P.add("pool", lambda e: e.tensor_tensor(out=vf_t[:, :], in0=vf_t[:, :], in1=t1[:, :], op=ALU.mult),
                      reads=[vf_t, t1], writes=[vf_t])
                P.add("pool", lambda e: e.tensor_tensor(out=v_t[:, :], in0=v_t[:, :], in1=vf_t[:, :], op=ALU.add),
                      reads=[vf_t, v_t], writes=[v_t])
            ovf = of32["vT"][0]
            ovb = obf["vbT"][0]
            P.add("pool", lambda e, fc=fc, ovf=ovf: e.tensor_copy(out=ovf[:, fc, :], in_=v_t[:, :]), reads=[v_t],
                  writes=[ovf])
            P.add("pool", lambda e, fc=fc, ovb=ovb: e.tensor_copy(out=ovb[:, fc, :], in_=v_t[:, :]), reads=[v_t],
                  writes=[ovb])
            P.add("dve", lambda e, fc=fc: e.tensor_scalar(out=kk_t[:, :], in0=k_t[:, :], scalar1=pr(8, fc),
                                                          scalar2=None, op0=ALU.mult), reads=[k_t, prm_sb],
                  writes=[kk_t])
            P.add("dve", lambda e: e.tensor_tensor(out=t2[:, :], in0=kk_t[:, :], in1=kk_t[:, :], op=ALU.mult),
                  reads=[kk_t], writes=[t2])
            bn = banks[6]
            P.mm(bn[:, :TT], bones_sb[:, :], t2[:, :], start=True, stop=True, reads=[bones_sb, t2], writes=[bn])
            P.add("act", lambda e, bn=bn: e.activation(out=t2[:, :], in_=bn[:, :TT], func=AF.Sqrt), reads=[bn],
                  writes=[t2])
            P.add("dve", lambda e: e.tensor_scalar(out=t2[:, :], in0=t2[:, :], scalar1=1e-12, scalar2=None,
                                                   op0=ALU.max), reads=[t2], writes=[t2])
            P.add("dve", lambda e: e.reciprocal(out=t2[:, :], in_=t2[:, :]), reads=[t2], writes=[t2])
            P.add("dve", lambda e: e.tensor_tensor(out=kk_t[:, :], in0=kk_t[:, :], in1=t2[:, :], op=ALU.mult),
                  reads=[kk_t, t2], writes=[kk_t])
            P.add("dve", lambda e, fc=fc: e.tensor_scalar(out=kh_t[:, :], in0=a_t[:, :], scalar1=-1.0,
                                                          scalar2=pr(9, fc), op0=ALU.add, op1=ALU.mult),
                  reads=[a_t, prm_sb], writes=[kh_t])
            P.add("dve", lambda e: e.scalar_tensor_tensor(out=kh_t[:, :], in0=kh_t[:, :], scalar=1.0, in1=k_t[:, :],
                                                          op0=ALU.add, op1=ALU.mult), reads=[kh_t, k_t],
                  writes=[kh_t])
            P.add("dve", lambda e, fc=fc: e.scalar_tensor_tensor(out=t3[:, :], in0=r_t[:, :], scalar=pr(10, fc),
                                                                 in1=kh_t[:, :], op0=ALU.mult, op1=ALU.mult),
                  reads=[r_t, kh_t, prm_sb], writes=[t3])
            bn2 = banks[7]
            P.mm(bn2[:, :TT], bones_sb[:, :], t3[:, :], start=True, stop=True, reads=[bones_sb, t3], writes=[bn2])
            ob = of32["bonT"][0]
            P.add("dve", lambda e, fc=fc, ob=ob, bn2=bn2: e.tensor_tensor(out=ob[:, fc, :], in0=bn2[:, :TT],
                                                                         in1=v_t[:, :], op=ALU.mult),
                  reads=[bn2, v_t], writes=[ob])
            P.add("dve", lambda e: e.tensor_tensor_scan(out=cum_t[:, :], data0=rmask_sb[:, :], data1=dl_t[:, :],
                                                        initial=0.0, op0=ALU.mult, op1=ALU.add),
                  reads=[rmask_sb, dl_t], writes=[cum_t])
            P.add("act", lambda e: e.activation(out=t1[:, :], in_=cum_t[:, :], func=AF.Exp, scale=-1.0),
                  reads=[cum_t], writes=[t1])
            P.add("act", lambda e: e.activation(out=t2[:, :], in_=cum_t[:, :], func=AF.Exp), reads=[cum_t],
                  writes=[t2])
            P.add("pool", lambda e: e.tensor_tensor(out=t4[:, :], in0=cum_t[:, :], in1=dl_t[:, :], op=ALU.subtract),
                  reads=[cum_t, dl_t], writes=[t4])
            P.add("act", lambda e: e.activation(out=t4[:, :], in_=t4[:, :], func=AF.Exp), reads=[t4], writes=[t4])
            o = obf["aT"][0]
            P.add("dve", lambda e, fc=fc, o=o: e.scalar_tensor_tensor(out=o[:, fc, :], in0=kk_t[:, :], scalar=-1.0,
                                                                     in1=t4[:, :], op0=ALU.mult, op1=ALU.mult),
                  reads=[kk_t, t4], writes=[o])
            o = obf["rT"][0]
            P.add("pool", lambda e, fc=fc, o=o: e.tensor_tensor(out=o[:, fc, :], in0=r_t[:, :], in1=t2[:, :],
                                                               op=ALU.mult), reads=[r_t, t2], writes=[o])
            P.add("dve", lambda e: e.tensor_tensor(out=t3[:, :], in0=kk_t[:, :], in1=a_t[:, :], op=ALU.mult),
                  reads=[kk_t, a_t], writes=[t3])
            P.add("dve", lambda e: e.tensor_tensor(out=t3[:, :], in0=t3[:, :], in1=t1[:, :], op=ALU.mult),
                  reads=[t3, t1], writes=[t3])
            P.add("pool", lambda e: e.tensor_tensor(out=kh_t[:, :], in0=kh_t[:, :], in1=t1[:, :], op=ALU.mult),
                  reads=[kh_t, t1], writes=[kh_t])
            o = obf["bT"][0]
            P.add("pool", lambda e, fc=fc, o=o: e.tensor_copy(out=o[:, fc, :], in_=t3[:, :]), reads=[t3], writes=[o])
            o = obf["kT"][0]
            P.add("pool", lambda e, fc=fc, o=o: e.tensor_copy(out=o[:, fc, :], in_=kh_t[:, :]), reads=[kh_t],
                  writes=[o])
            gcv = t2[:, :].rearrange("p (n c) -> p n c", c=CH)[:, :, CH - 1:CH]
            P.add("pool", lambda e, fc=fc, gcv=gcv: e.tensor_copy(out=ogc[:, fc, :].unsqueeze(2), in_=gcv),
                  reads=[t2], writes=[ogc])
            gcb = gcv.to_broadcast([128, TT // CH, CH])
            o = obf["BhT"][0]
            P.add("dve", lambda e, fc=fc, o=o, gcb=gcb: e.tensor_tensor(
                out=o[:, fc, :].rearrange("p (n c) -> p n c", c=CH), in0=t3[:, :].rearrange("p (n c) -> p n c", c=CH),
                in1=gcb, op=ALU.mult), reads=[t3, t2], writes=[o])
            o = obf["KhT"][0]
            P.add("dve", lambda e, fc=fc, o=o, gcb=gcb: e.tensor_tensor(
                out=o[:, fc, :].rearrange("p (n c) -> p n c", c=CH), in0=kh_t[:, :].rearrange("p (n c) -> p n c", c=CH),
                in1=gcb, op=ALU.mult), reads=[kh_t, t2], writes=[o])
        sl = slice(tt * TT, (tt + 1) * TT)
        for n in RW_OUT_BF:
            P.dma("pool", ov[n][:, :, sl], obf[n][0][:, :, :], reads=[obf[n][0]], dma_buf=obf[n][0])
        for n in RW_OUT_F:
            P.dma("pool", ov[n][:, :, sl], of32[n][0][:, :, :], reads=[of32[n][0]], dma_buf=of32[n][0])
        P.dma("pool", gCv[:, :, tt * (TT // CH):(tt + 1) * (TT // CH)], ogc[:, :, :], reads=[ogc], dma_buf=ogc)
    P.finalize()
    return nc


GN_EPS = 64e-5


def scan_consts():
    s = np.arange(128)[:, None]
    t = np.arange(128)[None, :]
    same = (s // CH) == (t // CH)
    return {"mstrict": (same & (s < t)).astype(np.float32), "mincl": (same & (s <= t)).astype(np.float32),
            "mstrictT": (same & (t < s)).astype(np.float32), "identf": np.eye(128, dtype=np.float32)}


def build_rwkv_scan(S):
    NW = S // 128
    NCH = 128 // CH
    L = int(np.log2(CH))
    nc = bass.Bass("TRN2", target_bir_lowering=False)
    din = lambda name, shape, dt=F32: nc.dram_tensor(name, shape, dt, kind="ExternalInput").ap()
    fm = din("fm", [4, 64, NW, 512], SDT)
    tk = din("tk", [4, 128, NW, 256], SDT)
    gC = din("gC", [4, 64, S // CH])
    lnw = din("lnw", [4, 64, 64])
    lnb = din("lnb", [4, 64, 64])
    c_ms = din("mstrict", [128, 128]); c_mi = din("mincl", [128, 128]); c_mt = din("mstrictT", [128, 128])
    c_id = din("identf", [128, 128])
    yout = nc.dram_tensor("yn", [4, S, 64], F32, kind="ExternalOutput").ap()
    C = Ctx(nc)
    P = C.P
    banks = C.banks

    def small(name, src, shape):
        b = P.sb(name, shape, F32)
        P.dma("sp", b[tuple(slice(None) for _ in shape)], src, writes=[b], dma_buf=b)
        return b

    ms = small("ms", c_ms, [128, 128]); mi = small("mi", c_mi, [128, 128]); mt = small("mt", c_mt, [128, 128])
    idf = small("idf", c_id, [128, 128])
    gC_sb = [small(f"gC{h}", gC[h], [64, S // CH]) for h in range(4)]
    lnw_sb = [small(f"lnw{h}", lnw[h], [64, 64]) for h in range(4)]
    lnb_sb = [small(f"lnb{h}", lnb[h], [64, 64]) for h in range(4)]
    NBUF = 3
    fmb = [[P.sb(f"fm{h}_{i}", [64, 512], SDT) for i in range(NBUF)] for h in range(4)]
    tkb = [[P.sb(f"tk{h}_{i}", [128, 256], SDT) for i in range(NBUF)] for h in range(4)]

    def per_head(name, shape, dt, n=2):
        return [[P.sb(f"{name}{h}_{i}", shape, dt) for i in range(n)] for h in range(4)]

    Abr = per_head("Abr", [128, 128], SDT)
    Aak = per_head("Aak", [128, 128], SDT)
    Akr = per_head("Akr", [128, 128], SDT)
    Xn = per_head("Xn", [128, 128], SDT)
    Xt = per_head("Xt", [128, 128], SDT)
    Pf = per_head("Pf", [128, 128], F32, 1)
    Pb = per_head("Pb", [128, 128], SDT)
    axb = per_head("axb", [128, 128], SDT)
    wv = per_head("wv", [128, 128], SDT)
    qeff = per_head("qeff", [64, 128], F32)
    Tc = per_head("Tc", [64, 64], F32)
    ST = per_head("ST", [64, 64], F32)
    yc = per_head("yc", [64, 64], F32)
    ysq = per_head("ysq", [64, 64], F32, 1)
    st = per_head("st", [64, 4], F32)
    yo = per_head("yo", [64, 64], F32)
    for h in range(4):
        P.add("pool", lambda e, h=h: e.memset(ST[h][0][:, :], 0.0), writes=[ST[h][0]])
    nb = [0]

    def bank():
        b = banks[nb[0] % 8]
        nb[0] += 1
        return b

    eng_rr = [0]

    def ev():
        e = ["dve", "pool"][eng_rr[0] % 2]
        eng_rr[0] += 1
        return e

    nstate = [0, 0, 0, 0]
    def load_win(w):
        for h in range(4):
            f = fmb[h][w % NBUF]
            t = tkb[h][w % NBUF]
            P.dma("sp", f[:, :], fm[h, :, w, :], writes=[f], dma_buf=f)
            P.dma("sp", t[:, :], tk[h, :, w, :], writes=[t], dma_buf=t)

    def make_gen(w):
        i2 = w % 2

        def head_gen(h, w=w, i2=i2):
            f = fmb[h][w % NBUF]
            t = tkb[h][w % NBUF]
            aT, rT, bT, kT = f[:, 0:128], f[:, 128:256], f[:, 256:384], f[:, 384:512]
            a_tok, Bh, Kh, v_tok = t[:, 0:64], t[:, 64:128], t[:, 128:192], t[:, 192:256]
            abr, aak, akr = Abr[h][i2], Aak[h][i2], Akr[h][i2]
            b1 = bank()
            P.mm(b1[:, 0:256], bT, f[:, 0:256], start=True, stop=True, reads=[f], writes=[b1])
            xn, xt = Xn[h][0], Xt[h][0]
            P.add("dve", lambda e, b1=b1, xn=xn: e.tensor_tensor(out=xn[:, :], in0=b1[:, 0:128], in1=ms[:, :],
                                                                op=ALU.mult), reads=[b1, ms], writes=[xn])
            P.add("dve", lambda e, b1=b1, abr=abr: e.tensor_tensor(out=abr[:, :], in0=b1[:, 128:256], in1=mi[:, :],
                                                                  op=ALU.mult), reads=[b1, mi], writes=[abr])
            pf, pb = Pf[h][0], Pb[h][0]
            P.add("dve", lambda e, b1=b1, pf=pf: e.tensor_tensor(out=pf[:, :], in0=b1[:, 0:128], in1=ms[:, :],
                                                                op=ALU.mult), reads=[b1, ms], writes=[pf])
            P.add("pool", lambda e, pf=pf: e.tensor_tensor(out=pf[:, :], in0=pf[:, :], in1=idf[:, :], op=ALU.add),
                  reads=[pf, idf], writes=[pf])
            P.add("pool", lambda e, pf=pf, pb=pb: e.tensor_copy(out=pb[:, :], in_=pf[:, :]), reads=[pf], writes=[pb])
            yield
            b2 = bank()
            P.mm(b2[:, 0:256], kT, f[:, 0:256], start=True, stop=True, reads=[f], writes=[b2])
            P.add("dve", lambda e, b2=b2, aak=aak: e.tensor_tensor(out=aak[:, :], in0=b2[:, 0:128], in1=ms[:, :],
                                                                  op=ALU.mult), reads=[b2, ms], writes=[aak])
            P.add("dve", lambda e, b2=b2, akr=akr: e.tensor_tensor(out=akr[:, :], in0=b2[:, 128:256], in1=mi[:, :],
                                                                  op=ALU.mult), reads=[b2, mi], writes=[akr])
            yield
            b3 = bank()
            P.mm(b3[:, 0:128], aT, bT, start=True, stop=True, reads=[f], writes=[b3])
            P.add("dve", lambda e, b3=b3, xt=xt: e.tensor_tensor(out=xt[:, :], in0=b3[:, 0:128], in1=mt[:, :],
                                                                op=ALU.mult), reads=[b3, mt], writes=[xt])
            yield
            cur_n, cur_t = xn, xt
            for k in range(1, L):
                nxt_n, nxt_t = Xn[h][k % 2], Xt[h][k % 2]
                bt_ = bank()
                P.mm(bt_[:, 0:128], cur_n[:, :], cur_t[:, :], start=True, stop=True, reads=[cur_n, cur_t],
                     writes=[bt_])
                if k < L - 1:
                    bn_ = bank()
                    P.mm(bn_[:, 0:128], cur_t[:, :], cur_n[:, :], start=True, stop=True, reads=[cur_n, cur_t],
                         writes=[bn_])
                P.add("act", lambda e, bt_=bt_, nxt_t=nxt_t: e.activation(out=nxt_t[:, :], in_=bt_[:, 0:128],
                                                                         func=AF.Copy), reads=[bt_], writes=[nxt_t])
                if k < L - 1:
                    P.add("act", lambda e, bn_=bn_, nxt_n=nxt_n: e.activation(out=nxt_n[:, :], in_=bn_[:, 0:128],
                                                                             func=AF.Copy), reads=[bn_],
                          writes=[nxt_n])
                yield
                bp = bank()
                P.mm(bp[:, 0:128], nxt_t[:, :], pb[:, :], start=True, stop=True, reads=[nxt_t, pb], writes=[bp])
                P.add("dve", lambda e, bp=bp, pf=pf: e.tensor_tensor(out=pf[:, :], in0=pf[:, :], in1=bp[:, 0:128],
                                                                    op=ALU.add), reads=[pf, bp], writes=[pf])
                pb = Pb[h][k % 2]
                P.add("pool", lambda e, pf=pf, pb=pb: e.tensor_copy(out=pb[:, :], in_=pf[:, :]), reads=[pf],
                      writes=[pb])
                cur_n, cur_t = nxt_n, nxt_t
                yield
            tinv = pb
            ax = axb[h][i2]
            bx = bank()
            P.mm(bx[:, 0:64], aak[:, :], v_tok, start=True, stop=True, reads=[aak, t], writes=[bx])
            P.add("pool", lambda e, ax=ax, a_tok=a_tok: e.tensor_copy(out=ax[:, 0:64], in_=a_tok), reads=[t],
                  writes=[ax])
            P.add("act", lambda e, ax=ax, bx=bx: e.activation(out=ax[:, 64:128], in_=bx[:, 0:64], func=AF.Copy),
                  reads=[bx, ax], writes=[ax])
            yield
            wvb = wv[h][i2]
            bw = bank()
            P.mm(bw[:, 0:128], tinv[:, :], ax[:, :], start=True, stop=True, reads=[tinv, ax], writes=[bw])
            P.add("act", lambda e, wvb=wvb, bw=bw: e.activation(out=wvb[:, :], in_=bw[:, 0:128], func=AF.Copy),
                  reads=[bw], writes=[wvb])
            yield
            qe = qeff[h][i2]
            bq = bank()
            P.mm(bq[0:64, 0:128], wvb[:, 0:64], abr[:, :], start=True, stop=True, reads=[wvb, abr], writes=[bq])
            P.add("dve", lambda e, qe=qe, bq=bq, rT=rT: e.tensor_tensor(out=qe[:, :], in0=bq[0:64, 0:128], in1=rT,
                                                                       op=ALU.add), reads=[bq, f], writes=[qe])
            yield
            yield "PHASE"
            for c in range(NCH):
                ps = slice(c * CH, (c + 1) * CH)
                ci = nstate[h]
                nstate[h] += 1
                s_old, s_new = ST[h][ci % 2], ST[h][(ci + 1) % 2]
                tc = Tc[h][ci % 2]
                btc = bank()
                P.mm(btc[0:64, 0:64], wvb[ps, 0:64], Bh[ps, :], start=True, stop=True, reads=[wvb, t], writes=[btc])
                gidx = w * NCH + c
                P.add("dve", lambda e, tc=tc, btc=btc, h=h, gidx=gidx: e.scalar_tensor_tensor(
                    out=tc[:, :], in0=idf[0:64, 0:64], scalar=gC_sb[h][:, gidx:gidx + 1], in1=btc[0:64, 0:64],
                    op0=ALU.mult, op1=ALU.add), reads=[idf, gC_sb[h], btc], writes=[tc])
                yield
                by = bank()
                P.mm(by[0:64, 0:64], abr[ps, ps], wvb[ps, 64:128], start=True, stop=False, reads=[abr, wvb],
                     writes=[by])
                P.mm(by[0:64, 0:64], akr[ps, ps], v_tok[ps, :], start=False, stop=False, reads=[akr, t], writes=[by])
                bys = bank()
                P.mm(bys[0:64, 0:64], qe[:, ps], s_old[:, :], start=True, stop=True, reads=[qe, s_old], writes=[bys])
                y = yc[h][ci % 2]
                s4 = st[h][ci % 2]
                P.add("act", lambda e, y=y, by=by: e.activation(out=y[:, :], in_=by[0:64, 0:64], func=AF.Copy),
                      reads=[by], writes=[y])
                P.add("dve", lambda e, y=y, bys=bys: e.tensor_tensor(out=y[:, :], in0=y[:, :], in1=bys[0:64, 0:64],
                                                                    op=ALU.add), reads=[y, bys], writes=[y])
                yield
                bs = bank()
                P.mm(bs[0:64, 0:64], Bh[ps, :], wvb[ps, 64:128], start=True, stop=False, reads=[t, wvb], writes=[bs])
                P.mm(bs[0:64, 0:64], Kh[ps, :], v_tok[ps, :], start=False, stop=True, reads=[t], writes=[bs])
                bs2 = bank()
                P.mm(bs2[0:64, 0:64], tc[:, :], s_old[:, :], start=True, stop=True, reads=[tc, s_old], writes=[bs2])
                P.add("act", lambda e, s_new=s_new, bs=bs: e.activation(out=s_new[:, :], in_=bs[0:64, 0:64],
                                                                       func=AF.Copy), reads=[bs], writes=[s_new])
                P.add("dve", lambda e, s_new=s_new, bs2=bs2: e.tensor_tensor(
                    out=s_new[:, :], in0=s_new[:, :], in1=bs2[0:64, 0:64], op=ALU.add), reads=[s_new, bs2],
                    writes=[s_new])
                yield
                P.add("dve", lambda e, y=y, s4=s4: e.tensor_reduce(out=s4[:, 0:1], in_=y[:, :], axis=AX.X,
                                                                   op=ALU.add), reads=[y], writes=[s4])
                P.add("dve", lambda e, s4=s4: e.tensor_scalar(out=s4[:, 0:1], in0=s4[:, 0:1], scalar1=-1.0 / 64,
                                                              scalar2=None, op0=ALU.mult), reads=[s4], writes=[s4])
                P.add("act", lambda e, y=y, s4=s4: e.activation(out=y[:, :], in_=y[:, :], func=AF.Identity,
                                                                bias=s4[:, 0:1]), reads=[y, s4], writes=[y])
                yield
                sqb = ysq[h][0]
                P.add("pool", lambda e, y=y, sqb=sqb: e.tensor_tensor(out=sqb[:, :], in0=y[:, :], in1=y[:, :],
                                                                     op=ALU.mult), reads=[y], writes=[sqb])
                P.add("dve", lambda e, sqb=sqb, s4=s4: e.tensor_reduce(out=s4[:, 1:2], in_=sqb[:, :], axis=AX.X,
                                                                       op=ALU.add), reads=[sqb], writes=[s4])
                P.add("dve", lambda e, s4=s4: e.tensor_scalar(out=s4[:, 1:2], in0=s4[:, 1:2], scalar1=1.0 / 64,
                                                              scalar2=GN_EPS, op0=ALU.mult, op1=ALU.add),
                      reads=[s4], writes=[s4])
                P.add("act", lambda e, s4=s4: e.activation(out=s4[:, 2:3], in_=s4[:, 1:2], func=AF.Sqrt),
                      reads=[s4], writes=[s4])
                P.add("dve", lambda e, s4=s4: e.reciprocal(out=s4[:, 3:4], in_=s4[:, 2:3]), reads=[s4], writes=[s4])
                yield
                o = yo[h][ci % 2]
                P.add("dve", lambda e, o=o, y=y, s4=s4, h=h: e.scalar_tensor_tensor(
                    out=o[:, :], in0=y[:, :], scalar=s4[:, 3:4], in1=lnw_sb[h][:, :], op0=ALU.mult, op1=ALU.mult),
                    reads=[y, s4, lnw_sb[h]], writes=[o])
                P.add("pool", lambda e, o=o, h=h: e.tensor_tensor(out=o[:, :], in0=o[:, :], in1=lnb_sb[h][:, :],
                                                                  op=ALU.add), reads=[o, lnb_sb[h]], writes=[o])
                t0 = w * 128 + c * CH
                P.dma("pool", yout[h, t0:t0 + CH, :], o[:, :], reads=[o], dma_buf=o)
        return [head_gen(h) for h in range(4)]

    def step(g_):
        try:
            return next(g_)
        except StopIteration:
            return "END"

    load_win(0)
    if NW > 1:
        load_win(1)
    pre = make_gen(0)
    alive = list(pre)
    while alive:
        for g_ in list(alive):
            if step(g_) == "PHASE":
                alive.remove(g_)
    for w in range(NW):
        chain = pre
        if w + 2 < NW:
            load_win(w + 2)
        pre = make_gen(w + 1) if w + 1 < NW else []
        a_pre, a_chain = list(pre), list(chain)
        while a_pre or a_chain:
            for g_ in list(a_chain):
                if step(g_) == "END":
                    a_chain.remove(g_)
            for g_ in list(a_pre):
                if step(g_) == "PHASE":
                    a_pre.remove(g_)
    P.finalize()
    return nc


def lay8(v):
    return np.ascontiguousarray(v.reshape(8, 128).T)


def run_rwkv_pre(x_tok, S, g, x_mix, w_rkv, w0, w1, w2, a0, a1, a2, g1, g2, k_k, k_a, r_k, vres, vfT_full, TT=128):
    ntok = x_tok.shape[0]
    T = ntok // NCORES
    has_vres = vres is not None
    nc = build_rwkv_pre(T, has_vres, TT=TT)
    v0 = vres[0] if has_vres else np.zeros(D, np.float32)
    prm = np.concatenate([lay8(x_mix[i]) for i in range(6)] + [lay8(w0), lay8(a0), lay8(k_k), lay8(k_a), lay8(r_k),
                                                                 lay8(v0)], axis=1)
    bones = np.kron(np.eye(2, dtype=np.float32), np.ones((64, 64), np.float32))
    rmask = np.ones((128, TT), np.float32)
    rmask[:, ::CH] = 0.0
    in_maps = []
    for c in range(NCORES):
        xs = np.zeros((T + 1, D), np.float32)
        xs[1:] = x_tok[c * T:(c + 1) * T]
        if (c * T) % S != 0:
            xs[0] = x_tok[c * T - 1]
        m = {"xT": np.ascontiguousarray(xs.T), "g": lay8(g), "prm": np.ascontiguousarray(prm),
             "wr": w_rkv[0], "wk": w_rkv[1], "wv": w_rkv[2], "w1": w1, "w2": w2, "a1": a1, "a2": a2, "g1": g1,
             "g2": g2, "bones": bones, "rmask": rmask}
        if has_vres:
            m.update({"v1": vres[1], "v2": vres[2], "vfT": np.ascontiguousarray(vfT_full[:, c * T:(c + 1) * T])})
        in_maps.append(m)
    res = run_bass_kernel_spmd(nc, in_maps, core_ids=list(range(NCORES)))
    out = {}
    for n in RW_OUT_BF + RW_OUT_F + ["gC"]:
        out[n] = np.concatenate([r[n] for r in res.results], axis=1)
    return out


def run_rwkv_scan(pre, ln_w, ln_b, B, S):
    NW = S // 128
    nc = build_rwkv_scan(S)
    consts = scan_consts()
    in_maps = []
    for c in range(NCORES):
        b, hg = c // 4, c % 4
        fm = np.zeros((4, 64, NW, 512), dtype=pre["aT"].dtype)
        tk = np.zeros((4, 128, NW, 256), dtype=pre["aT"].dtype)
        gC = np.zeros((4, 64, S // CH), np.float32)
        lnw = np.zeros((4, 64, 64), np.float32)
        lnb = np.zeros((4, 64, 64), np.float32)
        for hh in range(4):
            ch = slice((4 * hg + hh) * 64, (4 * hg + hh + 1) * 64)
            ts = slice(b * S, (b + 1) * S)
            pc = {n: pre[n][ch, ts].reshape(64, NW, 128) for n in RW_OUT_BF}
            for i, n in enumerate(["aT", "rT", "bT", "kT"]):
                fm[hh, :, :, i * 128:(i + 1) * 128] = pc[n]
            for i, n in enumerate(["aT", "BhT", "KhT", "vbT"]):
                tk[hh, :, :, i * 64:(i + 1) * 64] = pc[n].transpose(2, 1, 0)
            gC[hh] = pre["gC"][ch, b * (S // CH):(b + 1) * (S // CH)]
            lnw[hh] = np.broadcast_to(ln_w[ch][None, :], (64, 64))
            lnb[hh] = np.broadcast_to(ln_b[ch][None, :], (64, 64))
        m = {"fm": fm, "tk": tk, "gC": gC, "lnw": lnw, "lnb": lnb}
        m.update(consts)
        in_maps.append(m)
    res = run_bass_kernel_spmd(nc, in_maps, core_ids=list(range(NCORES)))
    ynT = np.zeros((D, B * S), np.float32)
    for c in range(NCORES):
        b, hg = c // 4, c % 4
        y = res.results[c]["yn"]
        for hh in range(4):
            ynT[(4 * hg + hh) * 64:(4 * hg + hh + 1) * 64, b * S:(b + 1) * S] = y[hh].T
    return ynT


def run_linres(xT_full, w, ins):
    ntok = xT_full.shape[1]
    T = ntok // NCORES
    nc = build_linres(T, n_in=len(ins))
    names = ["aT", "bT", "cT"]
    in_maps = []
    for c in range(NCORES):
        sl = slice(c * T, (c + 1) * T)
        m = {"xT": np.ascontiguousarray(xT_full[:, sl]), "w": w}
        for n, a in zip(names, ins):
            m[n] = np.ascontiguousarray(a[:, sl].astype(np.float32))
        in_maps.append(m)
    res = run_bass_kernel_spmd(nc, in_maps, core_ids=list(range(NCORES)))
    return np.concatenate([r["yT"] for r in res.results], axis=1)


def kernel(**inp):
    inp = {k: np.asarray(v) for k, v in inp.items()}
    x = inp["x"]
    B, S, _ = x.shape
    ntok = B * S
    xT = np.ascontiguousarray(x.reshape(ntok, D).T)
    vfT = None
    for i in range(4):
        j = i // 2
        if i % 2 == 0:
            qkv, gates = run_nsa_inproj(np.ascontiguousarray(xT.T), inp["norm_mix"][i], inp["nsa_w_in"][j], S)
            o = run_nsa_attn(qkv, gates, inp["nsa_cmp_w1_k"][j], inp["nsa_cmp_w2_k"][j], inp["nsa_cmp_pe_k"][j],
                             inp["nsa_cmp_w1_v"][j], inp["nsa_cmp_w2_v"][j], inp["nsa_cmp_pe_v"][j], B, S)
            xT = run_linres(xT, inp["nsa_w_out"][j], [np.ascontiguousarray(o.T)])
        else:
            vres = None if j == 0 else (inp["rwkv_v0"][j - 1], inp["rwkv_v1"][j - 1], inp["rwkv_v2"][j - 1])
            pre = run_rwkv_pre(np.ascontiguousarray(xT.T), S, inp["norm_mix"][i], inp["rwkv_x_mix"][j],
                               inp["rwkv_w_rkv"][j], inp["rwkv_w0"][j], inp["rwkv_w1"][j], inp["rwkv_w2"][j],
                               inp["rwkv_a0"][j], inp["rwkv_a1"][j], inp["rwkv_a2"][j], inp["rwkv_g1"][j],
                               inp["rwkv_g2"][j], inp["rwkv_k_k"][j], inp["rwkv_k_a"][j], inp["rwkv_r_k"][j],
                               vres, vfT)
            if j == 0:
                vfT = pre["vT"]
            ynT = run_rwkv_scan(pre, inp["rwkv_ln_w"][j], inp["rwkv_ln_b"][j], B, S)
            xT = run_linres(xT, inp["rwkv_w_out"][j], [ynT, pre["bonT"], pre["gT"]])
        xT = run_mlp_T(xT, inp["norm_mlp"][i], inp["mlp_w1"][i], inp["mlp_w2"][i],
                       gf=inp["norm_final"] if i == 3 else None)
    return np.ascontiguousarray(xT.T).reshape(B, S, D).astype(np.float32)
```
